# Optimizing a Trainium2 kernel written in Bass

```python
import math
import jax, jax.numpy as jnp
from jax import lax
import numpy as np

D_MODEL = 1024
BATCH = 8
SEQ = 4096
DEPTH = 2
DEC_BATCH = 8
DEC_SEQ = 16
PAST_LEN = 2048

CHUNK = 64
HEAD_DIM = 64
FOX_HEADS = 8
SB_HEADS = 8
FOX_W = FOX_HEADS * HEAD_DIM
SB_W = SB_HEADS * HEAD_DIM
MIX_W = FOX_W + SB_W
IN_W = 3 * FOX_W + FOX_HEADS + 3 * SB_W
MEM_TOKENS = 256
MEM_HEADS = 4
MEM_HEAD_DIM = D_MODEL // MEM_HEADS
MEM_W = MEM_HEADS * MEM_HEAD_DIM
D_FF = 4 * D_MODEL
QBLK = 128
EPS = 1e-6
NEG = -1e30

kernel_name = "hybrid_fox_stickbreak_streaming_step"


def rmsnorm(x, g):
    xf = x.astype(jnp.float32)
    y = xf * lax.rsqrt(jnp.mean(xf * xf, axis=-1, keepdims=True) + EPS)
    return (y * g.astype(jnp.float32)).astype(x.dtype)


def _block_size(tq):
    return QBLK if tq % QBLK == 0 else tq


def _split_blocks(a, blk):
    b, t = a.shape[:2]
    return jnp.moveaxis(a.reshape((b, t // blk, blk) + a.shape[2:]), 1, 0)


def _merge_blocks(a):
    nb, b, blk = a.shape[:3]
    return jnp.moveaxis(a, 0, 1).reshape((b, nb * blk) + a.shape[3:])


def fox_attention(q, k, v, logf_all):
    B, Tq, H, hd = q.shape
    Tk = k.shape[1]
    off = Tk - Tq
    F = jnp.cumsum(logf_all.astype(jnp.float32), axis=1)
    FkT = jnp.transpose(F, (0, 2, 1))
    Fq = F[:, off:]
    blk = _block_size(Tq)
    kpos = jnp.arange(Tk)
    scale = 1.0 / math.sqrt(hd)

    def one_block(args):
        qb, fqb, start = args
        s = jnp.einsum('bqhd,bkhd->bhqk', qb, k).astype(jnp.float32) * scale
        s = s + jnp.transpose(fqb, (0, 2, 1))[..., None] - FkT[:, :, None, :]
        qpos = off + start + jnp.arange(blk)
        s = jnp.where(kpos[None, :] <= qpos[:, None], s, NEG)
        p = jax.nn.softmax(s, axis=-1)
        return jnp.einsum('bhqk,bkhd->bqhd', p.astype(v.dtype), v)

    starts = jnp.arange(Tq // blk) * blk
    out = lax.map(one_block, (_split_blocks(q, blk), _split_blocks(Fq, blk), starts))
    return _merge_blocks(out)


def sb_attention(q, k, v):
    B, Tq, H, hd = q.shape
    Tk = k.shape[1]
    off = Tk - Tq
    blk = _block_size(Tq)
    kpos = jnp.arange(Tk)
    scale = 1.0 / math.sqrt(hd)

    def one_block(args):
        qb, start = args
        z = jnp.einsum('bqhd,bkhd->bhqk', qb, k).astype(jnp.float32) * scale
        qpos = off + start + jnp.arange(blk)
        strict = kpos[None, :] < qpos[:, None]
        log_1m = jnp.where(strict, jax.nn.log_sigmoid(-z), 0.0)
        rest = lax.cumsum(log_1m, axis=3, reverse=True) - log_1m
        a = jnp.where(strict, jnp.exp(jax.nn.log_sigmoid(z) + rest), 0.0)
        return jnp.einsum('bhqk,bkhd->bqhd', a.astype(v.dtype), v)

    starts = jnp.arange(Tq // blk) * blk
    out = lax.map(one_block, (_split_blocks(q, blk), starts))
    return _merge_blocks(out)


def memory_kv(mem, g_mem, w_mk, w_mv, g_mk):
    B, M, _ = mem.shape
    m = rmsnorm(mem, g_mem)
    k = rmsnorm((m @ w_mk).reshape(B, M, MEM_HEADS, MEM_HEAD_DIM), g_mk)
    v = (m @ w_mv).reshape(B, M, MEM_HEADS, MEM_HEAD_DIM)
    return k, v


def cross_attention(h, mem_k, mem_v, w_mq, g_mq, w_mo):
    B, T, _ = h.shape
    q = rmsnorm((h @ w_mq).reshape(B, T, MEM_HEADS, MEM_HEAD_DIM), g_mq)
    s = jnp.einsum('bqhd,bkhd->bhqk', q, mem_k.astype(q.dtype)).astype(jnp.float32)
    p = jax.nn.softmax(s * (1.0 / math.sqrt(MEM_HEAD_DIM)), axis=-1)
    o = jnp.einsum('bhqk,bkhd->bqhd', p.astype(h.dtype), mem_v.astype(h.dtype))
    return o.reshape(B, T, MEM_W) @ w_mo


def _layer(x, mem_k, mem_v, past, p):
    B, T, _ = x.shape
    h = rmsnorm(x, p['g_mix'])
    proj = h @ p['w_in']
    q_f = proj[..., 0:FOX_W]
    k_f = proj[..., FOX_W:2 * FOX_W]
    v_f = proj[..., 2 * FOX_W:3 * FOX_W]
    f_lin = proj[..., 3 * FOX_W:3 * FOX_W + FOX_HEADS]
    o_sb = 3 * FOX_W + FOX_HEADS
    q_s = proj[..., o_sb:o_sb + SB_W]
    k_s = proj[..., o_sb + SB_W:o_sb + 2 * SB_W]
    v_s = proj[..., o_sb + 2 * SB_W:o_sb + 3 * SB_W]

    q_f = rmsnorm(q_f.reshape(B, T, FOX_HEADS, HEAD_DIM), p['g_fox_q'])
    k_f = rmsnorm(k_f.reshape(B, T, FOX_HEADS, HEAD_DIM), p['g_fox_k'])
    v_f = v_f.reshape(B, T, FOX_HEADS, HEAD_DIM)
    logf = jax.nn.log_sigmoid(f_lin.astype(jnp.float32) + p['b_forget'].astype(jnp.float32))
    q_s = q_s.reshape(B, T, SB_HEADS, HEAD_DIM)
    k_s = k_s.reshape(B, T, SB_HEADS, HEAD_DIM)
    v_s = v_s.reshape(B, T, SB_HEADS, HEAD_DIM)

    if past is None:
        kf_all, vf_all, lf_all, ks_all, vs_all = k_f, v_f, logf, k_s, v_s
    else:
        pk_f, pv_f, plf, pk_s, pv_s = past
        kf_all = jnp.concatenate([pk_f, k_f], axis=1)
        vf_all = jnp.concatenate([pv_f, v_f], axis=1)
        lf_all = jnp.concatenate([plf.astype(jnp.float32), logf], axis=1)
        ks_all = jnp.concatenate([pk_s, k_s], axis=1)
        vs_all = jnp.concatenate([pv_s, v_s], axis=1)

    o_f = fox_attention(q_f, kf_all, vf_all, lf_all)
    o_s = sb_attention(q_s, ks_all, vs_all)
    o = jnp.concatenate([rmsnorm(o_f.reshape(B, T, FOX_W), p['g_out_fox']),
                         rmsnorm(o_s.reshape(B, T, SB_W), p['g_out_sb'])], axis=-1)
    x = x + o @ p['w_out']
    x = x + cross_attention(rmsnorm(x, p['g_cross']), mem_k, mem_v, p['w_mq'], p['g_mq'], p['w_mo'])
    h = rmsnorm(x, p['g_ffn'])
    x = x + jnp.square(jax.nn.relu(h @ p['w_ff1'])) @ p['w_ff2']
    return x, (k_f, v_f, logf, k_s, v_s)


def setup_inputs(seed: int = 0) -> dict:
    key = jax.random.key(seed)
    ks = iter(jax.random.split(key, 40))

    def nrm(shape, scale=1.0):
        return jax.random.normal(next(ks), shape, jnp.float32) * scale

    def gain(shape):
        return 1.0 + 0.02 * nrm(shape)

    L = DEPTH
    return {
        "x_prompt": nrm((BATCH, SEQ, D_MODEL)),
        "x_sample": nrm((DEC_BATCH, DEC_SEQ, D_MODEL)),
        "mem_prompt": nrm((BATCH, MEM_TOKENS, D_MODEL)),
        "cache_fox_k": nrm((L, DEC_BATCH, PAST_LEN, FOX_HEADS, HEAD_DIM)),
        "cache_fox_v": nrm((L, DEC_BATCH, PAST_LEN, FOX_HEADS, HEAD_DIM)),
        "cache_fox_logf": jax.nn.log_sigmoid(nrm((L, DEC_BATCH, PAST_LEN, FOX_HEADS)) + 1.0),
        "cache_sb_k": nrm((L, DEC_BATCH, PAST_LEN, SB_HEADS, HEAD_DIM)),
        "cache_sb_v": nrm((L, DEC_BATCH, PAST_LEN, SB_HEADS, HEAD_DIM)),
        "cache_mem_k": nrm((L, DEC_BATCH, MEM_TOKENS, MEM_HEADS, MEM_HEAD_DIM)),
        "cache_mem_v": nrm((L, DEC_BATCH, MEM_TOKENS, MEM_HEADS, MEM_HEAD_DIM)),
        "g_mix": gain((L, D_MODEL)),
        "w_in": nrm((L, D_MODEL, IN_W), D_MODEL ** -0.5),
        "b_forget": 1.0 + 0.5 * nrm((L, FOX_HEADS)),
        "g_fox_q": gain((L, HEAD_DIM)),
        "g_fox_k": gain((L, HEAD_DIM)),
        "g_out_fox": gain((L, FOX_W)),
        "g_out_sb": gain((L, SB_W)),
        "w_out": nrm((L, MIX_W, D_MODEL), MIX_W ** -0.5),
        "g_cross": gain((L, D_MODEL)),
        "g_mem": gain((L, D_MODEL)),
        "w_mq": nrm((L, D_MODEL, MEM_W), D_MODEL ** -0.5),
        "w_mk": nrm((L, D_MODEL, MEM_W), D_MODEL ** -0.5),
        "w_mv": nrm((L, D_MODEL, MEM_W), D_MODEL ** -0.5),
        "g_mq": gain((L, MEM_HEAD_DIM)),
        "g_mk": gain((L, MEM_HEAD_DIM)),
        "w_mo": nrm((L, MEM_W, D_MODEL), MEM_W ** -0.5),
        "g_ffn": gain((L, D_MODEL)),
        "w_ff1": nrm((L, D_MODEL, D_FF), D_MODEL ** -0.5),
        "w_ff2": nrm((L, D_FF, D_MODEL), D_FF ** -0.5),
    }


def reference(x_prompt, x_sample, mem_prompt,
              cache_fox_k, cache_fox_v, cache_fox_logf, cache_sb_k, cache_sb_v,
              cache_mem_k, cache_mem_v,
              g_mix, w_in, b_forget, g_fox_q, g_fox_k, g_out_fox, g_out_sb, w_out,
              g_cross, g_mem, w_mq, w_mk, w_mv, g_mq, g_mk, w_mo,
              g_ffn, w_ff1, w_ff2):
    xp = x_prompt
    xs = x_sample
    p_fk, p_fv, p_lf, p_sk, p_sv, p_mk, p_mv = [], [], [], [], [], [], []
    s_fk, s_fv, s_lf, s_sk, s_sv = [], [], [], [], []
    for l in range(DEPTH):
        p = dict(g_mix=g_mix[l], w_in=w_in[l], b_forget=b_forget[l],
                 g_fox_q=g_fox_q[l], g_fox_k=g_fox_k[l],
                 g_out_fox=g_out_fox[l], g_out_sb=g_out_sb[l], w_out=w_out[l],
                 g_cross=g_cross[l], w_mq=w_mq[l], g_mq=g_mq[l], w_mo=w_mo[l],
                 g_ffn=g_ffn[l], w_ff1=w_ff1[l], w_ff2=w_ff2[l])
        mk, mv = memory_kv(mem_prompt, g_mem[l], w_mk[l], w_mv[l], g_mk[l])
        xp, (kf, vf, lf, ksb, vsb) = _layer(xp, mk, mv, None, p)
        p_fk.append(kf); p_fv.append(vf); p_lf.append(lf)
        p_sk.append(ksb); p_sv.append(vsb); p_mk.append(mk); p_mv.append(mv)
        past = (cache_fox_k[l], cache_fox_v[l], cache_fox_logf[l], cache_sb_k[l], cache_sb_v[l])
        xs, (kf, vf, lf, ksb, vsb) = _layer(xs, cache_mem_k[l], cache_mem_v[l], past, p)
        s_fk.append(kf); s_fv.append(vf); s_lf.append(lf)
        s_sk.append(ksb); s_sv.append(vsb)
    return (xp, xs,
            jnp.stack(p_fk), jnp.stack(p_fv), jnp.stack(p_lf),
            jnp.stack(p_sk), jnp.stack(p_sv), jnp.stack(p_mk), jnp.stack(p_mv),
            jnp.stack(s_fk), jnp.stack(s_fv), jnp.stack(s_lf),
            jnp.stack(s_sk), jnp.stack(s_sv))
```

```python
import os
import numpy as np
from contextlib import ExitStack
import concourse.bass as bass
import concourse.mybir as mybir
from concourse.bass_utils import run_bass_kernel_spmd

F32 = mybir.dt.float32
BF16 = mybir.dt.bfloat16
AF = mybir.ActivationFunctionType
ALU = mybir.AluOpType
AX = mybir.AxisListType

T = 4096
TS = 128
TT = T + TS
D = 1024
NQB = 33
NKB = 49
EPS = 1e-6
NEGM = -30000.0
NGB = 3464
NGC = 26
NCST = 1664
NCH = 10
SAME_ENGINE_SYNC = os.environ.get("KSES", "1") == "1"
SERIAL = os.environ.get("KSER", "0")
PSUM_EXCL = os.environ.get("KPX", "1") == "1"


class Tk:
    __slots__ = ("w", "r")

    def __init__(self):
        self.w = None
        self.r = {}


class Buf:
    def __init__(self, t):
        self.t = t
        self.k = Tk()

    def __getitem__(self, key):
        return self.t[key]


class Rot:
    def __init__(self, bufs):
        self.bufs = bufs
        self.i = 0

    def next(self):
        b = self.bufs[self.i]
        self.i = (self.i + 1) % len(self.bufs)
        return b


class Sched:
    def __init__(self, nc, stack):
        self.nc = nc
        self.eng = {"pe": nc.tensor, "act": nc.scalar, "dve": nc.vector, "pool": nc.gpsimd, "sp": nc.sync}
        self.sem = {}
        self.val = {}
        for e in self.eng:
            self.sem["E:" + e] = stack.enter_context(nc.semaphore("sem_" + e))
            self.val["E:" + e] = 0
        for q in ("sp", "pool"):
            for c in range(NCH):
                k = "D:%s:%d" % (q, c)
                self.sem[k] = stack.enter_context(nc.semaphore("dsem_%s_%d" % (q, c)))
                self.val[k] = 0
        self.known = {e: {} for e in self.eng}
        self.rr = {"sp": 0, "pool": 0}
        self.ninst = 0
        self.dead = False
        self.stop = os.environ.get("KSTOP", "")
        self.serial = SERIAL
        self.last_tok = None
        self.last_ew = None

    def ck(self, name):
        if self.stop and name == self.stop:
            self.dead = True

    def _deps(self, eng, reads, writes):
        need = {}

        def add(k, v):
            if need.get(k, 0) < v:
                need[k] = v

        for b in reads:
            if b.k.w is not None:
                add(*b.k.w)
        for b in writes:
            if b.k.w is not None:
                add(*b.k.w)
            for k, v in b.k.r.items():
                add(k, v)
        if self.serial == "1" and self.last_tok is not None:
            add(*self.last_tok)
        if self.serial == "2" and eng in ("act", "dve", "pool") and self.last_ew is not None:
            add(*self.last_ew)
        waits = []
        kn = self.known[eng]
        for k, v in need.items():
            if k == "E:" + eng and (eng == "pe" or not SAME_ENGINE_SYNC):
                continue
            if kn.get(k, 0) >= v:
                continue
            kn[k] = v
            waits.append((k, v))
        return waits

    def _mark(self, tok, reads, writes):
        k, v = tok
        for b in reads:
            if b.k.r.get(k, 0) < v:
                b.k.r[k] = v
        for b in writes:
            b.k.w = tok
            b.k.r = {}

    def op(self, eng, fn, reads=(), writes=(), signal=True):
        assert signal or eng == "pe"
        if self.dead:
            return
        if PSUM_EXCL:
            px = [b for b in reads if getattr(b, "px", False)]
            if px:
                writes = list(writes) + px
        waits = self._deps(eng, reads, writes)
        e = self.eng[eng]
        for k, v in waits:
            e.wait_ge(self.sem[k], v)
        inst = fn(e)
        key = "E:" + eng
        if signal:
            self.val[key] += 1
            inst.then_inc(self.sem[key], 1)
            tok = (key, self.val[key])
        else:
            tok = (key, self.val[key] + 1)
        self._mark(tok, reads, writes)
        self.last_tok = tok
        if eng in ("act", "dve", "pool"):
            self.last_ew = tok
        self.ninst += 1 + len(waits)

    def dma(self, q, out, in_, reads=(), writes=()):
        if self.dead:
            return
        c = self.rr[q]
        self.rr[q] = (c + 1) % NCH
        key = "D:%s:%d" % (q, c)
        waits = self._deps(q, reads, writes)
        prev = self.val[key]
        if prev > 0 and self.known[q].get(key, 0) < prev:
            waits.append((key, prev))
            self.known[q][key] = prev
        e = self.eng[q]
        for k, v in waits:
            e.wait_ge(self.sem[k], v)
        e.dma_start(out=out, in_=in_).then_inc(self.sem[key], 16)
        self.val[key] = prev + 16
        self._mark((key, prev + 16), reads, writes)
        self.last_tok = (key, prev + 16)
        self.ninst += 1 + len(waits)

    def barrier(self, engines=None):
        for e in (engines or list(self.eng)):
            kn = self.known[e]
            for k, v in self.val.items():
                if v > 0 and kn.get(k, 0) < v:
                    self.eng[e].wait_ge(self.sem[k], v)
                    kn[k] = v
                    self.ninst += 1


def build():
    nc = bass.Bass("TRN2", target_bir_lowering=False)

    def din(name, shape):
        return nc.dram_tensor(name, shape, F32, kind="ExternalInput").ap()

    def dout(name, shape):
        return nc.dram_tensor(name, shape, F32, kind="ExternalOutput").ap()

    def dint(name, shape, dt):
        return nc.dram_tensor(name, shape, dt, kind="Internal").ap()

    xp = din("xp", [T, D])
    xs = din("xs", [TS, D])
    mem = din("mem", [256, D])
    cfk = din("cfk", [2, 2048, 512])
    cfv = din("cfv", [2, 2048, 512])
    clf = din("clf", [2, 2048, 8])
    csk = din("csk", [2, 2048, 512])
    csv = din("csv", [2, 2048, 512])
    cmk = din("cmk", [2, 256, 1024])
    cmv = din("cmv", [2, 256, 1024])
    wshapes = {"w_in": [1024, 3080], "w_out": [1024, 1024], "w_mq": [1024, 1024], "w_mk": [1024, 1024],
               "w_mv": [1024, 1024], "w_mo": [1024, 1024], "w_ff1": [1024, 4096], "w_ff2": [4096, 1024]}
    W32 = {n: din(n, [2] + s) for n, s in wshapes.items()}
    gb = din("gb", [2, 128, NGB])
    gc = din("gc", [2, 128, NGC])
    cst = din("cst", [128, NCST])

    yp = dout("yp", [T, D])
    ys = dout("ys", [16, D])
    p_fk = dout("p_fk", [2, T, 512])
    p_fv = dout("p_fv", [2, T, 512])
    p_lf = dout("p_lf", [2, T, 8])
    p_sk = dout("p_sk", [2, T, 512])
    p_sv = dout("p_sv", [2, T, 512])
    p_mk = dout("p_mk", [2, 256, 1024])
    p_mv = dout("p_mv", [2, 256, 1024])
    s_fk = dout("s_fk", [2, 16, 512])
    s_fv = dout("s_fv", [2, 16, 512])
    s_lf = dout("s_lf", [2, 16, 8])
    s_sk = dout("s_sk", [2, 16, 512])
    s_sv = dout("s_sv", [2, 16, 512])

    XT = dint("XT", [128, 8, TT], F32)
    HTd = dint("HTd", [128, 8, TT], BF16)
    OT = dint("OT", [TT, 1024], F32)
    WB = {n: dint("b_" + n, [2] + s, BF16) for n, s in wshapes.items()}

    with ExitStack() as stack:
        S = Sched(nc, stack)

        uniq = [0]

        def sb(st, name, shape, dt):
            uniq[0] += 1
            nm = "%s_%d" % (name, uniq[0])
            b = Buf(st.enter_context(nc.sbuf_tensor(nm, shape, dt)))
            if os.environ.get("KADDR"):
                ml = nc.lookup_mloc(nm)
                print("ADDR", nm, ml.addr, ml.addr + int(np.prod(shape[1:])) * (4 if dt == F32 else 2))
            return b

        def ps(name, shape, dt):
            b = Buf(stack.enter_context(nc.psum_tensor(name, shape, dt)))
            b.px = True
            return b

        XTk = [Buf(None) for _ in range(NQB)]
        HTk = [Buf(None) for _ in range(NQB)]
        OTk = [Buf(None) for _ in range(NQB)]
        WBk = {n: [[] for _ in range(2)] for n in wshapes}

        CF = sb(stack, "CF", [128, 640], F32)
        CB = sb(stack, "CB", [128, NCST], BF16)
        ONESB = sb(stack, "ONESB", [128, 128], BF16)
        ONESF = sb(stack, "ONESF", [128, 512], F32)
        EPSC = sb(stack, "EPSC", [128, 1], F32)
        ONEC = sb(stack, "ONEC", [128, 1], F32)
        GB = sb(stack, "GB", [128, NGB], F32)
        GC = sb(stack, "GC", [128, NGC], F32)
        JUNK = sb(stack, "JUNK", [128, 1024], F32)
        XTOK = Rot([sb(stack, "xtok%d" % i, [128, 1024], F32) for i in range(2)])
        HB = Rot([sb(stack, "hb%d" % i, [128, 1024], BF16) for i in range(2)])
        XTT = Rot([sb(stack, "xtt%d" % i, [128, 8, 128], F32) for i in range(2)])
        HTT = Rot([sb(stack, "htt%d" % i, [128, 8, 128], BF16) for i in range(2)])
        SMALL = Rot([sb(stack, "small%d" % i, [128, 16], F32) for i in range(8)])

        PZ = Rot([ps("pz%d" % i, [128, 512], F32) for i in range(2)])
        PO = Rot([ps("po%d" % i, [128, 512], F32) for i in range(2)])
        PP = Rot([ps("pp%d" % i, [128, 512], F32) for i in range(2)])
        PT = Rot([ps("pt%d" % i, [128, 1024], BF16) for i in range(2)])
        P4 = Rot(PZ.bufs + PO.bufs)
        ZB = Rot(PZ.bufs + PP.bufs)
        P6 = Rot(PZ.bufs + PO.bufs + PP.bufs)

        IDF = lambda: CF[:, 0:128]
        TRIU = lambda: CF[:, 128:256]
        SEL127 = lambda: CF[:, 256:384]
        IDB = lambda: CB[:, 0:128]
        NEGI = lambda: CB[:, 384:512]
        NEGS = lambda: CB[:, 512:640]
        SELT = lambda h: CB[0:24, 640 + h * 128:640 + (h + 1) * 128]

        tog = [0]

        def evac_eng():
            tog[0] ^= 1
            return "act" if tog[0] else "dve"

        def copy(eng, out, in_, reads, writes):
            if eng == "act":
                S.op("act", lambda e: e.activation(out=out, in_=in_, func=AF.Copy), reads, writes)
            elif eng == "dve":
                S.op("dve", lambda e: e.tensor_copy(out=out, in_=in_), reads, writes)
            else:
                S.op("pool", lambda e: e.tensor_copy(out=out, in_=in_), reads, writes)

        def rstd_from_ss(ss, g0, g1, n):
            ms = SMALL.next()
            S.op("dve", lambda e: e.tensor_scalar(out=ms[:, g0:g1], in0=ss[:, g0:g1], scalar1=1.0 / n, scalar2=EPS,
                                                  op0=ALU.mult, op1=ALU.add), [ss], [ms])
            ln = SMALL.next()
            S.op("act", lambda e: e.activation(out=ln[:, g0:g1], in_=ms[:, g0:g1], func=AF.Ln), [ms], [ln])
            rs = SMALL.next()
            S.op("act", lambda e: e.activation(out=rs[:, g0:g1], in_=ln[:, g0:g1], func=AF.Exp, scale=-0.5), [ln], [rs])
            return rs

        st0 = ExitStack()
        CST32 = sb(st0, "CST32", [128, NCST], F32)
        STG = Rot([sb(st0, "stg%d" % i, [128, 4096], F32) for i in range(3)])
        STB = Rot([sb(st0, "stb%d" % i, [128, 4096], BF16) for i in range(3)])
        S.dma("sp", CF[:, :], cst[:, 0:640], [], [CF])
        S.dma("sp", CST32[:, :], cst[:, :], [], [CST32])
        copy("pool", CB[:, :], CST32[:, :], [CST32], [CB])
        S.op("dve", lambda e: e.memset(ONESB[:, :], 1.0), [], [ONESB])
        S.op("dve", lambda e: e.memset(ONESF[:, :], 1.0), [], [ONESF])
        S.op("dve", lambda e: e.memset(EPSC[:, :], EPS), [], [EPSC])
        S.op("dve", lambda e: e.memset(ONEC[:, :], 1.0), [], [ONEC])

        def load_gains(l):
            S.dma("sp", GB[:, :], gb[l], [], [GB])
            S.dma("sp", GC[:, :], gc[l], [], [GC])
            S.op("dve", lambda e: e.tensor_scalar(out=GB[:, 3072:3136], in0=GB[:, 3072:3136], scalar1=0.125, scalar2=None,
                                                  op0=ALU.mult), [GB], [GB])
            S.op("dve", lambda e: e.tensor_scalar(out=GC[:, 24:26], in0=GC[:, 24:26], scalar1=1.0 / 16.0, scalar2=None,
                                                  op0=ALU.mult), [GC], [GC])

        conv_jobs = []
        for l_ in range(2):
            for n in ["w_in", "w_out", "w_mq", "w_mk", "w_mv", "w_mo", "w_ff1", "w_ff2"]:
                K_, N_ = wshapes[n]
                F_ = K_ * N_ // 128
                src = W32[n][l_].rearrange("k n -> (k n)").rearrange("(p f) -> p f", p=128)
                dst = WB[n][l_].rearrange("k n -> (k n)").rearrange("(p f) -> p f", p=128)
                for f0 in range(0, F_, 4096):
                    f1 = min(F_, f0 + 4096)
                    conv_jobs.append((n, l_, src[:, f0:f1], dst[:, f0:f1], f1 - f0))

        def convert_some(k):
            for _ in range(k):
                if not conv_jobs:
                    return
                n, l_, src, dst, w = conv_jobs.pop(0)
                a = STG.next()
                b = STB.next()
                S.dma("sp", a[:, 0:w], src, [], [a])
                copy("pool", b[:, 0:w], a[:, 0:w], [a], [b])
                kk = Buf(None)
                S.dma("sp", dst, b[:, 0:w], [b], [kk])
                WBk[n][l_].append(kk)

        S.ck("pre0")
        load_gains(0)
        S.ck("pre1")
        convert_some(7)
        S.ck("pre2")

        def to_feature_major_f32(src_buf, blk):
            xtt = XTT.next()
            for half in range(2):
                bank = P4.next()
                for c in range(4):
                    cc = half * 4 + c
                    S.op("pe", lambda e, c=c, cc=cc: e.transpose(bank[:, c * 128:(c + 1) * 128], src_buf[:, cc * 128:(cc + 1) * 128], IDF()),
                         [src_buf, CF], [bank], signal=(c == 3))
                copy(evac_eng(), xtt[:, half * 4:(half + 1) * 4, :], bank[:, 0:512].rearrange("p (c t) -> p c t", c=4), [bank], [xtt])
            S.dma("sp", XT[:, :, blk * 128:(blk + 1) * 128], xtt[:, :, :], [xtt], [XTk[blk]])

        def tok_to_hT(hb, blk, dst, dstk):
            bank = PT.next()
            for c in range(8):
                S.op("pe", lambda e, c=c: e.transpose(bank[:, c * 128:(c + 1) * 128], hb[:, c * 128:(c + 1) * 128], IDB()),
                     [hb, CB], [bank], signal=(c == 7))
            S.ck("p0b1")
            htt = HTT.next()
            copy(evac_eng(), htt[:, :, :], bank[:, 0:1024].rearrange("p (c t) -> p c t", c=8), [bank], [htt])
            S.ck("p0b2")
            S.dma("sp", dst[:, :, blk * 128:(blk + 1) * 128], htt[:, :, :], [htt], [dstk])

        for blk in range(NQB):
            xt = XTOK.next()
            src = xp[blk * 128:(blk + 1) * 128, :] if blk < 32 else xs[:, :]
            S.dma("sp", xt[:, :], src, [], [xt])
            SKIP = os.environ.get("KSKIP", "")
            if "a" not in SKIP:
                to_feature_major_f32(xt, blk)
            S.ck("p0a")
            if "b" in SKIP:
                continue
            ss = SMALL.next()
            S.op("act", lambda e: e.activation(out=JUNK[:, :], in_=xt[:, :], func=AF.Square, accum_out=ss[:, 0:1]), [xt], [JUNK, ss])
            rs = rstd_from_ss(ss, 0, 1, 1024.0)
            hb = HB.next()
            S.op("dve", lambda e: e.scalar_tensor_tensor(out=hb[:, :], in0=xt[:, :], scalar=rs[:, 0:1], in1=GB[:, 0:1024],
                                                         op0=ALU.mult, op1=ALU.mult), [xt, rs, GB], [hb])
            S.ck("p0b")
            tok_to_hT(hb, blk, HTd, HTk[blk])
            S.ck("p0c")
            S.ck("p0c_%d" % blk)
            convert_some(2)
        convert_some(1000)
        S.barrier()
        st0.close()


        if os.environ.get("KDBG"):
            dbb = HB.next()
            dbf = XTOK.next()
            for j, blk_ in enumerate((6, 5, 7, 12)):
                S.dma("sp", yp[j * 256:j * 256 + 128, :].rearrange("p (c t) -> p c t", c=8), XT[:, :, blk_ * 128:(blk_ + 1) * 128], [XTk[blk_]], [])
                S.dma("sp", dbb[:, :].rearrange("p (c t) -> p c t", c=8), HTd[:, :, blk_ * 128:(blk_ + 1) * 128], [HTk[blk_]], [dbb])
                S.op("dve", lambda e: e.tensor_copy(out=dbf[:, :], in_=dbb[:, :]), [dbb], [dbf])
                S.dma("sp", yp[j * 256 + 128:j * 256 + 256, :], dbf[:, :], [dbf], [])
        S.ck("p0")
        for l in range(2):
            if l == 1:
                load_gains(1)
            with ExitStack() as st:
                if os.environ.get("KPAD"):
                    PAD = sb(st, "PAD", [128, int(os.environ["KPAD"])], F32)
                WP = sb(st, "WP", [128, 8, 384], BF16)
                WF = sb(st, "WF", [128, 8, 8], BF16)
                HT = Rot([sb(st, "hT%d" % i, [128, 8, 512], BF16) for i in range(2)])
                QTA = sb(st, "QTA", [128, NQB * 128], BF16)
                QTB = sb(st, "QTB", [128, NQB * 128], BF16)
                KT = sb(st, "KT", [128, NKB * 128], BF16)
                V = sb(st, "V", [128, NKB, 2, 66], BF16)
                KBC = sb(st, "KBC", [128, 16, 128], BF16)
                K32 = sb(st, "K32", [128, 16, 128], F32)
                V32 = sb(st, "V32", [128, 16, 128], F32)
                SQ = Rot([sb(st, "sq%d" % i, [128, 256], F32) for i in range(2)])
                KST = Rot([sb(st, "kst%d" % i, [128, 128], F32) for i in range(3)])
                VST = Rot([sb(st, "vst%d" % i, [128, 128], F32) for i in range(3)])
                QB = Rot([sb(st, "qb%d" % i, [128, 128], BF16) for i in range(3)])
                KB = Rot([sb(st, "kb%d" % i, [128, 128], BF16) for i in range(3)])
                EB = Rot([sb(st, "eb%d" % i, [128, 512], F32) for i in range(3)])
                SPB = Rot([sb(st, "spb%d" % i, [128, 512], F32) for i in range(3)])
                CBF = Rot([sb(st, "cbf%d" % i, [128, 512], F32) for i in range(2)])
                AB = Rot([sb(st, "ab%d" % i, [128, 512], BF16) for i in range(3)])
                ATB = Rot([sb(st, "atb%d" % i, [128, 512], BF16) for i in range(3)])
                OST = Rot([sb(st, "ost%d" % i, [128, 128], F32) for i in range(2)])
                LF = sb(st, "LF", [128, NKB, 8], F32)
                WC = sb(st, "WC", [128, NKB, 8], F32)
                TB = sb(st, "TB", [128, NKB, 8], F32)
                CS = sb(st, "CS", [128, NKB, 8], F32)
                FM = sb(st, "FM", [128, NKB, 8], F32)
                R1 = sb(st, "R1", [128, NKB, 8], F32)
                FS = sb(st, "FS", [128, NKB, 3, 8], BF16)
                FKT = sb(st, "FKT", [128, NKB * 128], BF16)
                FT1 = sb(st, "FT1", [128, NQB * 8], F32)
                FT2 = sb(st, "FT2", [128, NQB * 8], F32)

                S.op("dve", lambda e: e.memset(V[:, :, :, 64:65], 1.0), [], [V])
                S.op("dve", lambda e: e.memset(V[:, :, :, 65:66], 0.0), [], [V])
                S.op("dve", lambda e: e.memset(LF[:, :, :], 0.0), [], [LF])
                S.op("pool", lambda e: e.memset(QTA[:, :], 0.0), [], [QTA])
                S.op("pool", lambda e: e.memset(QTB[:, :], 0.0), [], [QTB])
                S.op("pool", lambda e: e.memset(FKT[:, :], 0.0), [], [FKT])

                for p in range(8):
                    fox = p < 4
                    pp = p % 4
                    base = 0 if fox else 1544
                    for j in range(3):
                        c0 = base + j * 512 + pp * 128
                        S.dma("sp", WP[:, :, j * 128:(j + 1) * 128],
                              WB["w_in"][l].rearrange("(c p) n -> p c n", p=128)[:, :, c0:c0 + 128], WBk["w_in"][l], [WP])
                    if p == 0:
                        S.dma("sp", WF[:, :, :], WB["w_in"][l].rearrange("(c p) n -> p c n", p=128)[:, :, 1536:1544],
                              WBk["w_in"][l], [WF])
                    ck, cv = (cfk, cfv) if fox else (csk, csv)
                    S.dma("sp", K32[:, :, :], ck[l].rearrange("(b p) n -> p b n", p=128)[:, :, pp * 128:(pp + 1) * 128], [], [K32])
                    copy("pool", KBC[:, :, :], K32[:, :, :], [K32], [KBC])
                    S.dma("sp", V32[:, :, :], cv[l].rearrange("(b p) n -> p b n", p=128)[:, :, pp * 128:(pp + 1) * 128], [], [V32])
                    copy("pool", V[:, 32:48, :, 0:64], V32[:, :, :].rearrange("p b (h d) -> p b h d", h=2), [V32], [V])
                    if p == 0:
                        S.dma("sp", LF[:, 32:48, :], clf[l].rearrange("(b p) h -> p b h", p=128), [], [LF])
                    S.ck("pj_a")
                    FL = PO.bufs[1]
                    ok_out, ov_out = (p_fk, p_fv) if fox else (p_sk, p_sv)
                    sk_out, sv_out = (s_fk, s_fv) if fox else (s_sk, s_sv)
                    for ti in range(9):
                        W_ = 512 if ti < 8 else 128
                        c0 = ti * 512
                        hT = HT.next()
                        S.dma("sp", hT[:, :, 0:W_], HTd[:, :, c0:c0 + W_], HTk[ti * 4:ti * 4 + W_ // 128], [hT])
                        bq = PT.next()
                        bk = PT.next()
                        for tb in range(W_ // 128):
                            blk = ti * 4 + tb
                            kblk = blk if blk < 32 else 48
                            bank = PP.next()
                            for c in range(8):
                                S.op("pe", lambda e, c=c: e.matmul(bank[:, 0:384], hT[:, c, tb * 128:(tb + 1) * 128], WP[:, c, :],
                                                                   start=(c == 0), stop=(c == 7)), [hT, WP], [bank], signal=(c == 7))
                            if p == 0 and "f" not in os.environ.get("KSKIP", ""):
                                for c in range(8):
                                    S.op("pe", lambda e, c=c: e.matmul(FL[:, blk * 8:(blk + 1) * 8], hT[:, c, tb * 128:(tb + 1) * 128], WF[:, c, :],
                                                                       start=(c == 0), stop=(c == 7)), [hT, WF], [FL], signal=(c == 7))
                            S.ck("pj_b")
                            S.ck("pj_b%d" % blk)
                            kst = KST.next()
                            vst = VST.next()
                            qb = QB.next()
                            kb = KB.next()
                            if fox:
                                sq = SQ.next()
                                S.op("act", lambda e: e.activation(out=sq[:, :], in_=bank[:, 0:256], func=AF.Square), [bank], [sq])
                                ss = SMALL.next()
                                S.op("dve", lambda e: e.tensor_reduce(out=ss[:, 0:4], in_=sq[:, :].rearrange("p (g d) -> p g d", g=4),
                                                                      axis=AX.X, op=ALU.add), [sq], [ss])
                                rs = rstd_from_ss(ss, 0, 4, 64.0)
                                for hh in range(2):
                                    S.op("dve", lambda e, hh=hh: e.scalar_tensor_tensor(
                                        out=qb[:, hh * 64:(hh + 1) * 64], in0=bank[:, hh * 64:(hh + 1) * 64], scalar=rs[:, hh:hh + 1],
                                        in1=GB[:, 3072:3136], op0=ALU.mult, op1=ALU.mult), [bank, rs, GB], [qb])
                                    S.op("dve", lambda e, hh=hh: e.scalar_tensor_tensor(
                                        out=kst[:, hh * 64:(hh + 1) * 64], in0=bank[:, 128 + hh * 64:128 + (hh + 1) * 64], scalar=rs[:, 2 + hh:3 + hh],
                                        in1=GB[:, 3136:3200], op0=ALU.mult, op1=ALU.mult), [bank, rs, GB], [kst])
                                copy("pool", kb[:, :], kst[:, :], [kst], [kb])
                            else:
                                S.op("act", lambda e: e.activation(out=qb[:, :], in_=bank[:, 0:128], func=AF.Copy, scale=0.125), [bank], [qb])
                                S.op("act", lambda e: e.activation(out=kst[:, :], in_=bank[:, 128:256], func=AF.Copy), [bank], [kst])
                                S.op("dve", lambda e: e.tensor_copy(out=kb[:, :], in_=bank[:, 128:256]), [bank], [kb])
                            S.op("act", lambda e: e.activation(out=vst[:, :], in_=bank[:, 256:384], func=AF.Copy), [bank], [vst])
                            S.op("dve", lambda e: e.tensor_copy(out=V[:, kblk, :, 0:64], in_=bank[:, 256:384].rearrange("p (h d) -> p h d", h=2)),
                                 [bank], [V])
                            S.ck("pj_c")
                            S.ck("pj_c%d" % blk)
                            if blk < 32:
                                S.dma("sp", ok_out[l, blk * 128:(blk + 1) * 128, pp * 128:(pp + 1) * 128], kst[:, :], [kst], [])
                                S.dma("sp", ov_out[l, blk * 128:(blk + 1) * 128, pp * 128:(pp + 1) * 128], vst[:, :], [vst], [])
                            else:
                                S.dma("sp", sk_out[l, :, pp * 128:(pp + 1) * 128], kst[0:16, :], [kst], [])
                                S.dma("sp", sv_out[l, :, pp * 128:(pp + 1) * 128], vst[0:16, :], [vst], [])
                            S.ck("pj_g%d" % blk)
                            S.op("pe", lambda e: e.transpose(bq[:, tb * 128:(tb + 1) * 128], qb[:, :], IDB()), [qb, CB], [bq])
                            S.op("pe", lambda e: e.transpose(bk[:, tb * 128:(tb + 1) * 128], kb[:, :], IDB()), [kb, CB], [bk])
                            S.ck("pj_h%d" % blk)
                        S.ck("pj_d")
                        S.ck("pj_d%d" % ti)
                        kc0 = c0 if ti < 8 else 48 * 128
                        copy("act", QTA[0:64, c0:c0 + W_], bq[0:64, 0:W_], [bq], [QTA])
                        copy("act", QTB[64:128, c0:c0 + W_], bq[64:128, 0:W_], [bq], [QTB])
                        copy("dve", KT[:, kc0:kc0 + W_], bk[:, 0:W_], [bk], [KT])
                        S.ck("pj_f%d" % ti)
                    S.ck("pj_e")
                    for g in range(4):
                        bk = PT.next()
                        for i in range(4):
                            S.op("pe", lambda e, i=i: e.transpose(bk[:, i * 128:(i + 1) * 128], KBC[:, g * 4 + i, :], IDB()), [KBC, CB], [bk], signal=(i == 3))
                        copy(evac_eng(), KT[:, (32 + g * 4) * 128:(36 + g * 4) * 128], bk[:, 0:512], [bk], [KT])

                    S.ck("proj%d_%d" % (l, p))
                    if p == 0:
                        NB8 = NQB * 8
                        S.op("dve", lambda e: e.tensor_tensor(out=FT1[:, :].rearrange("p (b h) -> p b h", h=8),
                                                              in0=FL[:, 0:NB8].rearrange("p (b h) -> p b h", h=8),
                                                              in1=GB[:, 3456:3464].unsqueeze(1).broadcast_to([128, NQB, 8]), op=ALU.add),
                             [FL, GB], [FT1])
                        S.op("act", lambda e: e.activation(out=FT2[:, :], in_=FT1[:, :], func=AF.Exp, scale=-1.0), [FT1], [FT2])
                        S.op("act", lambda e: e.activation(out=FT1[:, :], in_=FT2[:, :], func=AF.Ln, bias=ONEC[:, 0:1]), [FT2, ONEC], [FT1])
                        S.op("dve", lambda e: e.tensor_scalar(out=LF[:, 0:32, :], in0=FT1[:, 0:256].rearrange("p (b h) -> p b h", h=8),
                                                              scalar1=-1.0, scalar2=None, op0=ALU.mult), [FT1], [LF])
                        S.op("dve", lambda e: e.tensor_scalar(out=LF[:, 48, :], in0=FT1[:, 256:264], scalar1=-1.0, scalar2=None, op0=ALU.mult),
                             [FT1], [LF])
                        for q4 in range(4):
                            S.dma("sp", p_lf[l].rearrange("(b p) h -> p b h", p=128)[:, q4 * 8:(q4 + 1) * 8, :], LF[:, q4 * 8:(q4 + 1) * 8, :], [LF], [])
                        S.dma("sp", s_lf[l], LF[0:16, 48, :], [LF], [])
                        bank = PP.next()
                        S.op("pe", lambda e: e.matmul(bank[:, 0:NKB * 8], TRIU(), LF[:, :, :].rearrange("p b h -> p (b h)"), start=True, stop=True),
                             [CF, LF], [bank])
                        S.op("dve", lambda e: e.tensor_copy(out=WC[:, :, :].rearrange("p b h -> p (b h)"), in_=bank[:, 0:NKB * 8]), [bank], [WC])
                        bank2 = PP.next()
                        S.op("pe", lambda e: e.matmul(bank2[:, 0:NKB * 8], SEL127(), WC[:, :, :].rearrange("p b h -> p (b h)"), start=True, stop=True),
                             [CF, WC], [bank2])
                        S.op("act", lambda e: e.activation(out=TB[:, :, :].rearrange("p b h -> p (b h)"), in_=bank2[:, 0:NKB * 8], func=AF.Copy),
                             [bank2], [TB])
                        for (a, b_) in ((0, 32), (32, 49)):
                            for h in range(8):
                                S.op("dve", lambda e, h=h: e.tensor_tensor_scan(out=CS[:, a:b_, h], data0=ONESF[:, 0:b_ - a], data1=TB[:, a:b_, h],
                                                                                initial=0.0, op0=ALU.mult, op1=ALU.add), [TB, ONESF], [CS])
                        S.op("dve", lambda e: e.tensor_tensor(out=FM[:, :, :], in0=WC[:, :, :], in1=CS[:, :, :], op=ALU.add), [WC, CS], [FM])
                        S.op("dve", lambda e: e.tensor_tensor(out=FM[:, :, :], in0=FM[:, :, :], in1=TB[:, :, :], op=ALU.subtract), [FM, TB], [FM])
                        S.op("dve", lambda e: e.tensor_scalar(out=FS[:, :, 0, :], in0=FM[:, :, :], scalar1=-1.0, scalar2=None, op0=ALU.mult), [FM], [FS])
                        S.op("dve", lambda e: e.scalar_tensor_tensor(out=R1[:, :, :], in0=FM[:, :, :], scalar=-1.0, in1=FS[:, :, 0, :],
                                                                     op0=ALU.mult, op1=ALU.subtract), [FM, FS], [R1])
                        S.op("dve", lambda e: e.tensor_copy(out=FS[:, :, 1, :], in_=R1[:, :, :]), [R1], [FS])
                        S.op("dve", lambda e: e.tensor_tensor(out=R1[:, :, :], in0=R1[:, :, :], in1=FS[:, :, 1, :], op=ALU.subtract), [R1, FS], [R1])
                        S.op("dve", lambda e: e.tensor_copy(out=FS[:, :, 2, :], in_=R1[:, :, :]), [R1], [FS])
                        for g in range(7):
                            n_ = min(8, NKB - g * 8)
                            bk = PT.next()
                            for i in range(n_):
                                S.op("pe", lambda e, i=i: e.transpose(bk[0:24, i * 128:(i + 1) * 128],
                                                                      FS[:, g * 8 + i, :, :].rearrange("p s h -> p (s h)"), IDB()),
                                     [FS, CB], [bk], signal=(i == n_ - 1))
                            copy(evac_eng(), FKT[0:24, g * 1024:g * 1024 + n_ * 128], bk[0:24, 0:n_ * 128], [bk], [FKT])

                    S.ck("f%d_%d" % (l, p))
                    chunks = []
                    for qblk in range(NQB):
                        kbs = list(range(0, qblk + 1)) if qblk < 32 else list(range(32, 49))
                        groups = [kbs[i:i + 4] for i in range(0, len(kbs), 4)]
                        for e_ in range(2):
                            for gi in range(len(groups) - 1, -1, -1):
                                chunks.append(dict(q=qblk, e=e_, kbs=groups[gi], diag=(gi == len(groups) - 1),
                                                   first=(gi == len(groups) - 1), last=(gi == 0), own=kbs[-1]))
                    state = {}

                    def stage_z(ch):
                        z = ZB.next()
                        ch["z"] = z
                        P0 = 64 * ch["e"]
                        w = 128 * len(ch["kbs"])
                        ch["w"] = w
                        q0 = ch["q"] * 128
                        k0 = ch["kbs"][0] * 128
                        more = fox or ch["diag"]
                        QTe = QTA if ch["e"] == 0 else QTB
                        S.op("pe", lambda e: e.matmul(z[:, 0:w], QTe[:, q0:q0 + 128], KT[:, k0:k0 + w], start=True, stop=not more),
                             [QTe, KT], [z], signal=not more)
                        if fox:
                            h = 2 * pp + ch["e"]
                            S.op("pe", lambda e: e.matmul(z[:, 0:w], CB[:, 640 + h * 128:640 + (h + 1) * 128], FKT[:, k0:k0 + w], start=False, stop=not ch["diag"]),
                                 [CB, FKT], [z], signal=not ch["diag"])
                        if ch["diag"]:
                            S.op("pe", lambda e: e.matmul(z[:, w - 128:w], IDB(), NEGI() if fox else NEGS(), start=False, stop=True), [CB], [z])

                    def stage_e1(ch):
                        if fox:
                            return
                        z = ch["z"]
                        w = ch["w"]
                        eb = EB.next()
                        ch["eb"] = eb
                        S.op("act", lambda e: e.activation(out=eb[:, 0:w], in_=z[:, 0:w], func=AF.Exp), [z], [eb])

                    def stage_e1b(ch):
                        if fox:
                            return
                        w = ch["w"]
                        eb = ch["eb"]
                        sp = SPB.next()
                        ch["sp"] = sp
                        S.op("act", lambda e: e.activation(out=sp[:, 0:w], in_=eb[:, 0:w], func=AF.Ln, bias=ONEC[:, 0:1]), [eb, ONEC], [sp])

                    def stage_e2(ch):
                        z = ch["z"]
                        w = ch["w"]
                        a = AB.next()
                        ch["a"] = a
                        if fox:
                            h = 2 * pp + ch["e"]
                            S.op("act", lambda e: e.activation(out=a[:, 0:w], in_=z[:, 0:w], func=AF.Exp, bias=FM[:, ch["own"], h:h + 1]),
                                 [z, FM], [a])
                        else:
                            eb = ch["eb"]
                            sp = ch["sp"]
                            c = CBF.next()
                            if ch["first"]:
                                S.op("dve", lambda e: e.tensor_tensor_scan(out=c[:, 0:w][:, ::-1], data0=ONESF[:, 0:w], data1=sp[:, 0:w][:, ::-1],
                                                                           initial=0.0, op0=ALU.mult, op1=ALU.add), [sp, ONESF], [c])
                            else:
                                pc = state["prevc"]
                                S.op("dve", lambda e: e.tensor_tensor_scan(out=c[:, 0:w][:, ::-1], data0=ONESF[:, 0:w], data1=sp[:, 0:w][:, ::-1],
                                                                           initial=pc[:, 0:1], op0=ALU.mult, op1=ALU.add), [sp, ONESF, pc], [c])
                            state["prevc"] = c
                            S.op("dve", lambda e: e.tensor_tensor(out=eb[:, 0:w], in0=z[:, 0:w], in1=c[:, 0:w], op=ALU.subtract), [z, c], [eb])
                            S.op("act", lambda e: e.activation(out=a[:, 0:w], in_=eb[:, 0:w], func=AF.Exp), [eb], [a])

                    def stage_pv(ch):
                        a = ch["a"]
                        w = ch["w"]
                        nb = len(ch["kbs"])
                        bt = PT.next()
                        for i in range(nb):
                            S.op("pe", lambda e, i=i: e.transpose(bt[:, i * 128:(i + 1) * 128], a[:, i * 128:(i + 1) * 128], IDB()),
                                 [a, CB], [bt], signal=(i == nb - 1))
                        at = ATB.next()
                        ch["at"] = at
                        copy("dve" if fox else "act", at[:, 0:w], bt[:, 0:w], [bt], [at])

                    def stage_pvm(ch):
                        at = ch["at"]
                        w = ch["w"]
                        nb = len(ch["kbs"])
                        if ch["first"]:
                            state["o"] = PO.next()
                        o = state["o"]
                        for i in range(nb):
                            kb_ = ch["kbs"][i]
                            lastmm = ch["last"] and i == nb - 1
                            S.op("pe", lambda e, i=i, kb_=kb_: e.matmul(o[:, 0:66], at[:, i * 128:(i + 1) * 128], V[:, kb_, ch["e"], :],
                                                                        start=(ch["first"] and i == 0), stop=lastmm),
                                 [at, V], [o], signal=(i == nb - 1))
                        if ch["last"]:
                            if ch["e"] == 0:
                                state["ost"] = OST.next()
                            ost = state["ost"]
                            e_ = ch["e"]
                            if fox:
                                rc = SMALL.next()
                                S.op("dve", lambda e: e.reciprocal(out=rc[:, 0:1], in_=o[:, 64:65]), [o], [rc])
                                S.op("dve", lambda e: e.tensor_scalar(out=ost[:, e_ * 64:(e_ + 1) * 64], in0=o[:, 0:64], scalar1=rc[:, 0:1], scalar2=None,
                                                                      op0=ALU.mult), [o, rc], [ost])
                            else:
                                copy("act", ost[:, e_ * 64:(e_ + 1) * 64], o[:, 0:64], [o], [ost])
                            if e_ == 1:
                                col0 = (0 if fox else 512) + pp * 128
                                qb_ = ch["q"]
                                S.dma("sp", OT[qb_ * 128:(qb_ + 1) * 128, col0:col0 + 128], ost[:, :], [ost], [OTk[qb_]])

                    n = len(chunks)
                    stage_z(chunks[0])
                    stage_z(chunks[1])
                    stage_e1(chunks[0])
                    stage_e1b(chunks[0])
                    for i in range(n):
                        if i + 2 < n:
                            stage_z(chunks[i + 2])
                        if i + 1 < n:
                            stage_e1(chunks[i + 1])
                        if i >= 1:
                            stage_pv(chunks[i - 1])
                        if i + 1 < n:
                            stage_e1b(chunks[i + 1])
                        stage_e2(chunks[i])
                        if i >= 1:
                            stage_pvm(chunks[i - 1])
                    stage_pv(chunks[n - 1])
                    stage_pvm(chunks[n - 1])
                    S.ck("att%d_%d" % (l, p))
                S.barrier()

            with ExitStack() as st:
                XTL = sb(st, "XTL", [128, 8, 512], F32)
                ACTA = sb(st, "ACTA", [128, 8, 512], BF16)
                ACTB = sb(st, "ACTB", [128, 8, 512], BF16)
                QM = sb(st, "QM", [128, 8, 512], F32)
                SQD = sb(st, "SQD", [128, 8, 512], BF16)
                RS = Rot([sb(st, "rs%d" % i, [128, 512], F32) for i in range(2)])
                PTB = sb(st, "PTB", [128, 2, 512], BF16)
                HID = sb(st, "HID", [128, 16, 512], BF16)
                RL = Rot([sb(st, "rl%d" % i, [128, 512], F32) for i in range(2)])
                WBLK = Rot([sb(st, "wblk%d" % i, [128, 8, 512], BF16) for i in range(3)])
                MKT = [sb(st, "MKT%d" % i, [128, 8, 256], BF16) for i in range(2)]
                MV = [sb(st, "MV%d" % i, [128, 2, 1024], BF16) for i in range(2)]
                MST = Rot([sb(st, "mst%d" % i, [128, 1024], F32) for i in range(2)])
                MB = Rot([sb(st, "mb%d" % i, [128, 1024], BF16) for i in range(2)])
                MT = sb(st, "MT", [128, 8, 256], BF16)

                def load_wblk(name, k0, n0):
                    wb = WBLK.next()
                    S.dma("sp", wb[:, :, :], WB[name][l].rearrange("(c p) n -> p c n", p=128)[:, k0:k0 + 8, n0:n0 + 512], WBk[name][l], [wb])
                    return wb

                for mb_ in range(2):
                    mt = XTOK.next()
                    S.dma("sp", mt[:, :], mem[mb_ * 128:(mb_ + 1) * 128, :], [], [mt])
                    ss = SMALL.next()
                    S.op("act", lambda e: e.activation(out=JUNK[:, :], in_=mt[:, :], func=AF.Square, accum_out=ss[:, 0:1]), [mt], [JUNK, ss])
                    rs = rstd_from_ss(ss, 0, 1, 1024.0)
                    hb = HB.next()
                    S.op("dve", lambda e: e.scalar_tensor_tensor(out=hb[:, :], in0=mt[:, :], scalar=rs[:, 0:1], in1=GB[:, 1024:2048],
                                                                 op0=ALU.mult, op1=ALU.mult), [mt, rs, GB], [hb])
                    bank = PT.next()
                    for c in range(8):
                        S.op("pe", lambda e, c=c: e.transpose(bank[:, c * 128:(c + 1) * 128], hb[:, c * 128:(c + 1) * 128], IDB()),
                             [hb, CB], [bank], signal=(c == 7))
                    copy(evac_eng(), MT[:, :, mb_ * 128:(mb_ + 1) * 128], bank[:, 0:1024].rearrange("p (c t) -> p c t", c=8), [bank], [MT])
                for which in ("w_mk", "w_mv"):
                    for mb_ in range(2):
                        stg = MST.next()
                        for ng in range(2):
                            wb = load_wblk(which, 0, ng * 512)
                            bank = P4.next()
                            for c in range(8):
                                S.op("pe", lambda e: e.matmul(bank[:, :], MT[:, c, mb_ * 128:(mb_ + 1) * 128], wb[:, c, :], start=(c == 0), stop=(c == 7)),
                                     [MT, wb], [bank], signal=(c == 7))
                            if which == "w_mk":
                                ss = SMALL.next()
                                for hh in range(2):
                                    S.op("act", lambda e: e.activation(out=JUNK[:, 0:256], in_=bank[:, hh * 256:(hh + 1) * 256], func=AF.Square,
                                                                       accum_out=ss[:, hh:hh + 1]), [bank], [JUNK, ss])
                                rs = rstd_from_ss(ss, 0, 2, 256.0)
                                for hh in range(2):
                                    S.op("dve", lambda e: e.scalar_tensor_tensor(
                                        out=stg[:, ng * 512 + hh * 256:ng * 512 + (hh + 1) * 256], in0=bank[:, hh * 256:(hh + 1) * 256],
                                        scalar=rs[:, hh:hh + 1], in1=GB[:, 3200:3456], op0=ALU.mult, op1=ALU.mult), [bank, rs, GB], [stg])
                            else:
                                copy("act", stg[:, ng * 512:(ng + 1) * 512], bank[:, :], [bank], [stg])
                        dst = p_mk if which == "w_mk" else p_mv
                        S.dma("sp", dst[l, mb_ * 128:(mb_ + 1) * 128, :], stg[:, :], [stg], [])
                        if which == "w_mk":
                            mbb = MB.next()
                            copy("pool", mbb[:, :], stg[:, :], [stg], [mbb])
                            bank2 = PT.next()
                            for c in range(8):
                                S.op("pe", lambda e: e.transpose(bank2[:, c * 128:(c + 1) * 128], mbb[:, c * 128:(c + 1) * 128], IDB()),
                                     [mbb, CB], [bank2], signal=(c == 7))
                            copy(evac_eng(), MKT[0][:, :, mb_ * 128:(mb_ + 1) * 128], bank2[:, 0:1024].rearrange("p (c t) -> p c t", c=8),
                                 [bank2], [MKT[0]])
                        else:
                            copy("pool", MV[0][:, mb_, :], stg[:, :], [stg], [MV[0]])
                for mb_ in range(2):
                    mbb = MB.next()
                    m32 = XTOK.next()
                    S.dma("sp", m32[:, :], cmk[l, mb_ * 128:(mb_ + 1) * 128, :], [], [m32])
                    copy("pool", mbb[:, :], m32[:, :], [m32], [mbb])
                    bank2 = PT.next()
                    for c in range(8):
                        S.op("pe", lambda e, c=c: e.transpose(bank2[:, c * 128:(c + 1) * 128], mbb[:, c * 128:(c + 1) * 128], IDB()),
                             [mbb, CB], [bank2], signal=(c == 7))
                    copy(evac_eng(), MKT[1][:, :, mb_ * 128:(mb_ + 1) * 128], bank2[:, 0:1024].rearrange("p (c t) -> p c t", c=8), [bank2], [MKT[1]])
                    v32 = XTOK.next()
                    S.dma("sp", v32[:, :], cmv[l, mb_ * 128:(mb_ + 1) * 128, :], [], [v32])
                    copy("pool", MV[1][:, mb_, :], v32[:, :], [v32], [MV[1]])

                S.ck("memkv%d" % l)

                def fm_rmsnorm(W_, gcol0, out_buf):
                    for c in range(8):
                        S.op("act", lambda e, c=c: e.activation(out=SQD[:, c, 0:W_], in_=XTL[:, c, 0:W_], func=AF.Square), [XTL], [SQD])
                    bank = PP.next()
                    for c in range(8):
                        S.op("pe", lambda e, c=c: e.matmul(bank[:, 0:W_], ONESB[:, :], SQD[:, c, 0:W_], start=(c == 0), stop=(c == 7)),
                             [ONESB, SQD], [bank], signal=(c == 7))
                    t1 = RS.next()
                    S.op("act", lambda e: e.activation(out=t1[:, 0:W_], in_=bank[:, 0:W_], func=AF.Ln, scale=1.0 / 1024.0, bias=EPSC[:, 0:1]),
                         [bank, EPSC], [t1])
                    r = RS.next()
                    S.op("act", lambda e: e.activation(out=r[:, 0:W_], in_=t1[:, 0:W_], func=AF.Exp, scale=-0.5), [t1], [r])
                    for c in range(8):
                        S.op("dve", lambda e, c=c: e.scalar_tensor_tensor(out=out_buf[:, c, 0:W_], in0=XTL[:, c, 0:W_], scalar=GC[:, gcol0 + c:gcol0 + c + 1],
                                                                          in1=r[:, 0:W_], op0=ALU.mult, op1=ALU.mult), [XTL, GC, r], [out_buf])

                def dense(name, in_buf, kc0, nk8, n0, nm, W_, out_fn, wk0=0):
                    for mg in range(0, nm, 4):
                        if nk8 == 1:
                            wb = load_wblk(name, wk0, n0 + mg * 128)
                            for m in range(4):
                                bank = P6.next()
                                for c in range(8):
                                    S.op("pe", lambda e: e.matmul(bank[:, 0:W_], wb[:, c, m * 128:(m + 1) * 128], in_buf[:, kc0 + c, 0:W_],
                                                                  start=(c == 0), stop=(c == 7)), [wb, in_buf], [bank], signal=(c == 7))
                                out_fn(mg + m, bank)
                            continue
                        banks = [P6.next() for _ in range(4)]
                        for kg in range(nk8):
                            wb = load_wblk(name, wk0 + kg * 8, n0 + mg * 128)
                            for m in range(4):
                                for c in range(8):
                                    first = (kg == 0 and c == 0)
                                    lastk = (kg == nk8 - 1 and c == 7)
                                    S.op("pe", lambda e: e.matmul(
                                        banks[m][:, 0:W_], wb[:, c, m * 128:(m + 1) * 128], in_buf[:, kc0 + kg * 8 + c, 0:W_], start=first, stop=lastk),
                                         [wb, in_buf], [banks[m]], signal=(c == 7))
                        for m in range(4):
                            out_fn(mg + m, banks[m])

                def add_to_x(W_):
                    def f(m, bank):
                        S.op("dve", lambda e: e.tensor_tensor(out=XTL[:, m, 0:W_], in0=bank[:, 0:W_], in1=XTL[:, m, 0:W_], op=ALU.add), [bank, XTL], [XTL])
                    return f

                for ti in range(9):
                    W_ = 512 if ti < 8 else 128
                    c0 = ti * 512
                    si = 0 if ti < 8 else 1
                    nblk = W_ // 128
                    for tb in range(nblk):
                        blk = ti * 4 + tb
                        ot = XTOK.next()
                        S.dma("sp", ot[:, :], OT[blk * 128:(blk + 1) * 128, :], [OTk[blk]], [ot])
                        ss = SMALL.next()
                        for hf in range(2):
                            S.op("act", lambda e, hf=hf: e.activation(out=JUNK[:, 0:512], in_=ot[:, hf * 512:(hf + 1) * 512], func=AF.Square,
                                                                       accum_out=ss[:, hf:hf + 1]), [ot], [JUNK, ss])
                        rs = rstd_from_ss(ss, 0, 2, 512.0)
                        hb = HB.next()
                        for hf in range(2):
                            S.op("dve", lambda e, hf=hf: e.scalar_tensor_tensor(out=hb[:, hf * 512:(hf + 1) * 512], in0=ot[:, hf * 512:(hf + 1) * 512],
                                                                                scalar=rs[:, hf:hf + 1], in1=GB[:, 2048 + hf * 512:2048 + (hf + 1) * 512],
                                                                                op0=ALU.mult, op1=ALU.mult), [ot, rs, GB], [hb])
                        bank = PT.next()
                        for c in range(8):
                            S.op("pe", lambda e, c=c: e.transpose(bank[:, c * 128:(c + 1) * 128], hb[:, c * 128:(c + 1) * 128], IDB()),
                                 [hb, CB], [bank], signal=(c == 7))
                        copy(evac_eng(), ACTA[:, :, tb * 128:(tb + 1) * 128], bank[:, 0:1024].rearrange("p (c t) -> p c t", c=8), [bank], [ACTA])
                    S.ck("c1_%d_%d" % (l, ti))
                    S.dma("sp", XTL[:, :, 0:W_], XT[:, :, c0:c0 + W_], XTk[ti * 4:ti * 4 + nblk], [XTL])
                    dense("w_out", ACTA, 0, 1, 0, 8, W_, add_to_x(W_))
                    S.ck("wout_%d_%d" % (l, ti))
                    fm_rmsnorm(W_, 8, ACTA)

                    def q_out(m, bank):
                        S.op("act", lambda e: e.activation(out=QM[:, m, 0:W_], in_=bank[:, 0:W_], func=AF.Copy), [bank], [QM])
                        S.op("act", lambda e: e.activation(out=SQD[:, m, 0:W_], in_=bank[:, 0:W_], func=AF.Square), [bank], [SQD])
                    dense("w_mq", ACTA, 0, 1, 0, 8, W_, q_out)
                    for hh in range(4):
                        bank = PP.next()
                        for c in range(2):
                            S.op("pe", lambda e, c=c: e.matmul(bank[:, 0:W_], ONESB[:, :], SQD[:, 2 * hh + c, 0:W_], start=(c == 0), stop=(c == 1)),
                                 [ONESB, SQD], [bank], signal=(c == 1))
                        t1 = RS.next()
                        S.op("act", lambda e: e.activation(out=t1[:, 0:W_], in_=bank[:, 0:W_], func=AF.Ln, scale=1.0 / 256.0, bias=EPSC[:, 0:1]),
                             [bank, EPSC], [t1])
                        r = RS.next()
                        S.op("act", lambda e: e.activation(out=r[:, 0:W_], in_=t1[:, 0:W_], func=AF.Exp, scale=-0.5), [t1], [r])
                        for c in range(2):
                            S.op("dve", lambda e, c=c: e.scalar_tensor_tensor(out=ACTB[:, 2 * hh + c, 0:W_], in0=QM[:, 2 * hh + c, 0:W_],
                                                                              scalar=GC[:, 24 + c:25 + c], in1=r[:, 0:W_], op0=ALU.mult, op1=ALU.mult),
                                 [QM, GC, r], [ACTB])
                    for hh in range(4):
                        for mc in range(2):
                            bank = P4.next()
                            for c in range(2):
                                S.op("pe", lambda e, c=c: e.matmul(bank[:, 0:W_], MKT[si][:, 2 * hh + c, mc * 128:(mc + 1) * 128], ACTB[:, 2 * hh + c, 0:W_],
                                                                   start=(c == 0), stop=(c == 1)), [MKT[si], ACTB], [bank], signal=(c == 1))
                            S.op("act", lambda e: e.activation(out=PTB[:, mc, 0:W_], in_=bank[:, 0:W_], func=AF.Exp), [bank], [PTB])
                        bank = PP.next()
                        for mc in range(2):
                            S.op("pe", lambda e, mc=mc: e.matmul(bank[:, 0:W_], ONESB[:, :], PTB[:, mc, 0:W_], start=(mc == 0), stop=(mc == 1)),
                                 [ONESB, PTB], [bank], signal=(mc == 1))
                        t1 = RS.next()
                        S.op("act", lambda e: e.activation(out=t1[:, 0:W_], in_=bank[:, 0:W_], func=AF.Ln), [bank], [t1])
                        rd = RS.next()
                        S.op("act", lambda e: e.activation(out=rd[:, 0:W_], in_=t1[:, 0:W_], func=AF.Exp, scale=-1.0), [t1], [rd])
                        for dc in range(2):
                            bank = P4.next()
                            for mc in range(2):
                                S.op("pe", lambda e, mc=mc: e.matmul(bank[:, 0:W_], MV[si][:, mc, (2 * hh + dc) * 128:(2 * hh + dc + 1) * 128], PTB[:, mc, 0:W_],
                                                                     start=(mc == 0), stop=(mc == 1)), [MV[si], PTB], [bank], signal=(mc == 1))
                            S.op("dve", lambda e: e.tensor_tensor(out=ACTA[:, 2 * hh + dc, 0:W_], in0=bank[:, 0:W_], in1=rd[:, 0:W_], op=ALU.mult),
                                 [bank, rd], [ACTA])
                    dense("w_mo", ACTA, 0, 1, 0, 8, W_, add_to_x(W_))
                    S.ck("cross_%d_%d" % (l, ti))
                    fm_rmsnorm(W_, 16, ACTB)
                    for half in range(2):
                        def h_out(m, bank):
                            rl = RL.next()
                            S.op("act", lambda e: e.activation(out=rl[:, 0:W_], in_=bank[:, 0:W_], func=AF.Relu), [bank], [rl])
                            S.op("pool", lambda e: e.tensor_tensor(out=HID[:, m, 0:W_], in0=rl[:, 0:W_], in1=rl[:, 0:W_], op=ALU.mult), [rl], [HID])
                        dense("w_ff1", ACTB, 0, 1, half * 2048, 16, W_, h_out)
                        dense("w_ff2", HID, 0, 2, 0, 8, W_, add_to_x(W_), wk0=half * 16)
                    S.ck("ffn_%d_%d" % (l, ti))
                    if l == 0:
                        S.dma("sp", XT[:, :, c0:c0 + W_], XTL[:, :, 0:W_], [XTL], XTk[ti * 4:ti * 4 + nblk])
                        fm_rmsnorm(W_, 0, ACTA)
                        S.dma("sp", HTd[:, :, c0:c0 + W_], ACTA[:, :, 0:W_], [ACTA], HTk[ti * 4:ti * 4 + nblk])
                    else:
                        for tb in range(nblk):
                            blk = ti * 4 + tb
                            yt = XTOK.next()
                            for half in range(2):
                                bank = P4.next()
                                for c in range(4):
                                    cc = half * 4 + c
                                    S.op("pe", lambda e, c=c, cc=cc: e.transpose(bank[:, c * 128:(c + 1) * 128], XTL[:, cc, tb * 128:(tb + 1) * 128], IDF()),
                                         [XTL, CF], [bank], signal=(c == 3))
                                copy(evac_eng(), yt[:, half * 512:(half + 1) * 512], bank[:, 0:512], [bank], [yt])
                            if blk < 32:
                                S.dma("sp", yp[blk * 128:(blk + 1) * 128, :], yt[:, :], [yt], [])
                            else:
                                S.dma("sp", ys[:, :], yt[0:16, :], [yt], [])
                    S.ck("tile_%d_%d" % (l, ti))
                S.barrier()

        S.dead = False
        S.barrier(["sp"])
        print("instructions emitted:", S.ninst, "sp dmas:", sum(v for k, v in S.val.items() if k.startswith("D:sp")) // 16,
              {k: v for k, v in S.val.items() if k.startswith("E:")})
    return nc


_NC_CACHE = {}


def _consts():
    c = np.zeros((128, NCST), np.float32)
    idx = np.arange(128)
    c[:, 0:128] = np.eye(128, dtype=np.float32)
    c[:, 128:256] = (idx[:, None] <= idx[None, :]).astype(np.float32)
    c[127, 256:384] = 1.0
    c[:, 384:512] = np.where(idx[None, :] > idx[:, None], NEGM, 0.0)
    c[:, 512:640] = np.where(idx[None, :] >= idx[:, None], NEGM, 0.0)
    for r in range(24):
        h = r % 8
        c[r, 640 + h * 128:640 + (h + 1) * 128] = 1.0
    return c


def kernel(x_prompt, x_sample, mem_prompt, cache_fox_k, cache_fox_v, cache_fox_logf, cache_sb_k, cache_sb_v,
           cache_mem_k, cache_mem_v, g_mix, w_in, b_forget, g_fox_q, g_fox_k, g_out_fox, g_out_sb, w_out,
           g_cross, g_mem, w_mq, w_mk, w_mv, g_mq, g_mk, w_mo, g_ffn, w_ff1, w_ff2):
    f = lambda a: np.ascontiguousarray(np.asarray(a, dtype=np.float32))
    if "nc" not in _NC_CACHE:
        _NC_CACHE["nc"] = build()
    nc = _NC_CACHE["nc"]
    gbp = np.zeros((2, 128, NGB), np.float32)
    gcp = np.zeros((2, 128, NGC), np.float32)
    for l in range(2):
        row = np.concatenate([f(g_mix)[l], f(g_mem)[l], f(g_out_fox)[l], f(g_out_sb)[l], f(g_fox_q)[l], f(g_fox_k)[l],
                              f(g_mk)[l], f(b_forget)[l]])
        gbp[l] = np.broadcast_to(row[None, :], (128, NGB))
        gcp[l, :, 0:8] = f(g_mix)[min(l + 1, 1)].reshape(8, 128).T
        gcp[l, :, 8:16] = f(g_cross)[l].reshape(8, 128).T
        gcp[l, :, 16:24] = f(g_ffn)[l].reshape(8, 128).T
        gcp[l, :, 24:26] = f(g_mq)[l].reshape(2, 128).T
    cst = _consts()
    shared = {"w_in": f(w_in), "w_out": f(w_out), "w_mq": f(w_mq), "w_mk": f(w_mk), "w_mv": f(w_mv), "w_mo": f(w_mo),
              "w_ff1": f(w_ff1), "w_ff2": f(w_ff2), "gb": gbp, "gc": gcp, "cst": cst}
    in_maps = []
    for b in range(8):
        xs_pad = np.zeros((TS, D), np.float32)
        xs_pad[0:16] = f(x_sample)[b]
        m = dict(shared)
        m.update({
            "xp": f(x_prompt)[b], "xs": xs_pad, "mem": f(mem_prompt)[b],
            "cfk": f(cache_fox_k)[:, b].reshape(2, 2048, 512), "cfv": f(cache_fox_v)[:, b].reshape(2, 2048, 512),
            "clf": f(cache_fox_logf)[:, b], "csk": f(cache_sb_k)[:, b].reshape(2, 2048, 512),
            "csv": f(cache_sb_v)[:, b].reshape(2, 2048, 512),
            "cmk": f(cache_mem_k)[:, b].reshape(2, 256, 1024), "cmv": f(cache_mem_v)[:, b].reshape(2, 256, 1024),
        })
        m = {k: np.ascontiguousarray(v) for k, v in m.items()}
        in_maps.append(m)
    res = run_bass_kernel_spmd(nc, in_maps, core_ids=list(range(8)))
    R = res.results

    def g(name):
        return np.stack([np.asarray(R[b][name]) for b in range(8)], axis=0)

    def kv(name, t):
        return np.ascontiguousarray(np.transpose(g(name), (1, 0, 2, 3)).reshape(2, 8, t, 8, 64))

    y_p = g("yp")
    y_s = g("ys")
    outs = (y_p, y_s,
            kv("p_fk", T), kv("p_fv", T), np.ascontiguousarray(np.transpose(g("p_lf"), (1, 0, 2, 3))),
            kv("p_sk", T), kv("p_sv", T),
            np.ascontiguousarray(np.transpose(g("p_mk"), (1, 0, 2, 3)).reshape(2, 8, 256, 4, 256)),
            np.ascontiguousarray(np.transpose(g("p_mv"), (1, 0, 2, 3)).reshape(2, 8, 256, 4, 256)),
            kv("s_fk", 16), kv("s_fv", 16), np.ascontiguousarray(np.transpose(g("s_lf"), (1, 0, 2, 3))),
            kv("s_sk", 16), kv("s_sv", 16))
    return tuple(np.asarray(o, dtype=np.float32) for o in outs)
```

```python
import os
import numpy as np
from contextlib import ExitStack
import concourse.bass as bass
import concourse.mybir as mybir
from concourse.bass_utils import run_bass_kernel_spmd

F32 = mybir.dt.float32
BF16 = mybir.dt.bfloat16
AF = mybir.ActivationFunctionType
ALU = mybir.AluOpType
AX = mybir.AxisListType

T = 4096
TS = 128
TT = T + TS
D = 1024
NQB = 33
NKB = 49
EPS = 1e-6
NEGM = -30000.0
NGB = 3464
NGC = 26
NCST = 1664
NCH = 10
SAME_ENGINE_SYNC = os.environ.get("KSES", "1") == "1"
SERIAL = os.environ.get("KSER", "0")
PSUM_EXCL = os.environ.get("KPX", "1") == "1"


class Tk:
    __slots__ = ("w", "r")

    def __init__(self):
        self.w = None
        self.r = {}


class Buf:
    def __init__(self, t):
        self.t = t
        self.k = Tk()

    def __getitem__(self, key):
        return self.t[key]


class Rot:
    def __init__(self, bufs):
        self.bufs = bufs
        self.i = 0

    def next(self):
        b = self.bufs[self.i]
        self.i = (self.i + 1) % len(self.bufs)
        return b


class Sched:
    def __init__(self, nc, stack):
        self.nc = nc
        self.eng = {"pe": nc.tensor, "act": nc.scalar, "dve": nc.vector, "pool": nc.gpsimd, "sp": nc.sync}
        self.sem = {}
        self.val = {}
        for e in self.eng:
            self.sem["E:" + e] = stack.enter_context(nc.semaphore("sem_" + e))
            self.val["E:" + e] = 0
        for q in ("sp", "pool"):
            for c in range(NCH):
                k = "D:%s:%d" % (q, c)
                self.sem[k] = stack.enter_context(nc.semaphore("dsem_%s_%d" % (q, c)))
                self.val[k] = 0
        self.known = {e: {} for e in self.eng}
        self.rr = {"sp": 0, "pool": 0}
        self.ninst = 0
        self.dead = False
        self.stop = os.environ.get("KSTOP", "")
        self.serial = SERIAL
        self.last_tok = None
        self.last_ew = None

    def ck(self, name):
        if self.stop and name == self.stop:
            self.dead = True

    def _deps(self, eng, reads, writes):
        need = {}

        def add(k, v):
            if need.get(k, 0) < v:
                need[k] = v

        for b in reads:
            if b.k.w is not None:
                add(*b.k.w)
        for b in writes:
            if b.k.w is not None:
                add(*b.k.w)
            for k, v in b.k.r.items():
                add(k, v)
        if self.serial == "1" and self.last_tok is not None:
            add(*self.last_tok)
        if self.serial == "2" and eng in ("act", "dve", "pool") and self.last_ew is not None:
            add(*self.last_ew)
        waits = []
        kn = self.known[eng]
        for k, v in need.items():
            if k == "E:" + eng and (eng == "pe" or not SAME_ENGINE_SYNC):
                continue
            if kn.get(k, 0) >= v:
                continue
            kn[k] = v
            waits.append((k, v))
        return waits

    def _mark(self, tok, reads, writes):
        k, v = tok
        for b in reads:
            if b.k.r.get(k, 0) < v:
                b.k.r[k] = v
        for b in writes:
            b.k.w = tok
            b.k.r = {}

    def op(self, eng, fn, reads=(), writes=(), signal=True):
        assert signal or eng == "pe"
        if self.dead:
            return
        if PSUM_EXCL:
            px = [b for b in reads if getattr(b, "px", False)]
            if px:
                writes = list(writes) + px
        waits = self._deps(eng, reads, writes)
        e = self.eng[eng]
        for k, v in waits:
            e.wait_ge(self.sem[k], v)
        inst = fn(e)
        key = "E:" + eng
        if signal:
            self.val[key] += 1
            inst.then_inc(self.sem[key], 1)
            tok = (key, self.val[key])
        else:
            tok = (key, self.val[key] + 1)
        self._mark(tok, reads, writes)
        self.last_tok = tok
        if eng in ("act", "dve", "pool"):
            self.last_ew = tok
        self.ninst += 1 + len(waits)

    def dma(self, q, out, in_, reads=(), writes=()):
        if self.dead:
            return
        c = self.rr[q]
        self.rr[q] = (c + 1) % NCH
        key = "D:%s:%d" % (q, c)
        waits = self._deps(q, reads, writes)
        prev = self.val[key]
        if prev > 0 and self.known[q].get(key, 0) < prev:
            waits.append((key, prev))
            self.known[q][key] = prev
        e = self.eng[q]
        for k, v in waits:
            e.wait_ge(self.sem[k], v)
        e.dma_start(out=out, in_=in_).then_inc(self.sem[key], 16)
        self.val[key] = prev + 16
        self._mark((key, prev + 16), reads, writes)
        self.last_tok = (key, prev + 16)
        self.ninst += 1 + len(waits)

    def barrier(self, engines=None):
        for e in (engines or list(self.eng)):
            kn = self.known[e]
            for k, v in self.val.items():
                if v > 0 and kn.get(k, 0) < v:
                    self.eng[e].wait_ge(self.sem[k], v)
                    kn[k] = v
                    self.ninst += 1


def build():
    nc = bass.Bass("TRN2", target_bir_lowering=False)

    def din(name, shape):
        return nc.dram_tensor(name, shape, F32, kind="ExternalInput").ap()

    def dout(name, shape):
        return nc.dram_tensor(name, shape, F32, kind="ExternalOutput").ap()

    def dint(name, shape, dt):
        return nc.dram_tensor(name, shape, dt, kind="Internal").ap()

    xp = din("xp", [T, D])
    xs = din("xs", [TS, D])
    mem = din("mem", [256, D])
    cfk = din("cfk", [2, 2048, 512])
    cfv = din("cfv", [2, 2048, 512])
    clf = din("clf", [2, 2048, 8])
    csk = din("csk", [2, 2048, 512])
    csv = din("csv", [2, 2048, 512])
    cmk = din("cmk", [2, 256, 1024])
    cmv = din("cmv", [2, 256, 1024])
    wshapes = {"w_in": [1024, 3080], "w_out": [1024, 1024], "w_mq": [1024, 1024], "w_mk": [1024, 1024],
               "w_mv": [1024, 1024], "w_mo": [1024, 1024], "w_ff1": [1024, 4096], "w_ff2": [4096, 1024]}
    W32 = {n: din(n, [2] + s) for n, s in wshapes.items()}
    gb = din("gb", [2, 128, NGB])
    gc = din("gc", [2, 128, NGC])
    cst = din("cst", [128, NCST])

    yp = dout("yp", [T, D])
    ys = dout("ys", [16, D])
    p_fk = dout("p_fk", [2, T, 512])
    p_fv = dout("p_fv", [2, T, 512])
    p_lf = dout("p_lf", [2, T, 8])
    p_sk = dout("p_sk", [2, T, 512])
    p_sv = dout("p_sv", [2, T, 512])
    p_mk = dout("p_mk", [2, 256, 1024])
    p_mv = dout("p_mv", [2, 256, 1024])
    s_fk = dout("s_fk", [2, 16, 512])
    s_fv = dout("s_fv", [2, 16, 512])
    s_lf = dout("s_lf", [2, 16, 8])
    s_sk = dout("s_sk", [2, 16, 512])
    s_sv = dout("s_sv", [2, 16, 512])

    XT = dint("XT", [128, 8, TT], F32)
    HTd = dint("HTd", [128, 8, TT], BF16)
    OT = dint("OT", [TT, 1024], F32)
    WB = {n: dint("b_" + n, [2] + s, BF16) for n, s in wshapes.items()}

    with ExitStack() as stack:
        S = Sched(nc, stack)

        uniq = [0]

        def sb(st, name, shape, dt):
            uniq[0] += 1
            nm = "%s_%d" % (name, uniq[0])
            b = Buf(st.enter_context(nc.sbuf_tensor(nm, shape, dt)))
            if os.environ.get("KADDR"):
                ml = nc.lookup_mloc(nm)
                print("ADDR", nm, ml.addr, ml.addr + int(np.prod(shape[1:])) * (4 if dt == F32 else 2))
            return b

        def ps(name, shape, dt):
            b = Buf(stack.enter_context(nc.psum_tensor(name, shape, dt)))
            b.px = True
            return b

        XTk = [Buf(None) for _ in range(NQB)]
        HTk = [Buf(None) for _ in range(NQB)]
        OTk = [Buf(None) for _ in range(NQB)]
        WBk = {n: [[] for _ in range(2)] for n in wshapes}

        CF = sb(stack, "CF", [128, 640], F32)
        CB = sb(stack, "CB", [128, NCST], BF16)
        ONESB = sb(stack, "ONESB", [128, 128], BF16)
        ONESF = sb(stack, "ONESF", [128, 512], F32)
        EPSC = sb(stack, "EPSC", [128, 1], F32)
        ONEC = sb(stack, "ONEC", [128, 1], F32)
        GB = sb(stack, "GB", [128, NGB], F32)
        GC = sb(stack, "GC", [128, NGC], F32)
        JUNK = sb(stack, "JUNK", [128, 1024], F32)
        XTOK = Rot([sb(stack, "xtok%d" % i, [128, 1024], F32) for i in range(2)])
        HB = Rot([sb(stack, "hb%d" % i, [128, 1024], BF16) for i in range(2)])
        XTT = Rot([sb(stack, "xtt%d" % i, [128, 8, 128], F32) for i in range(2)])
        HTT = Rot([sb(stack, "htt%d" % i, [128, 8, 128], BF16) for i in range(2)])
        SMALL = Rot([sb(stack, "small%d" % i, [128, 16], F32) for i in range(8)])

        PZ = Rot([ps("pz%d" % i, [128, 512], F32) for i in range(2)])
        PO = Rot([ps("po%d" % i, [128, 512], F32) for i in range(2)])
        PP = Rot([ps("pp%d" % i, [128, 512], F32) for i in range(2)])
        PT = Rot([ps("pt%d" % i, [128, 1024], BF16) for i in range(2)])
        P4 = Rot(PZ.bufs + PO.bufs)
        ZB = Rot(PZ.bufs + PP.bufs)
        P6 = Rot(PZ.bufs + PO.bufs + PP.bufs)

        IDF = lambda: CF[:, 0:128]
        TRIU = lambda: CF[:, 128:256]
        SEL127 = lambda: CF[:, 256:384]
        IDB = lambda: CB[:, 0:128]
        NEGI = lambda: CB[:, 384:512]
        NEGS = lambda: CB[:, 512:640]
        SELT = lambda h: CB[0:24, 640 + h * 128:640 + (h + 1) * 128]

        tog = [0]

        def evac_eng():
            tog[0] ^= 1
            return "act" if tog[0] else "dve"

        def copy(eng, out, in_, reads, writes):
            if eng == "act":
                S.op("act", lambda e: e.activation(out=out, in_=in_, func=AF.Copy), reads, writes)
            elif eng == "dve":
                S.op("dve", lambda e: e.tensor_copy(out=out, in_=in_), reads, writes)
            else:
                S.op("pool", lambda e: e.tensor_copy(out=out, in_=in_), reads, writes)

        def rstd_from_ss(ss, g0, g1, n):
            ms = SMALL.next()
            S.op("dve", lambda e: e.tensor_scalar(out=ms[:, g0:g1], in0=ss[:, g0:g1], scalar1=1.0 / n, scalar2=EPS,
                                                  op0=ALU.mult, op1=ALU.add), [ss], [ms])
            ln = SMALL.next()
            S.op("act", lambda e: e.activation(out=ln[:, g0:g1], in_=ms[:, g0:g1], func=AF.Ln), [ms], [ln])
            rs = SMALL.next()
            S.op("act", lambda e: e.activation(out=rs[:, g0:g1], in_=ln[:, g0:g1], func=AF.Exp, scale=-0.5), [ln], [rs])
            return rs

        st0 = ExitStack()
        CST32 = sb(st0, "CST32", [128, NCST], F32)
        STG = Rot([sb(st0, "stg%d" % i, [128, 4096], F32) for i in range(3)])
        STB = Rot([sb(st0, "stb%d" % i, [128, 4096], BF16) for i in range(3)])
        S.dma("sp", CF[:, :], cst[:, 0:640], [], [CF])
        S.dma("sp", CST32[:, :], cst[:, :], [], [CST32])
        copy("pool", CB[:, :], CST32[:, :], [CST32], [CB])
        S.op("dve", lambda e: e.memset(ONESB[:, :], 1.0), [], [ONESB])
        S.op("dve", lambda e: e.memset(ONESF[:, :], 1.0), [], [ONESF])
        S.op("dve", lambda e: e.memset(EPSC[:, :], EPS), [], [EPSC])
        S.op("dve", lambda e: e.memset(ONEC[:, :], 1.0), [], [ONEC])

        def load_gains(l):
            S.dma("sp", GB[:, :], gb[l], [], [GB])
            S.dma("sp", GC[:, :], gc[l], [], [GC])
            S.op("dve", lambda e: e.tensor_scalar(out=GB[:, 3072:3136], in0=GB[:, 3072:3136], scalar1=0.125, scalar2=None,
                                                  op0=ALU.mult), [GB], [GB])
            S.op("dve", lambda e: e.tensor_scalar(out=GC[:, 24:26], in0=GC[:, 24:26], scalar1=1.0 / 16.0, scalar2=None,
                                                  op0=ALU.mult), [GC], [GC])

        conv_jobs = []
        for l_ in range(2):
            for n in ["w_in", "w_out", "w_mq", "w_mk", "w_mv", "w_mo", "w_ff1", "w_ff2"]:
                K_, N_ = wshapes[n]
                F_ = K_ * N_ // 128
                src = W32[n][l_].rearrange("k n -> (k n)").rearrange("(p f) -> p f", p=128)
                dst = WB[n][l_].rearrange("k n -> (k n)").rearrange("(p f) -> p f", p=128)
                for f0 in range(0, F_, 4096):
                    f1 = min(F_, f0 + 4096)
                    conv_jobs.append((n, l_, src[:, f0:f1], dst[:, f0:f1], f1 - f0))

        def convert_some(k):
            for _ in range(k):
                if not conv_jobs:
                    return
                n, l_, src, dst, w = conv_jobs.pop(0)
                a = STG.next()
                b = STB.next()
                S.dma("sp", a[:, 0:w], src, [], [a])
                copy("pool", b[:, 0:w], a[:, 0:w], [a], [b])
                kk = Buf(None)
                S.dma("sp", dst, b[:, 0:w], [b], [kk])
                WBk[n][l_].append(kk)

        S.ck("pre0")
        load_gains(0)
        S.ck("pre1")
        convert_some(7)
        S.ck("pre2")

        def to_feature_major_f32(src_buf, blk):
            xtt = XTT.next()
            for half in range(2):
                bank = P4.next()
                for c in range(4):
                    cc = half * 4 + c
                    S.op("pe", lambda e, c=c, cc=cc: e.transpose(bank[:, c * 128:(c + 1) * 128], src_buf[:, cc * 128:(cc + 1) * 128], IDF()),
                         [src_buf, CF], [bank], signal=(c == 3))
                copy(evac_eng(), xtt[:, half * 4:(half + 1) * 4, :], bank[:, 0:512].rearrange("p (c t) -> p c t", c=4), [bank], [xtt])
            S.dma("sp", XT[:, :, blk * 128:(blk + 1) * 128], xtt[:, :, :], [xtt], [XTk[blk]])

        def tok_to_hT(hb, blk, dst, dstk):
            bank = PT.next()
            for c in range(8):
                S.op("pe", lambda e, c=c: e.transpose(bank[:, c * 128:(c + 1) * 128], hb[:, c * 128:(c + 1) * 128], IDB()),
                     [hb, CB], [bank], signal=(c == 7))
            S.ck("p0b1")
            htt = HTT.next()
            copy(evac_eng(), htt[:, :, :], bank[:, 0:1024].rearrange("p (c t) -> p c t", c=8), [bank], [htt])
            S.ck("p0b2")
            S.dma("sp", dst[:, :, blk * 128:(blk + 1) * 128], htt[:, :, :], [htt], [dstk])

        for blk in range(NQB):
            xt = XTOK.next()
            src = xp[blk * 128:(blk + 1) * 128, :] if blk < 32 else xs[:, :]
            S.dma("sp", xt[:, :], src, [], [xt])
            SKIP = os.environ.get("KSKIP", "")
            if "a" not in SKIP:
                to_feature_major_f32(xt, blk)
            S.ck("p0a")
            if "b" in SKIP:
                continue
            ss = SMALL.next()
            S.op("act", lambda e: e.activation(out=JUNK[:, :], in_=xt[:, :], func=AF.Square, accum_out=ss[:, 0:1]), [xt], [JUNK, ss])
            rs = rstd_from_ss(ss, 0, 1, 1024.0)
            hb = HB.next()
            S.op("dve", lambda e: e.scalar_tensor_tensor(out=hb[:, :], in0=xt[:, :], scalar=rs[:, 0:1], in1=GB[:, 0:1024],
                                                         op0=ALU.mult, op1=ALU.mult), [xt, rs, GB], [hb])
            S.ck("p0b")
            tok_to_hT(hb, blk, HTd, HTk[blk])
            S.ck("p0c")
            S.ck("p0c_%d" % blk)
            convert_some(2)
        convert_some(1000)
        S.barrier()
        st0.close()


        if os.environ.get("KDBG"):
            dbb = HB.next()
            dbf = XTOK.next()
            for j, blk_ in enumerate((6, 5, 7, 12)):
                S.dma("sp", yp[j * 256:j * 256 + 128, :].rearrange("p (c t) -> p c t", c=8), XT[:, :, blk_ * 128:(blk_ + 1) * 128], [XTk[blk_]], [])
                S.dma("sp", dbb[:, :].rearrange("p (c t) -> p c t", c=8), HTd[:, :, blk_ * 128:(blk_ + 1) * 128], [HTk[blk_]], [dbb])
                S.op("dve", lambda e: e.tensor_copy(out=dbf[:, :], in_=dbb[:, :]), [dbb], [dbf])
                S.dma("sp", yp[j * 256 + 128:j * 256 + 256, :], dbf[:, :], [dbf], [])
        S.ck("p0")
        for l in range(2):
            if l == 1:
                load_gains(1)
            with ExitStack() as st:
                if os.environ.get("KPAD"):
                    PAD = sb(st, "PAD", [128, int(os.environ["KPAD"])], F32)
                WP = sb(st, "WP", [128, 8, 384], BF16)
                WF = sb(st, "WF", [128, 8, 8], BF16)
                HT = Rot([sb(st, "hT%d" % i, [128, 8, 512], BF16) for i in range(2)])
                QTA = sb(st, "QTA", [128, NQB * 128], BF16)
                QTB = sb(st, "QTB", [128, NQB * 128], BF16)
                KT = sb(st, "KT", [128, NKB * 128], BF16)
                V = sb(st, "V", [128, NKB, 2, 66], BF16)
                KBC = sb(st, "KBC", [128, 16, 128], BF16)
                K32 = sb(st, "K32", [128, 16, 128], F32)
                V32 = sb(st, "V32", [128, 16, 128], F32)
                SQ = Rot([sb(st, "sq%d" % i, [128, 256], F32) for i in range(2)])
                KST = Rot([sb(st, "kst%d" % i, [128, 128], F32) for i in range(3)])
                VST = Rot([sb(st, "vst%d" % i, [128, 128], F32) for i in range(3)])
                QB = Rot([sb(st, "qb%d" % i, [128, 128], BF16) for i in range(3)])
                KB = Rot([sb(st, "kb%d" % i, [128, 128], BF16) for i in range(3)])
                EB = Rot([sb(st, "eb%d" % i, [128, 512], F32) for i in range(3)])
                SPB = Rot([sb(st, "spb%d" % i, [128, 512], F32) for i in range(3)])
                CBF = Rot([sb(st, "cbf%d" % i, [128, 512], F32) for i in range(2)])
                AB = Rot([sb(st, "ab%d" % i, [128, 512], BF16) for i in range(3)])
                ATB = Rot([sb(st, "atb%d" % i, [128, 512], BF16) for i in range(3)])
                OST = Rot([sb(st, "ost%d" % i, [128, 128], F32) for i in range(2)])
                LF = sb(st, "LF", [128, NKB, 8], F32)
                WC = sb(st, "WC", [128, NKB, 8], F32)
                TB = sb(st, "TB", [128, NKB, 8], F32)
                CS = sb(st, "CS", [128, NKB, 8], F32)
                FM = sb(st, "FM", [128, NKB, 8], F32)
                R1 = sb(st, "R1", [128, NKB, 8], F32)
                FS = sb(st, "FS", [128, NKB, 3, 8], BF16)
                FKT = sb(st, "FKT", [128, NKB * 128], BF16)
                FT1 = sb(st, "FT1", [128, NQB * 8], F32)
                FT2 = sb(st, "FT2", [128, NQB * 8], F32)

                S.op("dve", lambda e: e.memset(V[:, :, :, 64:65], 1.0), [], [V])
                S.op("dve", lambda e: e.memset(V[:, :, :, 65:66], 0.0), [], [V])
                S.op("dve", lambda e: e.memset(LF[:, :, :], 0.0), [], [LF])
                S.op("pool", lambda e: e.memset(QTA[:, :], 0.0), [], [QTA])
                S.op("pool", lambda e: e.memset(QTB[:, :], 0.0), [], [QTB])
                S.op("pool", lambda e: e.memset(FKT[:, :], 0.0), [], [FKT])

                for p in range(8):
                    fox = p < 4
                    pp = p % 4
                    base = 0 if fox else 1544
                    for j in range(3):
                        c0 = base + j * 512 + pp * 128
                        S.dma("sp", WP[:, :, j * 128:(j + 1) * 128],
                              WB["w_in"][l].rearrange("(c p) n -> p c n", p=128)[:, :, c0:c0 + 128], WBk["w_in"][l], [WP])
                    if p == 0:
                        S.dma("sp", WF[:, :, :], WB["w_in"][l].rearrange("(c p) n -> p c n", p=128)[:, :, 1536:1544],
                              WBk["w_in"][l], [WF])
                    ck, cv = (cfk, cfv) if fox else (csk, csv)
                    S.dma("sp", K32[:, :, :], ck[l].rearrange("(b p) n -> p b n", p=128)[:, :, pp * 128:(pp + 1) * 128], [], [K32])
                    copy("pool", KBC[:, :, :], K32[:, :, :], [K32], [KBC])
                    S.dma("sp", V32[:, :, :], cv[l].rearrange("(b p) n -> p b n", p=128)[:, :, pp * 128:(pp + 1) * 128], [], [V32])
                    copy("pool", V[:, 32:48, :, 0:64], V32[:, :, :].rearrange("p b (h d) -> p b h d", h=2), [V32], [V])
                    if p == 0:
                        S.dma("sp", LF[:, 32:48, :], clf[l].rearrange("(b p) h -> p b h", p=128), [], [LF])
                    S.ck("pj_a")
                    FL = PO.bufs[1]
                    ok_out, ov_out = (p_fk, p_fv) if fox else (p_sk, p_sv)
                    sk_out, sv_out = (s_fk, s_fv) if fox else (s_sk, s_sv)
                    for ti in range(9):
                        W_ = 512 if ti < 8 else 128
                        c0 = ti * 512
                        hT = HT.next()
                        S.dma("sp", hT[:, :, 0:W_], HTd[:, :, c0:c0 + W_], HTk[ti * 4:ti * 4 + W_ // 128], [hT])
                        bq = PT.next()
                        bk = PT.next()
                        for tb in range(W_ // 128):
                            blk = ti * 4 + tb
                            kblk = blk if blk < 32 else 48
                            bank = PP.next()
                            for c in range(8):
                                S.op("pe", lambda e, c=c: e.matmul(bank[:, 0:384], hT[:, c, tb * 128:(tb + 1) * 128], WP[:, c, :],
                                                                   start=(c == 0), stop=(c == 7)), [hT, WP], [bank], signal=(c == 7))
                            if p == 0 and "f" not in os.environ.get("KSKIP", ""):
                                for c in range(8):
                                    S.op("pe", lambda e, c=c: e.matmul(FL[:, blk * 8:(blk + 1) * 8], hT[:, c, tb * 128:(tb + 1) * 128], WF[:, c, :],
                                                                       start=(c == 0), stop=(c == 7)), [hT, WF], [FL], signal=(c == 7))
                            S.ck("pj_b")
                            S.ck("pj_b%d" % blk)
                            kst = KST.next()
                            vst = VST.next()
                            qb = QB.next()
                            kb = KB.next()
                            if fox:
                                sq = SQ.next()
                                S.op("act", lambda e: e.activation(out=sq[:, :], in_=bank[:, 0:256], func=AF.Square), [bank], [sq])
                                ss = SMALL.next()
                                S.op("dve", lambda e: e.tensor_reduce(out=ss[:, 0:4], in_=sq[:, :].rearrange("p (g d) -> p g d", g=4),
                                                                      axis=AX.X, op=ALU.add), [sq], [ss])
                                rs = rstd_from_ss(ss, 0, 4, 64.0)
                                for hh in range(2):
                                    S.op("dve", lambda e, hh=hh: e.scalar_tensor_tensor(
                                        out=qb[:, hh * 64:(hh + 1) * 64], in0=bank[:, hh * 64:(hh + 1) * 64], scalar=rs[:, hh:hh + 1],
                                        in1=GB[:, 3072:3136], op0=ALU.mult, op1=ALU.mult), [bank, rs, GB], [qb])
                                    S.op("dve", lambda e, hh=hh: e.scalar_tensor_tensor(
                                        out=kst[:, hh * 64:(hh + 1) * 64], in0=bank[:, 128 + hh * 64:128 + (hh + 1) * 64], scalar=rs[:, 2 + hh:3 + hh],
                                        in1=GB[:, 3136:3200], op0=ALU.mult, op1=ALU.mult), [bank, rs, GB], [kst])
                                copy("pool", kb[:, :], kst[:, :], [kst], [kb])
                            else:
                                S.op("act", lambda e: e.activation(out=qb[:, :], in_=bank[:, 0:128], func=AF.Copy, scale=0.125), [bank], [qb])
                                S.op("act", lambda e: e.activation(out=kst[:, :], in_=bank[:, 128:256], func=AF.Copy), [bank], [kst])
                                S.op("dve", lambda e: e.tensor_copy(out=kb[:, :], in_=bank[:, 128:256]), [bank], [kb])
                            S.op("act", lambda e: e.activation(out=vst[:, :], in_=bank[:, 256:384], func=AF.Copy), [bank], [vst])
                            S.op("dve", lambda e: e.tensor_copy(out=V[:, kblk, :, 0:64], in_=bank[:, 256:384].rearrange("p (h d) -> p h d", h=2)),
                                 [bank], [V])
                            S.ck("pj_c")
                            S.ck("pj_c%d" % blk)
                            if blk < 32:
                                S.dma("sp", ok_out[l, blk * 128:(blk + 1) * 128, pp * 128:(pp + 1) * 128], kst[:, :], [kst], [])
                                S.dma("sp", ov_out[l, blk * 128:(blk + 1) * 128, pp * 128:(pp + 1) * 128], vst[:, :], [vst], [])
                            else:
                                S.dma("sp", sk_out[l, :, pp * 128:(pp + 1) * 128], kst[0:16, :], [kst], [])
                                S.dma("sp", sv_out[l, :, pp * 128:(pp + 1) * 128], vst[0:16, :], [vst], [])
                            S.ck("pj_g%d" % blk)
                            S.op("pe", lambda e: e.transpose(bq[:, tb * 128:(tb + 1) * 128], qb[:, :], IDB()), [qb, CB], [bq])
                            S.op("pe", lambda e: e.transpose(bk[:, tb * 128:(tb + 1) * 128], kb[:, :], IDB()), [kb, CB], [bk])
                            S.ck("pj_h%d" % blk)
                        S.ck("pj_d")
                        S.ck("pj_d%d" % ti)
                        kc0 = c0 if ti < 8 else 48 * 128
                        copy("act", QTA[0:64, c0:c0 + W_], bq[0:64, 0:W_], [bq], [QTA])
                        copy("act", QTB[64:128, c0:c0 + W_], bq[64:128, 0:W_], [bq], [QTB])
                        copy("dve", KT[:, kc0:kc0 + W_], bk[:, 0:W_], [bk], [KT])
                        S.ck("pj_f%d" % ti)
                    S.ck("pj_e")
                    for g in range(4):
                        bk = PT.next()
                        for i in range(4):
                            S.op("pe", lambda e, i=i: e.transpose(bk[:, i * 128:(i + 1) * 128], KBC[:, g * 4 + i, :], IDB()), [KBC, CB], [bk], signal=(i == 3))
                        copy(evac_eng(), KT[:, (32 + g * 4) * 128:(36 + g * 4) * 128], bk[:, 0:512], [bk], [KT])

                    S.ck("proj%d_%d" % (l, p))
                    if p == 0:
                        NB8 = NQB * 8
                        S.op("dve", lambda e: e.tensor_tensor(out=FT1[:, :].rearrange("p (b h) -> p b h", h=8),
                                                              in0=FL[:, 0:NB8].rearrange("p (b h) -> p b h", h=8),
                                                              in1=GB[:, 3456:3464].unsqueeze(1).broadcast_to([128, NQB, 8]), op=ALU.add),
                             [FL, GB], [FT1])
                        S.op("act", lambda e: e.activation(out=FT2[:, :], in_=FT1[:, :], func=AF.Exp, scale=-1.0), [FT1], [FT2])
                        S.op("act", lambda e: e.activation(out=FT1[:, :], in_=FT2[:, :], func=AF.Ln, bias=ONEC[:, 0:1]), [FT2, ONEC], [FT1])
                        S.op("dve", lambda e: e.tensor_scalar(out=LF[:, 0:32, :], in0=FT1[:, 0:256].rearrange("p (b h) -> p b h", h=8),
                                                              scalar1=-1.0, scalar2=None, op0=ALU.mult), [FT1], [LF])
                        S.op("dve", lambda e: e.tensor_scalar(out=LF[:, 48, :], in0=FT1[:, 256:264], scalar1=-1.0, scalar2=None, op0=ALU.mult),
                             [FT1], [LF])
                        for q4 in range(4):
                            S.dma("sp", p_lf[l].rearrange("(b p) h -> p b h", p=128)[:, q4 * 8:(q4 + 1) * 8, :], LF[:, q4 * 8:(q4 + 1) * 8, :], [LF], [])
                        S.dma("sp", s_lf[l], LF[0:16, 48, :], [LF], [])
                        bank = PP.next()
                        S.op("pe", lambda e: e.matmul(bank[:, 0:NKB * 8], TRIU(), LF[:, :, :].rearrange("p b h -> p (b h)"), start=True, stop=True),
                             [CF, LF], [bank])
                        S.op("dve", lambda e: e.tensor_copy(out=WC[:, :, :].rearrange("p b h -> p (b h)"), in_=bank[:, 0:NKB * 8]), [bank], [WC])
                        bank2 = PP.next()
                        S.op("pe", lambda e: e.matmul(bank2[:, 0:NKB * 8], SEL127(), WC[:, :, :].rearrange("p b h -> p (b h)"), start=True, stop=True),
                             [CF, WC], [bank2])
                        S.op("act", lambda e: e.activation(out=TB[:, :, :].rearrange("p b h -> p (b h)"), in_=bank2[:, 0:NKB * 8], func=AF.Copy),
                             [bank2], [TB])
                        for (a, b_) in ((0, 32), (32, 49)):
                            for h in range(8):
                                S.op("dve", lambda e, h=h: e.tensor_tensor_scan(out=CS[:, a:b_, h], data0=ONESF[:, 0:b_ - a], data1=TB[:, a:b_, h],
                                                                                initial=0.0, op0=ALU.mult, op1=ALU.add), [TB, ONESF], [CS])
                        S.op("dve", lambda e: e.tensor_tensor(out=FM[:, :, :], in0=WC[:, :, :], in1=CS[:, :, :], op=ALU.add), [WC, CS], [FM])
                        S.op("dve", lambda e: e.tensor_tensor(out=FM[:, :, :], in0=FM[:, :, :], in1=TB[:, :, :], op=ALU.subtract), [FM, TB], [FM])
                        S.op("dve", lambda e: e.tensor_scalar(out=FS[:, :, 0, :], in0=FM[:, :, :], scalar1=-1.0, scalar2=None, op0=ALU.mult), [FM], [FS])
                        S.op("dve", lambda e: e.scalar_tensor_tensor(out=R1[:, :, :], in0=FM[:, :, :], scalar=-1.0, in1=FS[:, :, 0, :],
                                                                     op0=ALU.mult, op1=ALU.subtract), [FM, FS], [R1])
                        S.op("dve", lambda e: e.tensor_copy(out=FS[:, :, 1, :], in_=R1[:, :, :]), [R1], [FS])
                        S.op("dve", lambda e: e.tensor_tensor(out=R1[:, :, :], in0=R1[:, :, :], in1=FS[:, :, 1, :], op=ALU.subtract), [R1, FS], [R1])
                        S.op("dve", lambda e: e.tensor_copy(out=FS[:, :, 2, :], in_=R1[:, :, :]), [R1], [FS])
                        for g in range(7):
                            n_ = min(8, NKB - g * 8)
                            bk = PT.next()
                            for i in range(n_):
                                S.op("pe", lambda e, i=i: e.transpose(bk[0:24, i * 128:(i + 1) * 128],
                                                                      FS[:, g * 8 + i, :, :].rearrange("p s h -> p (s h)"), IDB()),
                                     [FS, CB], [bk], signal=(i == n_ - 1))
                            copy(evac_eng(), FKT[0:24, g * 1024:g * 1024 + n_ * 128], bk[0:24, 0:n_ * 128], [bk], [FKT])

                    S.ck("f%d_%d" % (l, p))
                    chunks = []
                    for qblk in range(NQB):
                        kbs = list(range(0, qblk + 1)) if qblk < 32 else list(range(32, 49))
                        groups = [kbs[i:i + 4] for i in range(0, len(kbs), 4)]
                        for e_ in range(2):
                            for gi in range(len(groups) - 1, -1, -1):
                                chunks.append(dict(q=qblk, e=e_, kbs=groups[gi], diag=(gi == len(groups) - 1),
                                                   first=(gi == len(groups) - 1), last=(gi == 0), own=kbs[-1]))
                    state = {}

                    def stage_z(ch):
                        z = ZB.next()
                        ch["z"] = z
                        P0 = 64 * ch["e"]
                        w = 128 * len(ch["kbs"])
                        ch["w"] = w
                        q0 = ch["q"] * 128
                        k0 = ch["kbs"][0] * 128
                        more = fox or ch["diag"]
                        QTe = QTA if ch["e"] == 0 else QTB
                        S.op("pe", lambda e: e.matmul(z[:, 0:w], QTe[:, q0:q0 + 128], KT[:, k0:k0 + w], start=True, stop=not more),
                             [QTe, KT], [z], signal=not more)
                        if fox:
                            h = 2 * pp + ch["e"]
                            S.op("pe", lambda e: e.matmul(z[:, 0:w], CB[:, 640 + h * 128:640 + (h + 1) * 128], FKT[:, k0:k0 + w], start=False, stop=not ch["diag"]),
                                 [CB, FKT], [z], signal=not ch["diag"])
                        if ch["diag"]:
                            S.op("pe", lambda e: e.matmul(z[:, w - 128:w], IDB(), NEGI() if fox else NEGS(), start=False, stop=True), [CB], [z])

                    def stage_e1(ch):
                        if fox:
                            return
                        z = ch["z"]
                        w = ch["w"]
                        eb = EB.next()
                        ch["eb"] = eb
                        S.op("act", lambda e: e.activation(out=eb[:, 0:w], in_=z[:, 0:w], func=AF.Exp), [z], [eb])

                    def stage_e1b(ch):
                        if fox:
                            return
                        w = ch["w"]
                        eb = ch["eb"]
                        sp = SPB.next()
                        ch["sp"] = sp
                        S.op("act", lambda e: e.activation(out=sp[:, 0:w], in_=eb[:, 0:w], func=AF.Ln, bias=ONEC[:, 0:1]), [eb, ONEC], [sp])

                    def stage_e2(ch):
                        z = ch["z"]
                        w = ch["w"]
                        a = AB.next()
                        ch["a"] = a
                        if fox:
                            h = 2 * pp + ch["e"]
                            S.op("act", lambda e: e.activation(out=a[:, 0:w], in_=z[:, 0:w], func=AF.Exp, bias=FM[:, ch["own"], h:h + 1]),
                                 [z, FM], [a])
                        else:
                            eb = ch["eb"]
                            sp = ch["sp"]
                            c = CBF.next()
                            if ch["first"]:
                                S.op("dve", lambda e: e.tensor_tensor_scan(out=c[:, 0:w][:, ::-1], data0=ONESF[:, 0:w], data1=sp[:, 0:w][:, ::-1],
                                                                           initial=0.0, op0=ALU.mult, op1=ALU.add), [sp, ONESF], [c])
                            else:
                                pc = state["prevc"]
                                S.op("dve", lambda e: e.tensor_tensor_scan(out=c[:, 0:w][:, ::-1], data0=ONESF[:, 0:w], data1=sp[:, 0:w][:, ::-1],
                                                                           initial=pc[:, 0:1], op0=ALU.mult, op1=ALU.add), [sp, ONESF, pc], [c])
                            state["prevc"] = c
                            S.op("dve", lambda e: e.tensor_tensor(out=eb[:, 0:w], in0=z[:, 0:w], in1=c[:, 0:w], op=ALU.subtract), [z, c], [eb])
                            S.op("act", lambda e: e.activation(out=a[:, 0:w], in_=eb[:, 0:w], func=AF.Exp), [eb], [a])

                    def stage_pv(ch):
                        a = ch["a"]
                        w = ch["w"]
                        nb = len(ch["kbs"])
                        bt = PT.next()
                        for i in range(nb):
                            S.op("pe", lambda e, i=i: e.transpose(bt[:, i * 128:(i + 1) * 128], a[:, i * 128:(i + 1) * 128], IDB()),
                                 [a, CB], [bt], signal=(i == nb - 1))
                        ch["bt"] = bt

                    def stage_ev(ch):
                        bt = ch["bt"]
                        w = ch["w"]
                        at = ATB.next()
                        ch["at"] = at
                        copy("dve" if fox else "act", at[:, 0:w], bt[:, 0:w], [bt], [at])

                    def stage_pvm(ch):
                        at = ch["at"]
                        w = ch["w"]
                        nb = len(ch["kbs"])
                        if ch["first"]:
                            state["o"] = PO.next()
                        o = state["o"]
                        for i in range(nb):
                            kb_ = ch["kbs"][i]
                            lastmm = ch["last"] and i == nb - 1
                            S.op("pe", lambda e, i=i, kb_=kb_: e.matmul(o[:, 0:66], at[:, i * 128:(i + 1) * 128], V[:, kb_, ch["e"], :],
                                                                        start=(ch["first"] and i == 0), stop=lastmm),
                                 [at, V], [o], signal=(i == nb - 1))
                        if ch["last"]:
                            if ch["e"] == 0:
                                state["ost"] = OST.next()
                            ost = state["ost"]
                            e_ = ch["e"]
                            if fox:
                                rc = SMALL.next()
                                S.op("dve", lambda e: e.reciprocal(out=rc[:, 0:1], in_=o[:, 64:65]), [o], [rc])
                                S.op("dve", lambda e: e.tensor_scalar(out=ost[:, e_ * 64:(e_ + 1) * 64], in0=o[:, 0:64], scalar1=rc[:, 0:1], scalar2=None,
                                                                      op0=ALU.mult), [o, rc], [ost])
                            else:
                                copy("act", ost[:, e_ * 64:(e_ + 1) * 64], o[:, 0:64], [o], [ost])
                            if e_ == 1:
                                col0 = (0 if fox else 512) + pp * 128
                                qb_ = ch["q"]
                                S.dma("sp", OT[qb_ * 128:(qb_ + 1) * 128, col0:col0 + 128], ost[:, :], [ost], [OTk[qb_]])

                    n = len(chunks)
                    stage_z(chunks[0])
                    stage_z(chunks[1])
                    stage_e1(chunks[0])
                    stage_e1b(chunks[0])
                    for i in range(n + 2):
                        if 1 <= i <= n:
                            stage_pv(chunks[i - 1])
                        if i + 2 < n:
                            stage_z(chunks[i + 2])
                        if i + 1 < n:
                            stage_e1(chunks[i + 1])
                        if 1 <= i <= n:
                            stage_ev(chunks[i - 1])
                        if i + 1 < n:
                            stage_e1b(chunks[i + 1])
                        if i < n:
                            stage_e2(chunks[i])
                        if 2 <= i <= n + 1:
                            stage_pvm(chunks[i - 2])
                    S.ck("att%d_%d" % (l, p))
                S.barrier()

            with ExitStack() as st:
                XTL = sb(st, "XTL", [128, 8, 512], F32)
                ACTA = sb(st, "ACTA", [128, 8, 512], BF16)
                ACTB = sb(st, "ACTB", [128, 8, 512], BF16)
                QM = sb(st, "QM", [128, 8, 512], F32)
                SQD = sb(st, "SQD", [128, 8, 512], BF16)
                RS = Rot([sb(st, "rs%d" % i, [128, 512], F32) for i in range(2)])
                PTB = sb(st, "PTB", [128, 2, 512], BF16)
                HID = sb(st, "HID", [128, 16, 512], BF16)
                RL = Rot([sb(st, "rl%d" % i, [128, 512], F32) for i in range(2)])
                WBLK = Rot([sb(st, "wblk%d" % i, [128, 8, 512], BF16) for i in range(3)])
                MKT = [sb(st, "MKT%d" % i, [128, 8, 256], BF16) for i in range(2)]
                MV = [sb(st, "MV%d" % i, [128, 2, 1024], BF16) for i in range(2)]
                MST = Rot([sb(st, "mst%d" % i, [128, 1024], F32) for i in range(2)])
                MB = Rot([sb(st, "mb%d" % i, [128, 1024], BF16) for i in range(2)])
                MT = sb(st, "MT", [128, 8, 256], BF16)

                def load_wblk(name, k0, n0):
                    wb = WBLK.next()
                    S.dma("sp", wb[:, :, :], WB[name][l].rearrange("(c p) n -> p c n", p=128)[:, k0:k0 + 8, n0:n0 + 512], WBk[name][l], [wb])
                    return wb

                for mb_ in range(2):
                    mt = XTOK.next()
                    S.dma("sp", mt[:, :], mem[mb_ * 128:(mb_ + 1) * 128, :], [], [mt])
                    ss = SMALL.next()
                    S.op("act", lambda e: e.activation(out=JUNK[:, :], in_=mt[:, :], func=AF.Square, accum_out=ss[:, 0:1]), [mt], [JUNK, ss])
                    rs = rstd_from_ss(ss, 0, 1, 1024.0)
                    hb = HB.next()
                    S.op("dve", lambda e: e.scalar_tensor_tensor(out=hb[:, :], in0=mt[:, :], scalar=rs[:, 0:1], in1=GB[:, 1024:2048],
                                                                 op0=ALU.mult, op1=ALU.mult), [mt, rs, GB], [hb])
                    bank = PT.next()
                    for c in range(8):
                        S.op("pe", lambda e, c=c: e.transpose(bank[:, c * 128:(c + 1) * 128], hb[:, c * 128:(c + 1) * 128], IDB()),
                             [hb, CB], [bank], signal=(c == 7))
                    copy(evac_eng(), MT[:, :, mb_ * 128:(mb_ + 1) * 128], bank[:, 0:1024].rearrange("p (c t) -> p c t", c=8), [bank], [MT])
                for which in ("w_mk", "w_mv"):
                    for mb_ in range(2):
                        stg = MST.next()
                        for ng in range(2):
                            wb = load_wblk(which, 0, ng * 512)
                            bank = P4.next()
                            for c in range(8):
                                S.op("pe", lambda e: e.matmul(bank[:, :], MT[:, c, mb_ * 128:(mb_ + 1) * 128], wb[:, c, :], start=(c == 0), stop=(c == 7)),
                                     [MT, wb], [bank], signal=(c == 7))
                            if which == "w_mk":
                                ss = SMALL.next()
                                for hh in range(2):
                                    S.op("act", lambda e: e.activation(out=JUNK[:, 0:256], in_=bank[:, hh * 256:(hh + 1) * 256], func=AF.Square,
                                                                       accum_out=ss[:, hh:hh + 1]), [bank], [JUNK, ss])
                                rs = rstd_from_ss(ss, 0, 2, 256.0)
                                for hh in range(2):
                                    S.op("dve", lambda e: e.scalar_tensor_tensor(
                                        out=stg[:, ng * 512 + hh * 256:ng * 512 + (hh + 1) * 256], in0=bank[:, hh * 256:(hh + 1) * 256],
                                        scalar=rs[:, hh:hh + 1], in1=GB[:, 3200:3456], op0=ALU.mult, op1=ALU.mult), [bank, rs, GB], [stg])
                            else:
                                copy("act", stg[:, ng * 512:(ng + 1) * 512], bank[:, :], [bank], [stg])
                        dst = p_mk if which == "w_mk" else p_mv
                        S.dma("sp", dst[l, mb_ * 128:(mb_ + 1) * 128, :], stg[:, :], [stg], [])
                        if which == "w_mk":
                            mbb = MB.next()
                            copy("pool", mbb[:, :], stg[:, :], [stg], [mbb])
                            bank2 = PT.next()
                            for c in range(8):
                                S.op("pe", lambda e: e.transpose(bank2[:, c * 128:(c + 1) * 128], mbb[:, c * 128:(c + 1) * 128], IDB()),
                                     [mbb, CB], [bank2], signal=(c == 7))
                            copy(evac_eng(), MKT[0][:, :, mb_ * 128:(mb_ + 1) * 128], bank2[:, 0:1024].rearrange("p (c t) -> p c t", c=8),
                                 [bank2], [MKT[0]])
                        else:
                            copy("pool", MV[0][:, mb_, :], stg[:, :], [stg], [MV[0]])
                for mb_ in range(2):
                    mbb = MB.next()
                    m32 = XTOK.next()
                    S.dma("sp", m32[:, :], cmk[l, mb_ * 128:(mb_ + 1) * 128, :], [], [m32])
                    copy("pool", mbb[:, :], m32[:, :], [m32], [mbb])
                    bank2 = PT.next()
                    for c in range(8):
                        S.op("pe", lambda e, c=c: e.transpose(bank2[:, c * 128:(c + 1) * 128], mbb[:, c * 128:(c + 1) * 128], IDB()),
                             [mbb, CB], [bank2], signal=(c == 7))
                    copy(evac_eng(), MKT[1][:, :, mb_ * 128:(mb_ + 1) * 128], bank2[:, 0:1024].rearrange("p (c t) -> p c t", c=8), [bank2], [MKT[1]])
                    v32 = XTOK.next()
                    S.dma("sp", v32[:, :], cmv[l, mb_ * 128:(mb_ + 1) * 128, :], [], [v32])
                    copy("pool", MV[1][:, mb_, :], v32[:, :], [v32], [MV[1]])

                S.ck("memkv%d" % l)

                def fm_rmsnorm(W_, gcol0, out_buf):
                    for c in range(8):
                        S.op("act", lambda e, c=c: e.activation(out=SQD[:, c, 0:W_], in_=XTL[:, c, 0:W_], func=AF.Square), [XTL], [SQD])
                    bank = PP.next()
                    for c in range(8):
                        S.op("pe", lambda e, c=c: e.matmul(bank[:, 0:W_], ONESB[:, :], SQD[:, c, 0:W_], start=(c == 0), stop=(c == 7)),
                             [ONESB, SQD], [bank], signal=(c == 7))
                    t1 = RS.next()
                    S.op("act", lambda e: e.activation(out=t1[:, 0:W_], in_=bank[:, 0:W_], func=AF.Ln, scale=1.0 / 1024.0, bias=EPSC[:, 0:1]),
                         [bank, EPSC], [t1])
                    r = RS.next()
                    S.op("act", lambda e: e.activation(out=r[:, 0:W_], in_=t1[:, 0:W_], func=AF.Exp, scale=-0.5), [t1], [r])
                    for c in range(8):
                        S.op("dve", lambda e, c=c: e.scalar_tensor_tensor(out=out_buf[:, c, 0:W_], in0=XTL[:, c, 0:W_], scalar=GC[:, gcol0 + c:gcol0 + c + 1],
                                                                          in1=r[:, 0:W_], op0=ALU.mult, op1=ALU.mult), [XTL, GC, r], [out_buf])

                def dense(name, in_buf, kc0, nk8, n0, nm, W_, out_fn, wk0=0):
                    for mg in range(0, nm, 4):
                        if nk8 == 1:
                            wb = load_wblk(name, wk0, n0 + mg * 128)
                            for m in range(4):
                                bank = P6.next()
                                for c in range(8):
                                    S.op("pe", lambda e: e.matmul(bank[:, 0:W_], wb[:, c, m * 128:(m + 1) * 128], in_buf[:, kc0 + c, 0:W_],
                                                                  start=(c == 0), stop=(c == 7)), [wb, in_buf], [bank], signal=(c == 7))
                                out_fn(mg + m, bank)
                            continue
                        banks = [P6.next() for _ in range(4)]
                        for kg in range(nk8):
                            wb = load_wblk(name, wk0 + kg * 8, n0 + mg * 128)
                            for m in range(4):
                                for c in range(8):
                                    first = (kg == 0 and c == 0)
                                    lastk = (kg == nk8 - 1 and c == 7)
                                    S.op("pe", lambda e: e.matmul(
                                        banks[m][:, 0:W_], wb[:, c, m * 128:(m + 1) * 128], in_buf[:, kc0 + kg * 8 + c, 0:W_], start=first, stop=lastk),
                                         [wb, in_buf], [banks[m]], signal=(c == 7))
                        for m in range(4):
                            out_fn(mg + m, banks[m])

                def add_to_x(W_):
                    def f(m, bank):
                        S.op("dve", lambda e: e.tensor_tensor(out=XTL[:, m, 0:W_], in0=bank[:, 0:W_], in1=XTL[:, m, 0:W_], op=ALU.add), [bank, XTL], [XTL])
                    return f

                for ti in range(9):
                    W_ = 512 if ti < 8 else 128
                    c0 = ti * 512
                    si = 0 if ti < 8 else 1
                    nblk = W_ // 128
                    for tb in range(nblk):
                        blk = ti * 4 + tb
                        ot = XTOK.next()
                        S.dma("sp", ot[:, :], OT[blk * 128:(blk + 1) * 128, :], [OTk[blk]], [ot])
                        ss = SMALL.next()
                        for hf in range(2):
                            S.op("act", lambda e, hf=hf: e.activation(out=JUNK[:, 0:512], in_=ot[:, hf * 512:(hf + 1) * 512], func=AF.Square,
                                                                       accum_out=ss[:, hf:hf + 1]), [ot], [JUNK, ss])
                        rs = rstd_from_ss(ss, 0, 2, 512.0)
                        hb = HB.next()
                        for hf in range(2):
                            S.op("dve", lambda e, hf=hf: e.scalar_tensor_tensor(out=hb[:, hf * 512:(hf + 1) * 512], in0=ot[:, hf * 512:(hf + 1) * 512],
                                                                                scalar=rs[:, hf:hf + 1], in1=GB[:, 2048 + hf * 512:2048 + (hf + 1) * 512],
                                                                                op0=ALU.mult, op1=ALU.mult), [ot, rs, GB], [hb])
                        bank = PT.next()
                        for c in range(8):
                            S.op("pe", lambda e, c=c: e.transpose(bank[:, c * 128:(c + 1) * 128], hb[:, c * 128:(c + 1) * 128], IDB()),
                                 [hb, CB], [bank], signal=(c == 7))
                        copy(evac_eng(), ACTA[:, :, tb * 128:(tb + 1) * 128], bank[:, 0:1024].rearrange("p (c t) -> p c t", c=8), [bank], [ACTA])
                    S.ck("c1_%d_%d" % (l, ti))
                    S.dma("sp", XTL[:, :, 0:W_], XT[:, :, c0:c0 + W_], XTk[ti * 4:ti * 4 + nblk], [XTL])
                    dense("w_out", ACTA, 0, 1, 0, 8, W_, add_to_x(W_))
                    S.ck("wout_%d_%d" % (l, ti))
                    fm_rmsnorm(W_, 8, ACTA)

                    def q_out(m, bank):
                        S.op("act", lambda e: e.activation(out=QM[:, m, 0:W_], in_=bank[:, 0:W_], func=AF.Copy), [bank], [QM])
                        S.op("act", lambda e: e.activation(out=SQD[:, m, 0:W_], in_=bank[:, 0:W_], func=AF.Square), [bank], [SQD])
                    dense("w_mq", ACTA, 0, 1, 0, 8, W_, q_out)
                    for hh in range(4):
                        bank = PP.next()
                        for c in range(2):
                            S.op("pe", lambda e, c=c: e.matmul(bank[:, 0:W_], ONESB[:, :], SQD[:, 2 * hh + c, 0:W_], start=(c == 0), stop=(c == 1)),
                                 [ONESB, SQD], [bank], signal=(c == 1))
                        t1 = RS.next()
                        S.op("act", lambda e: e.activation(out=t1[:, 0:W_], in_=bank[:, 0:W_], func=AF.Ln, scale=1.0 / 256.0, bias=EPSC[:, 0:1]),
                             [bank, EPSC], [t1])
                        r = RS.next()
                        S.op("act", lambda e: e.activation(out=r[:, 0:W_], in_=t1[:, 0:W_], func=AF.Exp, scale=-0.5), [t1], [r])
                        for c in range(2):
                            S.op("dve", lambda e, c=c: e.scalar_tensor_tensor(out=ACTB[:, 2 * hh + c, 0:W_], in0=QM[:, 2 * hh + c, 0:W_],
                                                                              scalar=GC[:, 24 + c:25 + c], in1=r[:, 0:W_], op0=ALU.mult, op1=ALU.mult),
                                 [QM, GC, r], [ACTB])
                    for hh in range(4):
                        for mc in range(2):
                            bank = P4.next()
                            for c in range(2):
                                S.op("pe", lambda e, c=c: e.matmul(bank[:, 0:W_], MKT[si][:, 2 * hh + c, mc * 128:(mc + 1) * 128], ACTB[:, 2 * hh + c, 0:W_],
                                                                   start=(c == 0), stop=(c == 1)), [MKT[si], ACTB], [bank], signal=(c == 1))
                            S.op("act", lambda e: e.activation(out=PTB[:, mc, 0:W_], in_=bank[:, 0:W_], func=AF.Exp), [bank], [PTB])
                        bank = PP.next()
                        for mc in range(2):
                            S.op("pe", lambda e, mc=mc: e.matmul(bank[:, 0:W_], ONESB[:, :], PTB[:, mc, 0:W_], start=(mc == 0), stop=(mc == 1)),
                                 [ONESB, PTB], [bank], signal=(mc == 1))
                        t1 = RS.next()
                        S.op("act", lambda e: e.activation(out=t1[:, 0:W_], in_=bank[:, 0:W_], func=AF.Ln), [bank], [t1])
                        rd = RS.next()
                        S.op("act", lambda e: e.activation(out=rd[:, 0:W_], in_=t1[:, 0:W_], func=AF.Exp, scale=-1.0), [t1], [rd])
                        for dc in range(2):
                            bank = P4.next()
                            for mc in range(2):
                                S.op("pe", lambda e, mc=mc: e.matmul(bank[:, 0:W_], MV[si][:, mc, (2 * hh + dc) * 128:(2 * hh + dc + 1) * 128], PTB[:, mc, 0:W_],
                                                                     start=(mc == 0), stop=(mc == 1)), [MV[si], PTB], [bank], signal=(mc == 1))
                            S.op("dve", lambda e: e.tensor_tensor(out=ACTA[:, 2 * hh + dc, 0:W_], in0=bank[:, 0:W_], in1=rd[:, 0:W_], op=ALU.mult),
                                 [bank, rd], [ACTA])
                    dense("w_mo", ACTA, 0, 1, 0, 8, W_, add_to_x(W_))
                    S.ck("cross_%d_%d" % (l, ti))
                    fm_rmsnorm(W_, 16, ACTB)
                    for half in range(2):
                        def h_out(m, bank):
                            rl = RL.next()
                            S.op("act", lambda e: e.activation(out=rl[:, 0:W_], in_=bank[:, 0:W_], func=AF.Relu), [bank], [rl])
                            S.op("pool", lambda e: e.tensor_tensor(out=HID[:, m, 0:W_], in0=rl[:, 0:W_], in1=rl[:, 0:W_], op=ALU.mult), [rl], [HID])
                        dense("w_ff1", ACTB, 0, 1, half * 2048, 16, W_, h_out)
                        dense("w_ff2", HID, 0, 2, 0, 8, W_, add_to_x(W_), wk0=half * 16)
                    S.ck("ffn_%d_%d" % (l, ti))
                    if l == 0:
                        S.dma("sp", XT[:, :, c0:c0 + W_], XTL[:, :, 0:W_], [XTL], XTk[ti * 4:ti * 4 + nblk])
                        fm_rmsnorm(W_, 0, ACTA)
                        S.dma("sp", HTd[:, :, c0:c0 + W_], ACTA[:, :, 0:W_], [ACTA], HTk[ti * 4:ti * 4 + nblk])
                    else:
                        for tb in range(nblk):
                            blk = ti * 4 + tb
                            yt = XTOK.next()
                            for half in range(2):
                                bank = P4.next()
                                for c in range(4):
                                    cc = half * 4 + c
                                    S.op("pe", lambda e, c=c, cc=cc: e.transpose(bank[:, c * 128:(c + 1) * 128], XTL[:, cc, tb * 128:(tb + 1) * 128], IDF()),
                                         [XTL, CF], [bank], signal=(c == 3))
                                copy(evac_eng(), yt[:, half * 512:(half + 1) * 512], bank[:, 0:512], [bank], [yt])
                            if blk < 32:
                                S.dma("sp", yp[blk * 128:(blk + 1) * 128, :], yt[:, :], [yt], [])
                            else:
                                S.dma("sp", ys[:, :], yt[0:16, :], [yt], [])
                    S.ck("tile_%d_%d" % (l, ti))
                S.barrier()

        S.dead = False
        S.barrier(["sp"])
        print("instructions emitted:", S.ninst, "sp dmas:", sum(v for k, v in S.val.items() if k.startswith("D:sp")) // 16,
              {k: v for k, v in S.val.items() if k.startswith("E:")})
    return nc


_NC_CACHE = {}


def _consts():
    c = np.zeros((128, NCST), np.float32)
    idx = np.arange(128)
    c[:, 0:128] = np.eye(128, dtype=np.float32)
    c[:, 128:256] = (idx[:, None] <= idx[None, :]).astype(np.float32)
    c[127, 256:384] = 1.0
    c[:, 384:512] = np.where(idx[None, :] > idx[:, None], NEGM, 0.0)
    c[:, 512:640] = np.where(idx[None, :] >= idx[:, None], NEGM, 0.0)
    for r in range(24):
        h = r % 8
        c[r, 640 + h * 128:640 + (h + 1) * 128] = 1.0
    return c


def kernel(x_prompt, x_sample, mem_prompt, cache_fox_k, cache_fox_v, cache_fox_logf, cache_sb_k, cache_sb_v,
           cache_mem_k, cache_mem_v, g_mix, w_in, b_forget, g_fox_q, g_fox_k, g_out_fox, g_out_sb, w_out,
           g_cross, g_mem, w_mq, w_mk, w_mv, g_mq, g_mk, w_mo, g_ffn, w_ff1, w_ff2):
    f = lambda a: np.ascontiguousarray(np.asarray(a, dtype=np.float32))
    if "nc" not in _NC_CACHE:
        _NC_CACHE["nc"] = build()
    nc = _NC_CACHE["nc"]
    gbp = np.zeros((2, 128, NGB), np.float32)
    gcp = np.zeros((2, 128, NGC), np.float32)
    for l in range(2):
        row = np.concatenate([f(g_mix)[l], f(g_mem)[l], f(g_out_fox)[l], f(g_out_sb)[l], f(g_fox_q)[l], f(g_fox_k)[l],
                              f(g_mk)[l], f(b_forget)[l]])
        gbp[l] = np.broadcast_to(row[None, :], (128, NGB))
        gcp[l, :, 0:8] = f(g_mix)[min(l + 1, 1)].reshape(8, 128).T
        gcp[l, :, 8:16] = f(g_cross)[l].reshape(8, 128).T
        gcp[l, :, 16:24] = f(g_ffn)[l].reshape(8, 128).T
        gcp[l, :, 24:26] = f(g_mq)[l].reshape(2, 128).T
    cst = _consts()
    shared = {"w_in": f(w_in), "w_out": f(w_out), "w_mq": f(w_mq), "w_mk": f(w_mk), "w_mv": f(w_mv), "w_mo": f(w_mo),
              "w_ff1": f(w_ff1), "w_ff2": f(w_ff2), "gb": gbp, "gc": gcp, "cst": cst}
    in_maps = []
    for b in range(8):
        xs_pad = np.zeros((TS, D), np.float32)
        xs_pad[0:16] = f(x_sample)[b]
        m = dict(shared)
        m.update({
            "xp": f(x_prompt)[b], "xs": xs_pad, "mem": f(mem_prompt)[b],
            "cfk": f(cache_fox_k)[:, b].reshape(2, 2048, 512), "cfv": f(cache_fox_v)[:, b].reshape(2, 2048, 512),
            "clf": f(cache_fox_logf)[:, b], "csk": f(cache_sb_k)[:, b].reshape(2, 2048, 512),
            "csv": f(cache_sb_v)[:, b].reshape(2, 2048, 512),
            "cmk": f(cache_mem_k)[:, b].reshape(2, 256, 1024), "cmv": f(cache_mem_v)[:, b].reshape(2, 256, 1024),
        })
        m = {k: np.ascontiguousarray(v) for k, v in m.items()}
        in_maps.append(m)
    res = run_bass_kernel_spmd(nc, in_maps, core_ids=list(range(8)))
    R = res.results

    def g(name):
        return np.stack([np.asarray(R[b][name]) for b in range(8)], axis=0)

    def kv(name, t):
        return np.ascontiguousarray(np.transpose(g(name), (1, 0, 2, 3)).reshape(2, 8, t, 8, 64))

    y_p = g("yp")
    y_s = g("ys")
    outs = (y_p, y_s,
            kv("p_fk", T), kv("p_fv", T), np.ascontiguousarray(np.transpose(g("p_lf"), (1, 0, 2, 3))),
            kv("p_sk", T), kv("p_sv", T),
            np.ascontiguousarray(np.transpose(g("p_mk"), (1, 0, 2, 3)).reshape(2, 8, 256, 4, 256)),
            np.ascontiguousarray(np.transpose(g("p_mv"), (1, 0, 2, 3)).reshape(2, 8, 256, 4, 256)),
            kv("s_fk", 16), kv("s_fv", 16), np.ascontiguousarray(np.transpose(g("s_lf"), (1, 0, 2, 3))),
            kv("s_sk", 16), kv("s_sv", 16))
    return tuple(np.asarray(o, dtype=np.float32) for o in outs)
```

```python
import os
import numpy as np
from contextlib import ExitStack
import concourse.bass as bass
import concourse.mybir as mybir
from concourse.bass_utils import run_bass_kernel_spmd

F32 = mybir.dt.float32
BF16 = mybir.dt.bfloat16
AF = mybir.ActivationFunctionType
ALU = mybir.AluOpType
AX = mybir.AxisListType

T = 4096
TS = 128
TT = T + TS
D = 1024
NQB = 33
NKB = 49
EPS = 1e-6
NEGM = -30000.0
NGB = 3464
NGC = 26
NCST = 1664
NCH = 10
SAME_ENGINE_SYNC = os.environ.get("KSES", "1") == "1"
SERIAL = os.environ.get("KSER", "0")
PSUM_EXCL = os.environ.get("KPX", "1") == "1"


class Tk:
    __slots__ = ("w", "r")

    def __init__(self):
        self.w = None
        self.r = {}


class Buf:
    def __init__(self, t):
        self.t = t
        self.k = Tk()

    def __getitem__(self, key):
        return self.t[key]


class Rot:
    def __init__(self, bufs):
        self.bufs = bufs
        self.i = 0

    def next(self):
        b = self.bufs[self.i]
        self.i = (self.i + 1) % len(self.bufs)
        return b


class Sched:
    def __init__(self, nc, stack):
        self.nc = nc
        self.eng = {"pe": nc.tensor, "act": nc.scalar, "dve": nc.vector, "pool": nc.gpsimd, "sp": nc.sync}
        self.sem = {}
        self.val = {}
        for e in self.eng:
            self.sem["E:" + e] = stack.enter_context(nc.semaphore("sem_" + e))
            self.val["E:" + e] = 0
        for q in ("sp", "pool"):
            for c in range(NCH):
                k = "D:%s:%d" % (q, c)
                self.sem[k] = stack.enter_context(nc.semaphore("dsem_%s_%d" % (q, c)))
                self.val[k] = 0
        self.known = {e: {} for e in self.eng}
        self.rr = {"sp": 0, "pool": 0}
        self.ninst = 0
        self.dead = False
        self.stop = os.environ.get("KSTOP", "")
        self.serial = SERIAL
        self.last_tok = None
        self.last_ew = None

    def ck(self, name):
        if self.stop and name == self.stop:
            self.dead = True

    def _deps(self, eng, reads, writes):
        need = {}

        def add(k, v):
            if need.get(k, 0) < v:
                need[k] = v

        for b in reads:
            if b.k.w is not None:
                add(*b.k.w)
        for b in writes:
            if b.k.w is not None:
                add(*b.k.w)
            for k, v in b.k.r.items():
                add(k, v)
        if self.serial == "1" and self.last_tok is not None:
            add(*self.last_tok)
        if self.serial == "2" and eng in ("act", "dve", "pool") and self.last_ew is not None:
            add(*self.last_ew)
        waits = []
        kn = self.known[eng]
        for k, v in need.items():
            if k == "E:" + eng and (eng == "pe" or not SAME_ENGINE_SYNC):
                continue
            if kn.get(k, 0) >= v:
                continue
            kn[k] = v
            waits.append((k, v))
        return waits

    def _mark(self, tok, reads, writes):
        k, v = tok
        for b in reads:
            if b.k.r.get(k, 0) < v:
                b.k.r[k] = v
        for b in writes:
            b.k.w = tok
            b.k.r = {}

    def op(self, eng, fn, reads=(), writes=(), signal=True):
        assert signal or eng == "pe"
        if self.dead:
            return
        if PSUM_EXCL:
            px = [b for b in reads if getattr(b, "px", False)]
            if px:
                writes = list(writes) + px
        waits = self._deps(eng, reads, writes)
        e = self.eng[eng]
        for k, v in waits:
            e.wait_ge(self.sem[k], v)
        inst = fn(e)
        key = "E:" + eng
        if signal:
            self.val[key] += 1
            inst.then_inc(self.sem[key], 1)
            tok = (key, self.val[key])
        else:
            tok = (key, self.val[key] + 1)
        self._mark(tok, reads, writes)
        self.last_tok = tok
        if eng in ("act", "dve", "pool"):
            self.last_ew = tok
        self.ninst += 1 + len(waits)

    def dma(self, q, out, in_, reads=(), writes=()):
        if self.dead:
            return
        c = self.rr[q]
        self.rr[q] = (c + 1) % NCH
        key = "D:%s:%d" % (q, c)
        waits = self._deps(q, reads, writes)
        prev = self.val[key]
        if prev > 0 and self.known[q].get(key, 0) < prev:
            waits.append((key, prev))
            self.known[q][key] = prev
        e = self.eng[q]
        for k, v in waits:
            e.wait_ge(self.sem[k], v)
        e.dma_start(out=out, in_=in_).then_inc(self.sem[key], 16)
        self.val[key] = prev + 16
        self._mark((key, prev + 16), reads, writes)
        self.last_tok = (key, prev + 16)
        self.ninst += 1 + len(waits)

    def barrier(self, engines=None):
        for e in (engines or list(self.eng)):
            kn = self.known[e]
            for k, v in self.val.items():
                if v > 0 and kn.get(k, 0) < v:
                    self.eng[e].wait_ge(self.sem[k], v)
                    kn[k] = v
                    self.ninst += 1


def build():
    nc = bass.Bass("TRN2", target_bir_lowering=False)

    def din(name, shape):
        return nc.dram_tensor(name, shape, F32, kind="ExternalInput").ap()

    def dout(name, shape):
        return nc.dram_tensor(name, shape, F32, kind="ExternalOutput").ap()

    def dint(name, shape, dt):
        return nc.dram_tensor(name, shape, dt, kind="Internal").ap()

    xp = din("xp", [T, D])
    xs = din("xs", [TS, D])
    mem = din("mem", [256, D])
    cfk = din("cfk", [2, 2048, 512])
    cfv = din("cfv", [2, 2048, 512])
    clf = din("clf", [2, 2048, 8])
    csk = din("csk", [2, 2048, 512])
    csv = din("csv", [2, 2048, 512])
    cmk = din("cmk", [2, 256, 1024])
    cmv = din("cmv", [2, 256, 1024])
    wshapes = {"w_in": [1024, 3080], "w_out": [1024, 1024], "w_mq": [1024, 1024], "w_mk": [1024, 1024],
               "w_mv": [1024, 1024], "w_mo": [1024, 1024], "w_ff1": [1024, 4096], "w_ff2": [4096, 1024]}
    W32 = {n: din(n, [2] + s) for n, s in wshapes.items()}
    gb = din("gb", [2, 128, NGB])
    gc = din("gc", [2, 128, NGC])
    cst = din("cst", [128, NCST])

    yp = dout("yp", [T, D])
    ys = dout("ys", [16, D])
    p_fk = dout("p_fk", [2, T, 512])
    p_fv = dout("p_fv", [2, T, 512])
    p_lf = dout("p_lf", [2, T, 8])
    p_sk = dout("p_sk", [2, T, 512])
    p_sv = dout("p_sv", [2, T, 512])
    p_mk = dout("p_mk", [2, 256, 1024])
    p_mv = dout("p_mv", [2, 256, 1024])
    s_fk = dout("s_fk", [2, 16, 512])
    s_fv = dout("s_fv", [2, 16, 512])
    s_lf = dout("s_lf", [2, 16, 8])
    s_sk = dout("s_sk", [2, 16, 512])
    s_sv = dout("s_sv", [2, 16, 512])

    XT = dint("XT", [128, 8, TT], F32)
    HTd = dint("HTd", [128, 8, TT], BF16)
    OT = dint("OT", [TT, 1024], F32)
    WB = {n: dint("b_" + n, [2] + s, BF16) for n, s in wshapes.items()}

    with ExitStack() as stack:
        S = Sched(nc, stack)

        uniq = [0]

        def sb(st, name, shape, dt):
            uniq[0] += 1
            nm = "%s_%d" % (name, uniq[0])
            b = Buf(st.enter_context(nc.sbuf_tensor(nm, shape, dt)))
            if os.environ.get("KADDR"):
                ml = nc.lookup_mloc(nm)
                print("ADDR", nm, ml.addr, ml.addr + int(np.prod(shape[1:])) * (4 if dt == F32 else 2))
            return b

        def ps(name, shape, dt):
            b = Buf(stack.enter_context(nc.psum_tensor(name, shape, dt)))
            b.px = True
            return b

        XTk = [Buf(None) for _ in range(NQB)]
        HTk = [Buf(None) for _ in range(NQB)]
        OTk = [Buf(None) for _ in range(NQB)]
        WBk = {n: [[] for _ in range(2)] for n in wshapes}

        CF = sb(stack, "CF", [128, 640], F32)
        CB = sb(stack, "CB", [128, NCST], BF16)
        ONESB = sb(stack, "ONESB", [128, 128], BF16)
        ONESF = sb(stack, "ONESF", [128, 512], F32)
        EPSC = sb(stack, "EPSC", [128, 1], F32)
        ONEC = sb(stack, "ONEC", [128, 1], F32)
        GB = sb(stack, "GB", [128, NGB], F32)
        GC = sb(stack, "GC", [128, NGC], F32)
        JUNK = sb(stack, "JUNK", [128, 1024], F32)
        XTOK = Rot([sb(stack, "xtok%d" % i, [128, 1024], F32) for i in range(2)])
        HB = Rot([sb(stack, "hb%d" % i, [128, 1024], BF16) for i in range(2)])
        XTT = Rot([sb(stack, "xtt%d" % i, [128, 8, 128], F32) for i in range(2)])
        HTT = Rot([sb(stack, "htt%d" % i, [128, 8, 128], BF16) for i in range(2)])
        SMALL = Rot([sb(stack, "small%d" % i, [128, 16], F32) for i in range(8)])

        PZ = Rot([ps("pz%d" % i, [128, 512], F32) for i in range(2)])
        PO = Rot([ps("po%d" % i, [128, 512], F32) for i in range(2)])
        PP = Rot([ps("pp%d" % i, [128, 512], F32) for i in range(2)])
        PT = Rot([ps("pt%d" % i, [128, 1024], BF16) for i in range(2)])
        P4 = Rot(PZ.bufs + PO.bufs)
        ZB = Rot(PZ.bufs + PP.bufs)
        P6 = Rot(PZ.bufs + PO.bufs + PP.bufs)

        IDF = lambda: CF[:, 0:128]
        TRIU = lambda: CF[:, 128:256]
        SEL127 = lambda: CF[:, 256:384]
        IDB = lambda: CB[:, 0:128]
        NEGI = lambda: CB[:, 384:512]
        NEGS = lambda: CB[:, 512:640]
        SELT = lambda h: CB[0:24, 640 + h * 128:640 + (h + 1) * 128]

        tog = [0]

        def evac_eng():
            tog[0] ^= 1
            return "act" if tog[0] else "dve"

        def copy(eng, out, in_, reads, writes):
            if eng == "act":
                S.op("act", lambda e: e.activation(out=out, in_=in_, func=AF.Copy), reads, writes)
            elif eng == "dve":
                S.op("dve", lambda e: e.tensor_copy(out=out, in_=in_), reads, writes)
            else:
                S.op("pool", lambda e: e.tensor_copy(out=out, in_=in_), reads, writes)

        def rstd_from_ss(ss, g0, g1, n):
            ms = SMALL.next()
            S.op("dve", lambda e: e.tensor_scalar(out=ms[:, g0:g1], in0=ss[:, g0:g1], scalar1=1.0 / n, scalar2=EPS,
                                                  op0=ALU.mult, op1=ALU.add), [ss], [ms])
            ln = SMALL.next()
            S.op("act", lambda e: e.activation(out=ln[:, g0:g1], in_=ms[:, g0:g1], func=AF.Ln), [ms], [ln])
            rs = SMALL.next()
            S.op("act", lambda e: e.activation(out=rs[:, g0:g1], in_=ln[:, g0:g1], func=AF.Exp, scale=-0.5), [ln], [rs])
            return rs

        st0 = ExitStack()
        CST32 = sb(st0, "CST32", [128, NCST], F32)
        STG = Rot([sb(st0, "stg%d" % i, [128, 4096], F32) for i in range(3)])
        STB = Rot([sb(st0, "stb%d" % i, [128, 4096], BF16) for i in range(3)])
        S.dma("sp", CF[:, :], cst[:, 0:640], [], [CF])
        S.dma("sp", CST32[:, :], cst[:, :], [], [CST32])
        copy("pool", CB[:, :], CST32[:, :], [CST32], [CB])
        S.op("dve", lambda e: e.memset(ONESB[:, :], 1.0), [], [ONESB])
        S.op("dve", lambda e: e.memset(ONESF[:, :], 1.0), [], [ONESF])
        S.op("dve", lambda e: e.memset(EPSC[:, :], EPS), [], [EPSC])
        S.op("dve", lambda e: e.memset(ONEC[:, :], 1.0), [], [ONEC])

        def load_gains(l):
            S.dma("sp", GB[:, :], gb[l], [], [GB])
            S.dma("sp", GC[:, :], gc[l], [], [GC])
            S.op("dve", lambda e: e.tensor_scalar(out=GB[:, 3072:3136], in0=GB[:, 3072:3136], scalar1=0.125, scalar2=None,
                                                  op0=ALU.mult), [GB], [GB])
            S.op("dve", lambda e: e.tensor_scalar(out=GC[:, 24:26], in0=GC[:, 24:26], scalar1=1.0 / 16.0, scalar2=None,
                                                  op0=ALU.mult), [GC], [GC])

        conv_jobs = []
        for l_ in range(2):
            for n in ["w_in", "w_out", "w_mq", "w_mk", "w_mv", "w_mo", "w_ff1", "w_ff2"]:
                K_, N_ = wshapes[n]
                F_ = K_ * N_ // 128
                src = W32[n][l_].rearrange("k n -> (k n)").rearrange("(p f) -> p f", p=128)
                dst = WB[n][l_].rearrange("k n -> (k n)").rearrange("(p f) -> p f", p=128)
                for f0 in range(0, F_, 4096):
                    f1 = min(F_, f0 + 4096)
                    conv_jobs.append((n, l_, src[:, f0:f1], dst[:, f0:f1], f1 - f0))

        def convert_some(k):
            for _ in range(k):
                if not conv_jobs:
                    return
                n, l_, src, dst, w = conv_jobs.pop(0)
                a = STG.next()
                b = STB.next()
                S.dma("sp", a[:, 0:w], src, [], [a])
                copy("pool", b[:, 0:w], a[:, 0:w], [a], [b])
                kk = Buf(None)
                S.dma("sp", dst, b[:, 0:w], [b], [kk])
                WBk[n][l_].append(kk)

        S.ck("pre0")
        load_gains(0)
        S.ck("pre1")
        convert_some(7)
        S.ck("pre2")

        def to_feature_major_f32(src_buf, blk):
            xtt = XTT.next()
            for half in range(2):
                bank = P4.next()
                for c in range(4):
                    cc = half * 4 + c
                    S.op("pe", lambda e, c=c, cc=cc: e.transpose(bank[:, c * 128:(c + 1) * 128], src_buf[:, cc * 128:(cc + 1) * 128], IDF()),
                         [src_buf, CF], [bank], signal=(c == 3))
                copy(evac_eng(), xtt[:, half * 4:(half + 1) * 4, :], bank[:, 0:512].rearrange("p (c t) -> p c t", c=4), [bank], [xtt])
            S.dma("sp", XT[:, :, blk * 128:(blk + 1) * 128], xtt[:, :, :], [xtt], [XTk[blk]])

        def tok_to_hT(hb, blk, dst, dstk):
            bank = PT.next()
            for c in range(8):
                S.op("pe", lambda e, c=c: e.transpose(bank[:, c * 128:(c + 1) * 128], hb[:, c * 128:(c + 1) * 128], IDB()),
                     [hb, CB], [bank], signal=(c == 7))
            S.ck("p0b1")
            htt = HTT.next()
            copy(evac_eng(), htt[:, :, :], bank[:, 0:1024].rearrange("p (c t) -> p c t", c=8), [bank], [htt])
            S.ck("p0b2")
            S.dma("sp", dst[:, :, blk * 128:(blk + 1) * 128], htt[:, :, :], [htt], [dstk])

        for blk in range(NQB):
            xt = XTOK.next()
            src = xp[blk * 128:(blk + 1) * 128, :] if blk < 32 else xs[:, :]
            S.dma("sp", xt[:, :], src, [], [xt])
            SKIP = os.environ.get("KSKIP", "")
            if "a" not in SKIP:
                to_feature_major_f32(xt, blk)
            S.ck("p0a")
            if "b" in SKIP:
                continue
            ss = SMALL.next()
            S.op("act", lambda e: e.activation(out=JUNK[:, :], in_=xt[:, :], func=AF.Square, accum_out=ss[:, 0:1]), [xt], [JUNK, ss])
            rs = rstd_from_ss(ss, 0, 1, 1024.0)
            hb = HB.next()
            S.op("dve", lambda e: e.scalar_tensor_tensor(out=hb[:, :], in0=xt[:, :], scalar=rs[:, 0:1], in1=GB[:, 0:1024],
                                                         op0=ALU.mult, op1=ALU.mult), [xt, rs, GB], [hb])
            S.ck("p0b")
            tok_to_hT(hb, blk, HTd, HTk[blk])
            S.ck("p0c")
            S.ck("p0c_%d" % blk)
            convert_some(2)
        convert_some(1000)
        S.barrier()
        st0.close()


        if os.environ.get("KDBG"):
            dbb = HB.next()
            dbf = XTOK.next()
            for j, blk_ in enumerate((6, 5, 7, 12)):
                S.dma("sp", yp[j * 256:j * 256 + 128, :].rearrange("p (c t) -> p c t", c=8), XT[:, :, blk_ * 128:(blk_ + 1) * 128], [XTk[blk_]], [])
                S.dma("sp", dbb[:, :].rearrange("p (c t) -> p c t", c=8), HTd[:, :, blk_ * 128:(blk_ + 1) * 128], [HTk[blk_]], [dbb])
                S.op("dve", lambda e: e.tensor_copy(out=dbf[:, :], in_=dbb[:, :]), [dbb], [dbf])
                S.dma("sp", yp[j * 256 + 128:j * 256 + 256, :], dbf[:, :], [dbf], [])
        S.ck("p0")
        for l in range(2):
            if l == 1:
                load_gains(1)
            with ExitStack() as st:
                if os.environ.get("KPAD"):
                    PAD = sb(st, "PAD", [128, int(os.environ["KPAD"])], F32)
                WP = sb(st, "WP", [128, 8, 384], BF16)
                WF = sb(st, "WF", [128, 8, 8], BF16)
                HT = Rot([sb(st, "hT%d" % i, [128, 8, 512], BF16) for i in range(2)])
                QTA = sb(st, "QTA", [128, NQB * 128], BF16)
                QTB = sb(st, "QTB", [128, NQB * 128], BF16)
                KT = sb(st, "KT", [128, NKB * 128], BF16)
                V = sb(st, "V", [128, NKB, 2, 66], BF16)
                KBC = sb(st, "KBC", [128, 16, 128], BF16)
                K32 = sb(st, "K32", [128, 16, 128], F32)
                V32 = sb(st, "V32", [128, 16, 128], F32)
                SQ = Rot([sb(st, "sq%d" % i, [128, 256], F32) for i in range(2)])
                KST = Rot([sb(st, "kst%d" % i, [128, 128], F32) for i in range(3)])
                VST = Rot([sb(st, "vst%d" % i, [128, 128], F32) for i in range(3)])
                QB = Rot([sb(st, "qb%d" % i, [128, 128], BF16) for i in range(3)])
                KB = Rot([sb(st, "kb%d" % i, [128, 128], BF16) for i in range(3)])
                EB = Rot([sb(st, "eb%d" % i, [128, 512], F32) for i in range(3)])
                SPB = Rot([sb(st, "spb%d" % i, [128, 512], F32) for i in range(3)])
                CBF = Rot([sb(st, "cbf%d" % i, [128, 512], F32) for i in range(2)])
                AB = Rot([sb(st, "ab%d" % i, [128, 512], BF16) for i in range(3)])
                ATB = Rot([sb(st, "atb%d" % i, [128, 512], BF16) for i in range(3)])
                OST = Rot([sb(st, "ost%d" % i, [128, 128], F32) for i in range(2)])
                LF = sb(st, "LF", [128, NKB, 8], F32)
                WC = sb(st, "WC", [128, NKB, 8], F32)
                TB = sb(st, "TB", [128, NKB, 8], F32)
                CS = sb(st, "CS", [128, NKB, 8], F32)
                FM = sb(st, "FM", [128, NKB, 8], F32)
                R1 = sb(st, "R1", [128, NKB, 8], F32)
                FS = sb(st, "FS", [128, NKB, 3, 8], BF16)
                FKT = sb(st, "FKT", [128, NKB * 128], BF16)
                FT1 = sb(st, "FT1", [128, NQB * 8], F32)
                FT2 = sb(st, "FT2", [128, NQB * 8], F32)

                S.op("dve", lambda e: e.memset(V[:, :, :, 64:65], 1.0), [], [V])
                S.op("dve", lambda e: e.memset(V[:, :, :, 65:66], 0.0), [], [V])
                S.op("dve", lambda e: e.memset(LF[:, :, :], 0.0), [], [LF])
                S.op("pool", lambda e: e.memset(QTA[:, :], 0.0), [], [QTA])
                S.op("pool", lambda e: e.memset(QTB[:, :], 0.0), [], [QTB])
                S.op("pool", lambda e: e.memset(FKT[:, :], 0.0), [], [FKT])

                for p in range(8):
                    fox = p < 4
                    pp = p % 4
                    base = 0 if fox else 1544
                    for j in range(3):
                        c0 = base + j * 512 + pp * 128
                        S.dma("sp", WP[:, :, j * 128:(j + 1) * 128],
                              WB["w_in"][l].rearrange("(c p) n -> p c n", p=128)[:, :, c0:c0 + 128], WBk["w_in"][l], [WP])
                    if p == 0:
                        S.dma("sp", WF[:, :, :], WB["w_in"][l].rearrange("(c p) n -> p c n", p=128)[:, :, 1536:1544],
                              WBk["w_in"][l], [WF])
                    ck, cv = (cfk, cfv) if fox else (csk, csv)
                    S.dma("sp", K32[:, :, :], ck[l].rearrange("(b p) n -> p b n", p=128)[:, :, pp * 128:(pp + 1) * 128], [], [K32])
                    copy("pool", KBC[:, :, :], K32[:, :, :], [K32], [KBC])
                    S.dma("sp", V32[:, :, :], cv[l].rearrange("(b p) n -> p b n", p=128)[:, :, pp * 128:(pp + 1) * 128], [], [V32])
                    copy("pool", V[:, 32:48, :, 0:64], V32[:, :, :].rearrange("p b (h d) -> p b h d", h=2), [V32], [V])
                    if p == 0:
                        S.dma("sp", LF[:, 32:48, :], clf[l].rearrange("(b p) h -> p b h", p=128), [], [LF])
                    S.ck("pj_a")
                    FL = PO.bufs[1]
                    ok_out, ov_out = (p_fk, p_fv) if fox else (p_sk, p_sv)
                    sk_out, sv_out = (s_fk, s_fv) if fox else (s_sk, s_sv)
                    for ti in range(9):
                        W_ = 512 if ti < 8 else 128
                        c0 = ti * 512
                        hT = HT.next()
                        S.dma("sp", hT[:, :, 0:W_], HTd[:, :, c0:c0 + W_], HTk[ti * 4:ti * 4 + W_ // 128], [hT])
                        bq = PT.next()
                        bk = PT.next()
                        for tb in range(W_ // 128):
                            blk = ti * 4 + tb
                            kblk = blk if blk < 32 else 48
                            bank = PP.next()
                            for c in range(8):
                                S.op("pe", lambda e, c=c: e.matmul(bank[:, 0:384], hT[:, c, tb * 128:(tb + 1) * 128], WP[:, c, :],
                                                                   start=(c == 0), stop=(c == 7)), [hT, WP], [bank], signal=(c == 7))
                            if p == 0 and "f" not in os.environ.get("KSKIP", ""):
                                for c in range(8):
                                    S.op("pe", lambda e, c=c: e.matmul(FL[:, blk * 8:(blk + 1) * 8], hT[:, c, tb * 128:(tb + 1) * 128], WF[:, c, :],
                                                                       start=(c == 0), stop=(c == 7)), [hT, WF], [FL], signal=(c == 7))
                            S.ck("pj_b")
                            S.ck("pj_b%d" % blk)
                            kst = KST.next()
                            vst = VST.next()
                            qb = QB.next()
                            kb = KB.next()
                            if fox:
                                sq = SQ.next()
                                S.op("act", lambda e: e.activation(out=sq[:, :], in_=bank[:, 0:256], func=AF.Square), [bank], [sq])
                                ss = SMALL.next()
                                S.op("dve", lambda e: e.tensor_reduce(out=ss[:, 0:4], in_=sq[:, :].rearrange("p (g d) -> p g d", g=4),
                                                                      axis=AX.X, op=ALU.add), [sq], [ss])
                                rs = rstd_from_ss(ss, 0, 4, 64.0)
                                for hh in range(2):
                                    S.op("dve", lambda e, hh=hh: e.scalar_tensor_tensor(
                                        out=qb[:, hh * 64:(hh + 1) * 64], in0=bank[:, hh * 64:(hh + 1) * 64], scalar=rs[:, hh:hh + 1],
                                        in1=GB[:, 3072:3136], op0=ALU.mult, op1=ALU.mult), [bank, rs, GB], [qb])
                                    S.op("dve", lambda e, hh=hh: e.scalar_tensor_tensor(
                                        out=kst[:, hh * 64:(hh + 1) * 64], in0=bank[:, 128 + hh * 64:128 + (hh + 1) * 64], scalar=rs[:, 2 + hh:3 + hh],
                                        in1=GB[:, 3136:3200], op0=ALU.mult, op1=ALU.mult), [bank, rs, GB], [kst])
                                copy("pool", kb[:, :], kst[:, :], [kst], [kb])
                            else:
                                S.op("act", lambda e: e.activation(out=qb[:, :], in_=bank[:, 0:128], func=AF.Copy, scale=0.125), [bank], [qb])
                                S.op("act", lambda e: e.activation(out=kst[:, :], in_=bank[:, 128:256], func=AF.Copy), [bank], [kst])
                                S.op("dve", lambda e: e.tensor_copy(out=kb[:, :], in_=bank[:, 128:256]), [bank], [kb])
                            S.op("act", lambda e: e.activation(out=vst[:, :], in_=bank[:, 256:384], func=AF.Copy), [bank], [vst])
                            S.op("dve", lambda e: e.tensor_copy(out=V[:, kblk, :, 0:64], in_=bank[:, 256:384].rearrange("p (h d) -> p h d", h=2)),
                                 [bank], [V])
                            S.ck("pj_c")
                            S.ck("pj_c%d" % blk)
                            if blk < 32:
                                S.dma("sp", ok_out[l, blk * 128:(blk + 1) * 128, pp * 128:(pp + 1) * 128], kst[:, :], [kst], [])
                                S.dma("sp", ov_out[l, blk * 128:(blk + 1) * 128, pp * 128:(pp + 1) * 128], vst[:, :], [vst], [])
                            else:
                                S.dma("sp", sk_out[l, :, pp * 128:(pp + 1) * 128], kst[0:16, :], [kst], [])
                                S.dma("sp", sv_out[l, :, pp * 128:(pp + 1) * 128], vst[0:16, :], [vst], [])
                            S.ck("pj_g%d" % blk)
                            S.op("pe", lambda e: e.transpose(bq[:, tb * 128:(tb + 1) * 128], qb[:, :], IDB()), [qb, CB], [bq])
                            S.op("pe", lambda e: e.transpose(bk[:, tb * 128:(tb + 1) * 128], kb[:, :], IDB()), [kb, CB], [bk])
                            S.ck("pj_h%d" % blk)
                        S.ck("pj_d")
                        S.ck("pj_d%d" % ti)
                        kc0 = c0 if ti < 8 else 48 * 128
                        copy("act", QTA[0:64, c0:c0 + W_], bq[0:64, 0:W_], [bq], [QTA])
                        copy("act", QTB[64:128, c0:c0 + W_], bq[64:128, 0:W_], [bq], [QTB])
                        copy("dve", KT[:, kc0:kc0 + W_], bk[:, 0:W_], [bk], [KT])
                        S.ck("pj_f%d" % ti)
                    S.ck("pj_e")
                    for g in range(4):
                        bk = PT.next()
                        for i in range(4):
                            S.op("pe", lambda e, i=i: e.transpose(bk[:, i * 128:(i + 1) * 128], KBC[:, g * 4 + i, :], IDB()), [KBC, CB], [bk], signal=(i == 3))
                        copy(evac_eng(), KT[:, (32 + g * 4) * 128:(36 + g * 4) * 128], bk[:, 0:512], [bk], [KT])

                    S.ck("proj%d_%d" % (l, p))
                    if p == 0:
                        NB8 = NQB * 8
                        S.op("dve", lambda e: e.tensor_tensor(out=FT1[:, :].rearrange("p (b h) -> p b h", h=8),
                                                              in0=FL[:, 0:NB8].rearrange("p (b h) -> p b h", h=8),
                                                              in1=GB[:, 3456:3464].unsqueeze(1).broadcast_to([128, NQB, 8]), op=ALU.add),
                             [FL, GB], [FT1])
                        S.op("act", lambda e: e.activation(out=FT2[:, :], in_=FT1[:, :], func=AF.Exp, scale=-1.0), [FT1], [FT2])
                        S.op("act", lambda e: e.activation(out=FT1[:, :], in_=FT2[:, :], func=AF.Ln, bias=ONEC[:, 0:1]), [FT2, ONEC], [FT1])
                        S.op("dve", lambda e: e.tensor_scalar(out=LF[:, 0:32, :], in0=FT1[:, 0:256].rearrange("p (b h) -> p b h", h=8),
                                                              scalar1=-1.0, scalar2=None, op0=ALU.mult), [FT1], [LF])
                        S.op("dve", lambda e: e.tensor_scalar(out=LF[:, 48, :], in0=FT1[:, 256:264], scalar1=-1.0, scalar2=None, op0=ALU.mult),
                             [FT1], [LF])
                        for q4 in range(4):
                            S.dma("sp", p_lf[l].rearrange("(b p) h -> p b h", p=128)[:, q4 * 8:(q4 + 1) * 8, :], LF[:, q4 * 8:(q4 + 1) * 8, :], [LF], [])
                        S.dma("sp", s_lf[l], LF[0:16, 48, :], [LF], [])
                        bank = PP.next()
                        S.op("pe", lambda e: e.matmul(bank[:, 0:NKB * 8], TRIU(), LF[:, :, :].rearrange("p b h -> p (b h)"), start=True, stop=True),
                             [CF, LF], [bank])
                        S.op("dve", lambda e: e.tensor_copy(out=WC[:, :, :].rearrange("p b h -> p (b h)"), in_=bank[:, 0:NKB * 8]), [bank], [WC])
                        bank2 = PP.next()
                        S.op("pe", lambda e: e.matmul(bank2[:, 0:NKB * 8], SEL127(), WC[:, :, :].rearrange("p b h -> p (b h)"), start=True, stop=True),
                             [CF, WC], [bank2])
                        S.op("act", lambda e: e.activation(out=TB[:, :, :].rearrange("p b h -> p (b h)"), in_=bank2[:, 0:NKB * 8], func=AF.Copy),
                             [bank2], [TB])
                        for (a, b_) in ((0, 32), (32, 49)):
                            for h in range(8):
                                S.op("dve", lambda e, h=h: e.tensor_tensor_scan(out=CS[:, a:b_, h], data0=ONESF[:, 0:b_ - a], data1=TB[:, a:b_, h],
                                                                                initial=0.0, op0=ALU.mult, op1=ALU.add), [TB, ONESF], [CS])
                        S.op("dve", lambda e: e.tensor_tensor(out=FM[:, :, :], in0=WC[:, :, :], in1=CS[:, :, :], op=ALU.add), [WC, CS], [FM])
                        S.op("dve", lambda e: e.tensor_tensor(out=FM[:, :, :], in0=FM[:, :, :], in1=TB[:, :, :], op=ALU.subtract), [FM, TB], [FM])
                        S.op("dve", lambda e: e.tensor_scalar(out=FS[:, :, 0, :], in0=FM[:, :, :], scalar1=-1.0, scalar2=None, op0=ALU.mult), [FM], [FS])
                        S.op("dve", lambda e: e.scalar_tensor_tensor(out=R1[:, :, :], in0=FM[:, :, :], scalar=-1.0, in1=FS[:, :, 0, :],
                                                                     op0=ALU.mult, op1=ALU.subtract), [FM, FS], [R1])
                        S.op("dve", lambda e: e.tensor_copy(out=FS[:, :, 1, :], in_=R1[:, :, :]), [R1], [FS])
                        S.op("dve", lambda e: e.tensor_tensor(out=R1[:, :, :], in0=R1[:, :, :], in1=FS[:, :, 1, :], op=ALU.subtract), [R1, FS], [R1])
                        S.op("dve", lambda e: e.tensor_copy(out=FS[:, :, 2, :], in_=R1[:, :, :]), [R1], [FS])
                        for g in range(7):
                            n_ = min(8, NKB - g * 8)
                            bk = PT.next()
                            for i in range(n_):
                                S.op("pe", lambda e, i=i: e.transpose(bk[0:24, i * 128:(i + 1) * 128],
                                                                      FS[:, g * 8 + i, :, :].rearrange("p s h -> p (s h)"), IDB()),
                                     [FS, CB], [bk], signal=(i == n_ - 1))
                            copy(evac_eng(), FKT[0:24, g * 1024:g * 1024 + n_ * 128], bk[0:24, 0:n_ * 128], [bk], [FKT])

                    S.ck("f%d_%d" % (l, p))
                    chunks = []
                    for qblk in range(NQB):
                        kbs = list(range(0, qblk + 1)) if qblk < 32 else list(range(32, 49))
                        groups = [kbs[i:i + 4] for i in range(0, len(kbs), 4)]
                        for e_ in range(2):
                            for gi in range(len(groups) - 1, -1, -1):
                                chunks.append(dict(q=qblk, e=e_, kbs=groups[gi], diag=(gi == len(groups) - 1),
                                                   first=(gi == len(groups) - 1), last=(gi == 0), own=kbs[-1]))
                    state = {}

                    def stage_z(ch):
                        z = ZB.next()
                        ch["z"] = z
                        P0 = 64 * ch["e"]
                        w = 128 * len(ch["kbs"])
                        ch["w"] = w
                        q0 = ch["q"] * 128
                        k0 = ch["kbs"][0] * 128
                        more = fox or ch["diag"]
                        QTe = QTA if ch["e"] == 0 else QTB
                        S.op("pe", lambda e: e.matmul(z[:, 0:w], QTe[:, q0:q0 + 128], KT[:, k0:k0 + w], start=True, stop=not more),
                             [QTe, KT], [z], signal=not more)
                        if fox:
                            h = 2 * pp + ch["e"]
                            S.op("pe", lambda e: e.matmul(z[:, 0:w], CB[:, 640 + h * 128:640 + (h + 1) * 128], FKT[:, k0:k0 + w], start=False, stop=not ch["diag"]),
                                 [CB, FKT], [z], signal=not ch["diag"])
                        if ch["diag"]:
                            S.op("pe", lambda e: e.matmul(z[:, w - 128:w], IDB(), NEGI() if fox else NEGS(), start=False, stop=True), [CB], [z])

                    def stage_e1(ch):
                        if fox:
                            return
                        z = ch["z"]
                        w = ch["w"]
                        eb = EB.next()
                        ch["eb"] = eb
                        S.op("act", lambda e: e.activation(out=eb[:, 0:w], in_=z[:, 0:w], func=AF.Exp), [z], [eb])

                    def stage_e1b(ch):
                        if fox:
                            return
                        w = ch["w"]
                        eb = ch["eb"]
                        sp = SPB.next()
                        ch["sp"] = sp
                        S.op("act", lambda e: e.activation(out=sp[:, 0:w], in_=eb[:, 0:w], func=AF.Ln, bias=ONEC[:, 0:1]), [eb, ONEC], [sp])

                    def stage_e2a(ch):
                        if fox:
                            return
                        z = ch["z"]
                        w = ch["w"]
                        eb = ch["eb"]
                        sp = ch["sp"]
                        c = CBF.next()
                        if ch["first"]:
                            S.op("dve", lambda e: e.tensor_tensor_scan(out=c[:, 0:w][:, ::-1], data0=ONESF[:, 0:w], data1=sp[:, 0:w][:, ::-1],
                                                                       initial=0.0, op0=ALU.mult, op1=ALU.add), [sp, ONESF], [c])
                        else:
                            pc = state["prevc"]
                            S.op("dve", lambda e: e.tensor_tensor_scan(out=c[:, 0:w][:, ::-1], data0=ONESF[:, 0:w], data1=sp[:, 0:w][:, ::-1],
                                                                       initial=pc[:, 0:1], op0=ALU.mult, op1=ALU.add), [sp, ONESF, pc], [c])
                        state["prevc"] = c
                        S.op("dve", lambda e: e.tensor_tensor(out=eb[:, 0:w], in0=z[:, 0:w], in1=c[:, 0:w], op=ALU.subtract), [z, c], [eb])

                    def stage_e2b(ch):
                        z = ch["z"]
                        w = ch["w"]
                        a = AB.next()
                        ch["a"] = a
                        if fox:
                            h = 2 * pp + ch["e"]
                            S.op("act", lambda e: e.activation(out=a[:, 0:w], in_=z[:, 0:w], func=AF.Exp, bias=FM[:, ch["own"], h:h + 1]),
                                 [z, FM], [a])
                        else:
                            eb = ch["eb"]
                            S.op("act", lambda e: e.activation(out=a[:, 0:w], in_=eb[:, 0:w], func=AF.Exp), [eb], [a])

                    def stage_pv(ch):
                        a = ch["a"]
                        w = ch["w"]
                        nb = len(ch["kbs"])
                        bt = PT.next()
                        for i in range(nb):
                            S.op("pe", lambda e, i=i: e.transpose(bt[:, i * 128:(i + 1) * 128], a[:, i * 128:(i + 1) * 128], IDB()),
                                 [a, CB], [bt], signal=(i == nb - 1))
                        ch["bt"] = bt

                    def stage_ev(ch):
                        bt = ch["bt"]
                        w = ch["w"]
                        at = ATB.next()
                        ch["at"] = at
                        copy("dve" if fox else "act", at[:, 0:w], bt[:, 0:w], [bt], [at])

                    def stage_pvm(ch):
                        at = ch["at"]
                        w = ch["w"]
                        nb = len(ch["kbs"])
                        if ch["first"]:
                            state["o"] = PO.next()
                        o = state["o"]
                        for i in range(nb):
                            kb_ = ch["kbs"][i]
                            lastmm = ch["last"] and i == nb - 1
                            S.op("pe", lambda e, i=i, kb_=kb_: e.matmul(o[:, 0:66], at[:, i * 128:(i + 1) * 128], V[:, kb_, ch["e"], :],
                                                                        start=(ch["first"] and i == 0), stop=lastmm),
                                 [at, V], [o], signal=(i == nb - 1))
                        if ch["last"]:
                            if ch["e"] == 0:
                                state["ost"] = OST.next()
                            ost = state["ost"]
                            e_ = ch["e"]
                            if fox:
                                rc = SMALL.next()
                                S.op("dve", lambda e: e.reciprocal(out=rc[:, 0:1], in_=o[:, 64:65]), [o], [rc])
                                S.op("dve", lambda e: e.tensor_scalar(out=ost[:, e_ * 64:(e_ + 1) * 64], in0=o[:, 0:64], scalar1=rc[:, 0:1], scalar2=None,
                                                                      op0=ALU.mult), [o, rc], [ost])
                            else:
                                copy("act", ost[:, e_ * 64:(e_ + 1) * 64], o[:, 0:64], [o], [ost])
                            if e_ == 1:
                                col0 = (0 if fox else 512) + pp * 128
                                qb_ = ch["q"]
                                S.dma("sp", OT[qb_ * 128:(qb_ + 1) * 128, col0:col0 + 128], ost[:, :], [ost], [OTk[qb_]])

                    n = len(chunks)
                    stage_z(chunks[0])
                    stage_z(chunks[1])
                    stage_e1(chunks[0])
                    stage_e1b(chunks[0])
                    for i in range(n + 3):
                        if 2 <= i <= n + 1:
                            stage_pv(chunks[i - 2])
                        if i + 2 < n:
                            stage_z(chunks[i + 2])
                        if i + 1 < n:
                            stage_e1(chunks[i + 1])
                        if 2 <= i <= n + 1:
                            stage_ev(chunks[i - 2])
                        if i + 1 < n:
                            stage_e1b(chunks[i + 1])
                        if i < n:
                            stage_e2a(chunks[i])
                        if 1 <= i <= n:
                            stage_e2b(chunks[i - 1])
                        if 3 <= i <= n + 2:
                            stage_pvm(chunks[i - 3])
                    S.ck("att%d_%d" % (l, p))
                S.barrier()

            with ExitStack() as st:
                XTL = sb(st, "XTL", [128, 8, 512], F32)
                ACTA = sb(st, "ACTA", [128, 8, 512], BF16)
                ACTB = sb(st, "ACTB", [128, 8, 512], BF16)
                QM = sb(st, "QM", [128, 8, 512], F32)
                SQD = sb(st, "SQD", [128, 8, 512], BF16)
                RS = Rot([sb(st, "rs%d" % i, [128, 512], F32) for i in range(2)])
                PTB = sb(st, "PTB", [128, 2, 512], BF16)
                HID = sb(st, "HID", [128, 16, 512], BF16)
                RL = Rot([sb(st, "rl%d" % i, [128, 512], F32) for i in range(2)])
                WBLK = Rot([sb(st, "wblk%d" % i, [128, 8, 512], BF16) for i in range(3)])
                MKT = [sb(st, "MKT%d" % i, [128, 8, 256], BF16) for i in range(2)]
                MV = [sb(st, "MV%d" % i, [128, 2, 1024], BF16) for i in range(2)]
                MST = Rot([sb(st, "mst%d" % i, [128, 1024], F32) for i in range(2)])
                MB = Rot([sb(st, "mb%d" % i, [128, 1024], BF16) for i in range(2)])
                MT = sb(st, "MT", [128, 8, 256], BF16)

                def load_wblk(name, k0, n0):
                    wb = WBLK.next()
                    S.dma("sp", wb[:, :, :], WB[name][l].rearrange("(c p) n -> p c n", p=128)[:, k0:k0 + 8, n0:n0 + 512], WBk[name][l], [wb])
                    return wb

                for mb_ in range(2):
                    mt = XTOK.next()
                    S.dma("sp", mt[:, :], mem[mb_ * 128:(mb_ + 1) * 128, :], [], [mt])
                    ss = SMALL.next()
                    S.op("act", lambda e: e.activation(out=JUNK[:, :], in_=mt[:, :], func=AF.Square, accum_out=ss[:, 0:1]), [mt], [JUNK, ss])
                    rs = rstd_from_ss(ss, 0, 1, 1024.0)
                    hb = HB.next()
                    S.op("dve", lambda e: e.scalar_tensor_tensor(out=hb[:, :], in0=mt[:, :], scalar=rs[:, 0:1], in1=GB[:, 1024:2048],
                                                                 op0=ALU.mult, op1=ALU.mult), [mt, rs, GB], [hb])
                    bank = PT.next()
                    for c in range(8):
                        S.op("pe", lambda e, c=c: e.transpose(bank[:, c * 128:(c + 1) * 128], hb[:, c * 128:(c + 1) * 128], IDB()),
                             [hb, CB], [bank], signal=(c == 7))
                    copy(evac_eng(), MT[:, :, mb_ * 128:(mb_ + 1) * 128], bank[:, 0:1024].rearrange("p (c t) -> p c t", c=8), [bank], [MT])
                for which in ("w_mk", "w_mv"):
                    for mb_ in range(2):
                        stg = MST.next()
                        for ng in range(2):
                            wb = load_wblk(which, 0, ng * 512)
                            bank = P4.next()
                            for c in range(8):
                                S.op("pe", lambda e: e.matmul(bank[:, :], MT[:, c, mb_ * 128:(mb_ + 1) * 128], wb[:, c, :], start=(c == 0), stop=(c == 7)),
                                     [MT, wb], [bank], signal=(c == 7))
                            if which == "w_mk":
                                ss = SMALL.next()
                                for hh in range(2):
                                    S.op("act", lambda e: e.activation(out=JUNK[:, 0:256], in_=bank[:, hh * 256:(hh + 1) * 256], func=AF.Square,
                                                                       accum_out=ss[:, hh:hh + 1]), [bank], [JUNK, ss])
                                rs = rstd_from_ss(ss, 0, 2, 256.0)
                                for hh in range(2):
                                    S.op("dve", lambda e: e.scalar_tensor_tensor(
                                        out=stg[:, ng * 512 + hh * 256:ng * 512 + (hh + 1) * 256], in0=bank[:, hh * 256:(hh + 1) * 256],
                                        scalar=rs[:, hh:hh + 1], in1=GB[:, 3200:3456], op0=ALU.mult, op1=ALU.mult), [bank, rs, GB], [stg])
                            else:
                                copy("act", stg[:, ng * 512:(ng + 1) * 512], bank[:, :], [bank], [stg])
                        dst = p_mk if which == "w_mk" else p_mv
                        S.dma("sp", dst[l, mb_ * 128:(mb_ + 1) * 128, :], stg[:, :], [stg], [])
                        if which == "w_mk":
                            mbb = MB.next()
                            copy("pool", mbb[:, :], stg[:, :], [stg], [mbb])
                            bank2 = PT.next()
                            for c in range(8):
                                S.op("pe", lambda e: e.transpose(bank2[:, c * 128:(c + 1) * 128], mbb[:, c * 128:(c + 1) * 128], IDB()),
                                     [mbb, CB], [bank2], signal=(c == 7))
                            copy(evac_eng(), MKT[0][:, :, mb_ * 128:(mb_ + 1) * 128], bank2[:, 0:1024].rearrange("p (c t) -> p c t", c=8),
                                 [bank2], [MKT[0]])
                        else:
                            copy("pool", MV[0][:, mb_, :], stg[:, :], [stg], [MV[0]])
                for mb_ in range(2):
                    mbb = MB.next()
                    m32 = XTOK.next()
                    S.dma("sp", m32[:, :], cmk[l, mb_ * 128:(mb_ + 1) * 128, :], [], [m32])
                    copy("pool", mbb[:, :], m32[:, :], [m32], [mbb])
                    bank2 = PT.next()
                    for c in range(8):
                        S.op("pe", lambda e, c=c: e.transpose(bank2[:, c * 128:(c + 1) * 128], mbb[:, c * 128:(c + 1) * 128], IDB()),
                             [mbb, CB], [bank2], signal=(c == 7))
                    copy(evac_eng(), MKT[1][:, :, mb_ * 128:(mb_ + 1) * 128], bank2[:, 0:1024].rearrange("p (c t) -> p c t", c=8), [bank2], [MKT[1]])
                    v32 = XTOK.next()
                    S.dma("sp", v32[:, :], cmv[l, mb_ * 128:(mb_ + 1) * 128, :], [], [v32])
                    copy("pool", MV[1][:, mb_, :], v32[:, :], [v32], [MV[1]])

                S.ck("memkv%d" % l)

                def fm_rmsnorm(W_, gcol0, out_buf):
                    for c in range(8):
                        S.op("act", lambda e, c=c: e.activation(out=SQD[:, c, 0:W_], in_=XTL[:, c, 0:W_], func=AF.Square), [XTL], [SQD])
                    bank = PP.next()
                    for c in range(8):
                        S.op("pe", lambda e, c=c: e.matmul(bank[:, 0:W_], ONESB[:, :], SQD[:, c, 0:W_], start=(c == 0), stop=(c == 7)),
                             [ONESB, SQD], [bank], signal=(c == 7))
                    t1 = RS.next()
                    S.op("act", lambda e: e.activation(out=t1[:, 0:W_], in_=bank[:, 0:W_], func=AF.Ln, scale=1.0 / 1024.0, bias=EPSC[:, 0:1]),
                         [bank, EPSC], [t1])
                    r = RS.next()
                    S.op("act", lambda e: e.activation(out=r[:, 0:W_], in_=t1[:, 0:W_], func=AF.Exp, scale=-0.5), [t1], [r])
                    for c in range(8):
                        S.op("dve", lambda e, c=c: e.scalar_tensor_tensor(out=out_buf[:, c, 0:W_], in0=XTL[:, c, 0:W_], scalar=GC[:, gcol0 + c:gcol0 + c + 1],
                                                                          in1=r[:, 0:W_], op0=ALU.mult, op1=ALU.mult), [XTL, GC, r], [out_buf])

                def dense(name, in_buf, kc0, nk8, n0, nm, W_, out_fn, wk0=0):
                    for mg in range(0, nm, 4):
                        if nk8 == 1:
                            wb = load_wblk(name, wk0, n0 + mg * 128)
                            for m in range(4):
                                bank = P6.next()
                                for c in range(8):
                                    S.op("pe", lambda e: e.matmul(bank[:, 0:W_], wb[:, c, m * 128:(m + 1) * 128], in_buf[:, kc0 + c, 0:W_],
                                                                  start=(c == 0), stop=(c == 7)), [wb, in_buf], [bank], signal=(c == 7))
                                out_fn(mg + m, bank)
                            continue
                        banks = [P6.next() for _ in range(4)]
                        for kg in range(nk8):
                            wb = load_wblk(name, wk0 + kg * 8, n0 + mg * 128)
                            for m in range(4):
                                for c in range(8):
                                    first = (kg == 0 and c == 0)
                                    lastk = (kg == nk8 - 1 and c == 7)
                                    S.op("pe", lambda e: e.matmul(
                                        banks[m][:, 0:W_], wb[:, c, m * 128:(m + 1) * 128], in_buf[:, kc0 + kg * 8 + c, 0:W_], start=first, stop=lastk),
                                         [wb, in_buf], [banks[m]], signal=(c == 7))
                        for m in range(4):
                            out_fn(mg + m, banks[m])

                def add_to_x(W_):
                    def f(m, bank):
                        S.op("dve", lambda e: e.tensor_tensor(out=XTL[:, m, 0:W_], in0=bank[:, 0:W_], in1=XTL[:, m, 0:W_], op=ALU.add), [bank, XTL], [XTL])
                    return f

                for ti in range(9):
                    W_ = 512 if ti < 8 else 128
                    c0 = ti * 512
                    si = 0 if ti < 8 else 1
                    nblk = W_ // 128
                    for tb in range(nblk):
                        blk = ti * 4 + tb
                        ot = XTOK.next()
                        S.dma("sp", ot[:, :], OT[blk * 128:(blk + 1) * 128, :], [OTk[blk]], [ot])
                        ss = SMALL.next()
                        for hf in range(2):
                            S.op("act", lambda e, hf=hf: e.activation(out=JUNK[:, 0:512], in_=ot[:, hf * 512:(hf + 1) * 512], func=AF.Square,
                                                                       accum_out=ss[:, hf:hf + 1]), [ot], [JUNK, ss])
                        rs = rstd_from_ss(ss, 0, 2, 512.0)
                        hb = HB.next()
                        for hf in range(2):
                            S.op("dve", lambda e, hf=hf: e.scalar_tensor_tensor(out=hb[:, hf * 512:(hf + 1) * 512], in0=ot[:, hf * 512:(hf + 1) * 512],
                                                                                scalar=rs[:, hf:hf + 1], in1=GB[:, 2048 + hf * 512:2048 + (hf + 1) * 512],
                                                                                op0=ALU.mult, op1=ALU.mult), [ot, rs, GB], [hb])
                        bank = PT.next()
                        for c in range(8):
                            S.op("pe", lambda e, c=c: e.transpose(bank[:, c * 128:(c + 1) * 128], hb[:, c * 128:(c + 1) * 128], IDB()),
                                 [hb, CB], [bank], signal=(c == 7))
                        copy(evac_eng(), ACTA[:, :, tb * 128:(tb + 1) * 128], bank[:, 0:1024].rearrange("p (c t) -> p c t", c=8), [bank], [ACTA])
                    S.ck("c1_%d_%d" % (l, ti))
                    S.dma("sp", XTL[:, :, 0:W_], XT[:, :, c0:c0 + W_], XTk[ti * 4:ti * 4 + nblk], [XTL])
                    dense("w_out", ACTA, 0, 1, 0, 8, W_, add_to_x(W_))
                    S.ck("wout_%d_%d" % (l, ti))
                    fm_rmsnorm(W_, 8, ACTA)

                    def q_out(m, bank):
                        S.op("act", lambda e: e.activation(out=QM[:, m, 0:W_], in_=bank[:, 0:W_], func=AF.Copy), [bank], [QM])
                        S.op("act", lambda e: e.activation(out=SQD[:, m, 0:W_], in_=bank[:, 0:W_], func=AF.Square), [bank], [SQD])
                    dense("w_mq", ACTA, 0, 1, 0, 8, W_, q_out)
                    for hh in range(4):
                        bank = PP.next()
                        for c in range(2):
                            S.op("pe", lambda e, c=c: e.matmul(bank[:, 0:W_], ONESB[:, :], SQD[:, 2 * hh + c, 0:W_], start=(c == 0), stop=(c == 1)),
                                 [ONESB, SQD], [bank], signal=(c == 1))
                        t1 = RS.next()
                        S.op("act", lambda e: e.activation(out=t1[:, 0:W_], in_=bank[:, 0:W_], func=AF.Ln, scale=1.0 / 256.0, bias=EPSC[:, 0:1]),
                             [bank, EPSC], [t1])
                        r = RS.next()
                        S.op("act", lambda e: e.activation(out=r[:, 0:W_], in_=t1[:, 0:W_], func=AF.Exp, scale=-0.5), [t1], [r])
                        for c in range(2):
                            S.op("dve", lambda e, c=c: e.scalar_tensor_tensor(out=ACTB[:, 2 * hh + c, 0:W_], in0=QM[:, 2 * hh + c, 0:W_],
                                                                              scalar=GC[:, 24 + c:25 + c], in1=r[:, 0:W_], op0=ALU.mult, op1=ALU.mult),
                                 [QM, GC, r], [ACTB])
                    for hh in range(4):
                        for mc in range(2):
                            bank = P4.next()
                            for c in range(2):
                                S.op("pe", lambda e, c=c: e.matmul(bank[:, 0:W_], MKT[si][:, 2 * hh + c, mc * 128:(mc + 1) * 128], ACTB[:, 2 * hh + c, 0:W_],
                                                                   start=(c == 0), stop=(c == 1)), [MKT[si], ACTB], [bank], signal=(c == 1))
                            S.op("act", lambda e: e.activation(out=PTB[:, mc, 0:W_], in_=bank[:, 0:W_], func=AF.Exp), [bank], [PTB])
                        bank = PP.next()
                        for mc in range(2):
                            S.op("pe", lambda e, mc=mc: e.matmul(bank[:, 0:W_], ONESB[:, :], PTB[:, mc, 0:W_], start=(mc == 0), stop=(mc == 1)),
                                 [ONESB, PTB], [bank], signal=(mc == 1))
                        t1 = RS.next()
                        S.op("act", lambda e: e.activation(out=t1[:, 0:W_], in_=bank[:, 0:W_], func=AF.Ln), [bank], [t1])
                        rd = RS.next()
                        S.op("act", lambda e: e.activation(out=rd[:, 0:W_], in_=t1[:, 0:W_], func=AF.Exp, scale=-1.0), [t1], [rd])
                        for dc in range(2):
                            bank = P4.next()
                            for mc in range(2):
                                S.op("pe", lambda e, mc=mc: e.matmul(bank[:, 0:W_], MV[si][:, mc, (2 * hh + dc) * 128:(2 * hh + dc + 1) * 128], PTB[:, mc, 0:W_],
                                                                     start=(mc == 0), stop=(mc == 1)), [MV[si], PTB], [bank], signal=(mc == 1))
                            S.op("dve", lambda e: e.tensor_tensor(out=ACTA[:, 2 * hh + dc, 0:W_], in0=bank[:, 0:W_], in1=rd[:, 0:W_], op=ALU.mult),
                                 [bank, rd], [ACTA])
                    dense("w_mo", ACTA, 0, 1, 0, 8, W_, add_to_x(W_))
                    S.ck("cross_%d_%d" % (l, ti))
                    fm_rmsnorm(W_, 16, ACTB)
                    for half in range(2):
                        def h_out(m, bank):
                            rl = RL.next()
                            S.op("act", lambda e: e.activation(out=rl[:, 0:W_], in_=bank[:, 0:W_], func=AF.Relu), [bank], [rl])
                            S.op("pool", lambda e: e.tensor_tensor(out=HID[:, m, 0:W_], in0=rl[:, 0:W_], in1=rl[:, 0:W_], op=ALU.mult), [rl], [HID])
                        dense("w_ff1", ACTB, 0, 1, half * 2048, 16, W_, h_out)
                        dense("w_ff2", HID, 0, 2, 0, 8, W_, add_to_x(W_), wk0=half * 16)
                    S.ck("ffn_%d_%d" % (l, ti))
                    if l == 0:
                        S.dma("sp", XT[:, :, c0:c0 + W_], XTL[:, :, 0:W_], [XTL], XTk[ti * 4:ti * 4 + nblk])
                        fm_rmsnorm(W_, 0, ACTA)
                        S.dma("sp", HTd[:, :, c0:c0 + W_], ACTA[:, :, 0:W_], [ACTA], HTk[ti * 4:ti * 4 + nblk])
                    else:
                        for tb in range(nblk):
                            blk = ti * 4 + tb
                            yt = XTOK.next()
                            for half in range(2):
                                bank = P4.next()
                                for c in range(4):
                                    cc = half * 4 + c
                                    S.op("pe", lambda e, c=c, cc=cc: e.transpose(bank[:, c * 128:(c + 1) * 128], XTL[:, cc, tb * 128:(tb + 1) * 128], IDF()),
                                         [XTL, CF], [bank], signal=(c == 3))
                                copy(evac_eng(), yt[:, half * 512:(half + 1) * 512], bank[:, 0:512], [bank], [yt])
                            if blk < 32:
                                S.dma("sp", yp[blk * 128:(blk + 1) * 128, :], yt[:, :], [yt], [])
                            else:
                                S.dma("sp", ys[:, :], yt[0:16, :], [yt], [])
                    S.ck("tile_%d_%d" % (l, ti))
                S.barrier()

        S.dead = False
        S.barrier(["sp"])
        print("instructions emitted:", S.ninst, "sp dmas:", sum(v for k, v in S.val.items() if k.startswith("D:sp")) // 16,
              {k: v for k, v in S.val.items() if k.startswith("E:")})
    return nc


_NC_CACHE = {}


def _consts():
    c = np.zeros((128, NCST), np.float32)
    idx = np.arange(128)
    c[:, 0:128] = np.eye(128, dtype=np.float32)
    c[:, 128:256] = (idx[:, None] <= idx[None, :]).astype(np.float32)
    c[127, 256:384] = 1.0
    c[:, 384:512] = np.where(idx[None, :] > idx[:, None], NEGM, 0.0)
    c[:, 512:640] = np.where(idx[None, :] >= idx[:, None], NEGM, 0.0)
    for r in range(24):
        h = r % 8
        c[r, 640 + h * 128:640 + (h + 1) * 128] = 1.0
    return c


def kernel(x_prompt, x_sample, mem_prompt, cache_fox_k, cache_fox_v, cache_fox_logf, cache_sb_k, cache_sb_v,
           cache_mem_k, cache_mem_v, g_mix, w_in, b_forget, g_fox_q, g_fox_k, g_out_fox, g_out_sb, w_out,
           g_cross, g_mem, w_mq, w_mk, w_mv, g_mq, g_mk, w_mo, g_ffn, w_ff1, w_ff2):
    f = lambda a: np.ascontiguousarray(np.asarray(a, dtype=np.float32))
    if "nc" not in _NC_CACHE:
        _NC_CACHE["nc"] = build()
    nc = _NC_CACHE["nc"]
    gbp = np.zeros((2, 128, NGB), np.float32)
    gcp = np.zeros((2, 128, NGC), np.float32)
    for l in range(2):
        row = np.concatenate([f(g_mix)[l], f(g_mem)[l], f(g_out_fox)[l], f(g_out_sb)[l], f(g_fox_q)[l], f(g_fox_k)[l],
                              f(g_mk)[l], f(b_forget)[l]])
        gbp[l] = np.broadcast_to(row[None, :], (128, NGB))
        gcp[l, :, 0:8] = f(g_mix)[min(l + 1, 1)].reshape(8, 128).T
        gcp[l, :, 8:16] = f(g_cross)[l].reshape(8, 128).T
        gcp[l, :, 16:24] = f(g_ffn)[l].reshape(8, 128).T
        gcp[l, :, 24:26] = f(g_mq)[l].reshape(2, 128).T
    cst = _consts()
    shared = {"w_in": f(w_in), "w_out": f(w_out), "w_mq": f(w_mq), "w_mk": f(w_mk), "w_mv": f(w_mv), "w_mo": f(w_mo),
              "w_ff1": f(w_ff1), "w_ff2": f(w_ff2), "gb": gbp, "gc": gcp, "cst": cst}
    in_maps = []
    for b in range(8):
        xs_pad = np.zeros((TS, D), np.float32)
        xs_pad[0:16] = f(x_sample)[b]
        m = dict(shared)
        m.update({
            "xp": f(x_prompt)[b], "xs": xs_pad, "mem": f(mem_prompt)[b],
            "cfk": f(cache_fox_k)[:, b].reshape(2, 2048, 512), "cfv": f(cache_fox_v)[:, b].reshape(2, 2048, 512),
            "clf": f(cache_fox_logf)[:, b], "csk": f(cache_sb_k)[:, b].reshape(2, 2048, 512),
            "csv": f(cache_sb_v)[:, b].reshape(2, 2048, 512),
            "cmk": f(cache_mem_k)[:, b].reshape(2, 256, 1024), "cmv": f(cache_mem_v)[:, b].reshape(2, 256, 1024),
        })
        m = {k: np.ascontiguousarray(v) for k, v in m.items()}
        in_maps.append(m)
    res = run_bass_kernel_spmd(nc, in_maps, core_ids=list(range(8)))
    R = res.results

    def g(name):
        return np.stack([np.asarray(R[b][name]) for b in range(8)], axis=0)

    def kv(name, t):
        return np.ascontiguousarray(np.transpose(g(name), (1, 0, 2, 3)).reshape(2, 8, t, 8, 64))

    y_p = g("yp")
    y_s = g("ys")
    outs = (y_p, y_s,
            kv("p_fk", T), kv("p_fv", T), np.ascontiguousarray(np.transpose(g("p_lf"), (1, 0, 2, 3))),
            kv("p_sk", T), kv("p_sv", T),
            np.ascontiguousarray(np.transpose(g("p_mk"), (1, 0, 2, 3)).reshape(2, 8, 256, 4, 256)),
            np.ascontiguousarray(np.transpose(g("p_mv"), (1, 0, 2, 3)).reshape(2, 8, 256, 4, 256)),
            kv("s_fk", 16), kv("s_fv", 16), np.ascontiguousarray(np.transpose(g("s_lf"), (1, 0, 2, 3))),
            kv("s_sk", 16), kv("s_sv", 16))
    return tuple(np.asarray(o, dtype=np.float32) for o in outs)
```

```python
import os
import numpy as np
from contextlib import ExitStack
import concourse.bass as bass
import concourse.mybir as mybir
from concourse.bass_utils import run_bass_kernel_spmd

F32 = mybir.dt.float32
BF16 = mybir.dt.bfloat16
AF = mybir.ActivationFunctionType
ALU = mybir.AluOpType
AX = mybir.AxisListType

T = 4096
TS = 128
TT = T + TS
D = 1024
NQB = 33
NKB = 49
EPS = 1e-6
NEGM = -30000.0
NGB = 3464
NGC = 26
NCST = 1664
NCH = 14
SAME_ENGINE_SYNC = os.environ.get("KSES", "1") == "1"
SERIAL = os.environ.get("KSER", "0")
PSUM_EXCL = os.environ.get("KPX", "1") == "1"


class Tk:
    __slots__ = ("w", "r")

    def __init__(self):
        self.w = None
        self.r = {}


class Buf:
    def __init__(self, t):
        self.t = t
        self.k = Tk()

    def __getitem__(self, key):
        return self.t[key]


class Rot:
    def __init__(self, bufs):
        self.bufs = bufs
        self.i = 0

    def next(self):
        b = self.bufs[self.i]
        self.i = (self.i + 1) % len(self.bufs)
        return b


class Sched:
    def __init__(self, nc, stack):
        self.nc = nc
        self.eng = {"pe": nc.tensor, "act": nc.scalar, "dve": nc.vector, "pool": nc.gpsimd, "sp": nc.sync}
        self.sem = {}
        self.val = {}
        for e in self.eng:
            self.sem["E:" + e] = stack.enter_context(nc.semaphore("sem_" + e))
            self.val["E:" + e] = 0
        for q in ("sp", "pool"):
            for c in range(NCH):
                k = "D:%s:%d" % (q, c)
                self.sem[k] = stack.enter_context(nc.semaphore("dsem_%s_%d" % (q, c)))
                self.val[k] = 0
        self.known = {e: {} for e in self.eng}
        self.rr = {"sp": 0, "pool": 0}
        self.ninst = 0
        self.dead = False
        self.stop = os.environ.get("KSTOP", "")
        self.serial = SERIAL
        self.last_tok = None
        self.last_ew = None

    def ck(self, name):
        if self.stop and name == self.stop:
            self.dead = True

    def _deps(self, eng, reads, writes):
        need = {}

        def add(k, v):
            if need.get(k, 0) < v:
                need[k] = v

        for b in reads:
            if b.k.w is not None:
                add(*b.k.w)
        for b in writes:
            if b.k.w is not None:
                add(*b.k.w)
            for k, v in b.k.r.items():
                add(k, v)
        if self.serial == "1" and self.last_tok is not None:
            add(*self.last_tok)
        if self.serial == "2" and eng in ("act", "dve", "pool") and self.last_ew is not None:
            add(*self.last_ew)
        waits = []
        kn = self.known[eng]
        for k, v in need.items():
            if k == "E:" + eng and (eng == "pe" or not SAME_ENGINE_SYNC):
                continue
            if kn.get(k, 0) >= v:
                continue
            kn[k] = v
            waits.append((k, v))
        return waits

    def _mark(self, tok, reads, writes):
        k, v = tok
        for b in reads:
            if b.k.r.get(k, 0) < v:
                b.k.r[k] = v
        for b in writes:
            b.k.w = tok
            b.k.r = {}

    def op(self, eng, fn, reads=(), writes=(), signal=True):
        assert signal or eng == "pe"
        if self.dead:
            return
        if PSUM_EXCL:
            px = [b for b in reads if getattr(b, "px", False)]
            if px:
                writes = list(writes) + px
        waits = self._deps(eng, reads, writes)
        e = self.eng[eng]
        for k, v in waits:
            e.wait_ge(self.sem[k], v)
        inst = fn(e)
        key = "E:" + eng
        if signal:
            self.val[key] += 1
            inst.then_inc(self.sem[key], 1)
            tok = (key, self.val[key])
        else:
            tok = (key, self.val[key] + 1)
        self._mark(tok, reads, writes)
        self.last_tok = tok
        if eng in ("act", "dve", "pool"):
            self.last_ew = tok
        self.ninst += 1 + len(waits)

    def dma(self, q, out, in_, reads=(), writes=()):
        if self.dead:
            return
        c = self.rr[q]
        self.rr[q] = (c + 1) % NCH
        key = "D:%s:%d" % (q, c)
        waits = self._deps(q, reads, writes)
        prev = self.val[key]
        if prev > 0 and self.known[q].get(key, 0) < prev:
            waits.append((key, prev))
            self.known[q][key] = prev
        e = self.eng[q]
        for k, v in waits:
            e.wait_ge(self.sem[k], v)
        e.dma_start(out=out, in_=in_).then_inc(self.sem[key], 16)
        self.val[key] = prev + 16
        self._mark((key, prev + 16), reads, writes)
        self.last_tok = (key, prev + 16)
        self.ninst += 1 + len(waits)

    def barrier(self, engines=None):
        for e in (engines or list(self.eng)):
            kn = self.known[e]
            for k, v in self.val.items():
                if v > 0 and kn.get(k, 0) < v:
                    self.eng[e].wait_ge(self.sem[k], v)
                    kn[k] = v
                    self.ninst += 1


def build():
    nc = bass.Bass("TRN2", target_bir_lowering=False)

    def din(name, shape):
        return nc.dram_tensor(name, shape, F32, kind="ExternalInput").ap()

    def dout(name, shape):
        return nc.dram_tensor(name, shape, F32, kind="ExternalOutput").ap()

    def dint(name, shape, dt):
        return nc.dram_tensor(name, shape, dt, kind="Internal").ap()

    xp = din("xp", [T, D])
    xs = din("xs", [TS, D])
    mem = din("mem", [256, D])
    cfk = din("cfk", [2, 2048, 512])
    cfv = din("cfv", [2, 2048, 512])
    clf = din("clf", [2, 2048, 8])
    csk = din("csk", [2, 2048, 512])
    csv = din("csv", [2, 2048, 512])
    cmk = din("cmk", [2, 256, 1024])
    cmv = din("cmv", [2, 256, 1024])
    wshapes = {"w_in": [1024, 3080], "w_out": [1024, 1024], "w_mq": [1024, 1024], "w_mk": [1024, 1024],
               "w_mv": [1024, 1024], "w_mo": [1024, 1024], "w_ff1": [1024, 4096], "w_ff2": [4096, 1024]}
    W32 = {n: din(n, [2] + s) for n, s in wshapes.items()}
    gb = din("gb", [2, 128, NGB])
    gc = din("gc", [2, 128, NGC])
    cst = din("cst", [128, NCST])

    yp = dout("yp", [T, D])
    ys = dout("ys", [16, D])
    p_fk = dout("p_fk", [2, T, 512])
    p_fv = dout("p_fv", [2, T, 512])
    p_lf = dout("p_lf", [2, T, 8])
    p_sk = dout("p_sk", [2, T, 512])
    p_sv = dout("p_sv", [2, T, 512])
    p_mk = dout("p_mk", [2, 256, 1024])
    p_mv = dout("p_mv", [2, 256, 1024])
    s_fk = dout("s_fk", [2, 16, 512])
    s_fv = dout("s_fv", [2, 16, 512])
    s_lf = dout("s_lf", [2, 16, 8])
    s_sk = dout("s_sk", [2, 16, 512])
    s_sv = dout("s_sv", [2, 16, 512])

    XT = dint("XT", [128, 8, TT], F32)
    HTd = dint("HTd", [128, 8, TT], BF16)
    OT = dint("OT", [TT, 1024], F32)
    WB = {n: dint("b_" + n, [2] + s, BF16) for n, s in wshapes.items()}

    with ExitStack() as stack:
        S = Sched(nc, stack)

        uniq = [0]

        def sb(st, name, shape, dt):
            uniq[0] += 1
            nm = "%s_%d" % (name, uniq[0])
            b = Buf(st.enter_context(nc.sbuf_tensor(nm, shape, dt)))
            if os.environ.get("KADDR"):
                ml = nc.lookup_mloc(nm)
                print("ADDR", nm, ml.addr, ml.addr + int(np.prod(shape[1:])) * (4 if dt == F32 else 2))
            return b

        def ps(name, shape, dt):
            b = Buf(stack.enter_context(nc.psum_tensor(name, shape, dt)))
            b.px = True
            return b

        XTk = [Buf(None) for _ in range(NQB)]
        HTk = [Buf(None) for _ in range(NQB)]
        OTk = [Buf(None) for _ in range(NQB)]
        WBk = {n: [[] for _ in range(2)] for n in wshapes}

        CF = sb(stack, "CF", [128, 640], F32)
        CB = sb(stack, "CB", [128, NCST], BF16)
        ONESB = sb(stack, "ONESB", [128, 128], BF16)
        ONESF = sb(stack, "ONESF", [128, 512], F32)
        EPSC = sb(stack, "EPSC", [128, 1], F32)
        ONEC = sb(stack, "ONEC", [128, 1], F32)
        GB = sb(stack, "GB", [128, NGB], F32)
        GC = sb(stack, "GC", [128, NGC], F32)
        JUNK = sb(stack, "JUNK", [128, 1024], F32)
        XTOK = Rot([sb(stack, "xtok%d" % i, [128, 1024], F32) for i in range(2)])
        HB = Rot([sb(stack, "hb%d" % i, [128, 1024], BF16) for i in range(2)])
        XTT = Rot([sb(stack, "xtt%d" % i, [128, 8, 128], F32) for i in range(2)])
        HTT = Rot([sb(stack, "htt%d" % i, [128, 8, 128], BF16) for i in range(2)])
        SMALL = Rot([sb(stack, "small%d" % i, [128, 16], F32) for i in range(8)])

        PZ = Rot([ps("pz%d" % i, [128, 512], F32) for i in range(2)])
        PO = Rot([ps("po%d" % i, [128, 512], F32) for i in range(2)])
        PP = Rot([ps("pp%d" % i, [128, 512], F32) for i in range(2)])
        PT = Rot([ps("pt%d" % i, [128, 1024], BF16) for i in range(2)])
        P4 = Rot(PZ.bufs + PO.bufs)
        ZB = Rot(PZ.bufs + PP.bufs)
        P6 = Rot(PZ.bufs + PO.bufs + PP.bufs)

        IDF = lambda: CF[:, 0:128]
        TRIU = lambda: CF[:, 128:256]
        SEL127 = lambda: CF[:, 256:384]
        IDB = lambda: CB[:, 0:128]
        NEGI = lambda: CB[:, 384:512]
        NEGS = lambda: CB[:, 512:640]
        SELT = lambda h: CB[0:24, 640 + h * 128:640 + (h + 1) * 128]

        tog = [0]

        def evac_eng():
            tog[0] ^= 1
            return "act" if tog[0] else "dve"

        def copy(eng, out, in_, reads, writes):
            if eng == "act":
                S.op("act", lambda e: e.activation(out=out, in_=in_, func=AF.Copy), reads, writes)
            elif eng == "dve":
                S.op("dve", lambda e: e.tensor_copy(out=out, in_=in_), reads, writes)
            else:
                S.op("pool", lambda e: e.tensor_copy(out=out, in_=in_), reads, writes)

        def rstd_from_ss(ss, g0, g1, n):
            ms = SMALL.next()
            S.op("dve", lambda e: e.tensor_scalar(out=ms[:, g0:g1], in0=ss[:, g0:g1], scalar1=1.0 / n, scalar2=EPS,
                                                  op0=ALU.mult, op1=ALU.add), [ss], [ms])
            ln = SMALL.next()
            S.op("act", lambda e: e.activation(out=ln[:, g0:g1], in_=ms[:, g0:g1], func=AF.Ln), [ms], [ln])
            rs = SMALL.next()
            S.op("act", lambda e: e.activation(out=rs[:, g0:g1], in_=ln[:, g0:g1], func=AF.Exp, scale=-0.5), [ln], [rs])
            return rs

        st0 = ExitStack()
        CST32 = sb(st0, "CST32", [128, NCST], F32)
        STG = Rot([sb(st0, "stg%d" % i, [128, 4096], F32) for i in range(3)])
        STB = Rot([sb(st0, "stb%d" % i, [128, 4096], BF16) for i in range(3)])
        S.dma("sp", CF[:, :], cst[:, 0:640], [], [CF])
        S.dma("sp", CST32[:, :], cst[:, :], [], [CST32])
        copy("pool", CB[:, :], CST32[:, :], [CST32], [CB])
        S.op("dve", lambda e: e.memset(ONESB[:, :], 1.0), [], [ONESB])
        S.op("dve", lambda e: e.memset(ONESF[:, :], 1.0), [], [ONESF])
        S.op("dve", lambda e: e.memset(EPSC[:, :], EPS), [], [EPSC])
        S.op("dve", lambda e: e.memset(ONEC[:, :], 1.0), [], [ONEC])

        def load_gains(l):
            S.dma("sp", GB[:, :], gb[l], [], [GB])
            S.dma("sp", GC[:, :], gc[l], [], [GC])
            S.op("dve", lambda e: e.tensor_scalar(out=GB[:, 3072:3136], in0=GB[:, 3072:3136], scalar1=0.125, scalar2=None,
                                                  op0=ALU.mult), [GB], [GB])
            S.op("dve", lambda e: e.tensor_scalar(out=GC[:, 24:26], in0=GC[:, 24:26], scalar1=1.0 / 16.0, scalar2=None,
                                                  op0=ALU.mult), [GC], [GC])

        conv_jobs = []
        for l_ in range(2):
            for n in ["w_in", "w_out", "w_mq", "w_mk", "w_mv", "w_mo", "w_ff1", "w_ff2"]:
                K_, N_ = wshapes[n]
                F_ = K_ * N_ // 128
                src = W32[n][l_].rearrange("k n -> (k n)").rearrange("(p f) -> p f", p=128)
                dst = WB[n][l_].rearrange("k n -> (k n)").rearrange("(p f) -> p f", p=128)
                for f0 in range(0, F_, 4096):
                    f1 = min(F_, f0 + 4096)
                    conv_jobs.append((n, l_, src[:, f0:f1], dst[:, f0:f1], f1 - f0))

        def convert_some(k):
            for _ in range(k):
                if not conv_jobs:
                    return
                n, l_, src, dst, w = conv_jobs.pop(0)
                a = STG.next()
                b = STB.next()
                S.dma("sp", a[:, 0:w], src, [], [a])
                copy("pool", b[:, 0:w], a[:, 0:w], [a], [b])
                kk = Buf(None)
                S.dma("sp", dst, b[:, 0:w], [b], [kk])
                WBk[n][l_].append(kk)

        S.ck("pre0")
        load_gains(0)
        S.ck("pre1")
        convert_some(7)
        S.ck("pre2")

        def to_feature_major_f32(src_buf, blk):
            xtt = XTT.next()
            for half in range(2):
                bank = P4.next()
                for c in range(4):
                    cc = half * 4 + c
                    S.op("pe", lambda e, c=c, cc=cc: e.transpose(bank[:, c * 128:(c + 1) * 128], src_buf[:, cc * 128:(cc + 1) * 128], IDF()),
                         [src_buf, CF], [bank], signal=(c == 3))
                copy(evac_eng(), xtt[:, half * 4:(half + 1) * 4, :], bank[:, 0:512].rearrange("p (c t) -> p c t", c=4), [bank], [xtt])
            S.dma("sp", XT[:, :, blk * 128:(blk + 1) * 128], xtt[:, :, :], [xtt], [XTk[blk]])

        def tok_to_hT(hb, blk, dst, dstk):
            bank = PT.next()
            for c in range(8):
                S.op("pe", lambda e, c=c: e.transpose(bank[:, c * 128:(c + 1) * 128], hb[:, c * 128:(c + 1) * 128], IDB()),
                     [hb, CB], [bank], signal=(c == 7))
            S.ck("p0b1")
            htt = HTT.next()
            copy(evac_eng(), htt[:, :, :], bank[:, 0:1024].rearrange("p (c t) -> p c t", c=8), [bank], [htt])
            S.ck("p0b2")
            S.dma("sp", dst[:, :, blk * 128:(blk + 1) * 128], htt[:, :, :], [htt], [dstk])

        for blk in range(NQB):
            xt = XTOK.next()
            src = xp[blk * 128:(blk + 1) * 128, :] if blk < 32 else xs[:, :]
            S.dma("sp", xt[:, :], src, [], [xt])
            SKIP = os.environ.get("KSKIP", "")
            if "a" not in SKIP:
                to_feature_major_f32(xt, blk)
            S.ck("p0a")
            if "b" in SKIP:
                continue
            ss = SMALL.next()
            S.op("act", lambda e: e.activation(out=JUNK[:, :], in_=xt[:, :], func=AF.Square, accum_out=ss[:, 0:1]), [xt], [JUNK, ss])
            rs = rstd_from_ss(ss, 0, 1, 1024.0)
            hb = HB.next()
            S.op("dve", lambda e: e.scalar_tensor_tensor(out=hb[:, :], in0=xt[:, :], scalar=rs[:, 0:1], in1=GB[:, 0:1024],
                                                         op0=ALU.mult, op1=ALU.mult), [xt, rs, GB], [hb])
            S.ck("p0b")
            tok_to_hT(hb, blk, HTd, HTk[blk])
            S.ck("p0c")
            S.ck("p0c_%d" % blk)
            convert_some(2)
        convert_some(1000)
        S.barrier()
        st0.close()


        if os.environ.get("KDBG"):
            dbb = HB.next()
            dbf = XTOK.next()
            for j, blk_ in enumerate((6, 5, 7, 12)):
                S.dma("sp", yp[j * 256:j * 256 + 128, :].rearrange("p (c t) -> p c t", c=8), XT[:, :, blk_ * 128:(blk_ + 1) * 128], [XTk[blk_]], [])
                S.dma("sp", dbb[:, :].rearrange("p (c t) -> p c t", c=8), HTd[:, :, blk_ * 128:(blk_ + 1) * 128], [HTk[blk_]], [dbb])
                S.op("dve", lambda e: e.tensor_copy(out=dbf[:, :], in_=dbb[:, :]), [dbb], [dbf])
                S.dma("sp", yp[j * 256 + 128:j * 256 + 256, :], dbf[:, :], [dbf], [])
        S.ck("p0")
        for l in range(2):
            if l == 1:
                load_gains(1)
            with ExitStack() as st:
                if os.environ.get("KPAD"):
                    PAD = sb(st, "PAD", [128, int(os.environ["KPAD"])], F32)
                WP = sb(st, "WP", [128, 8, 384], BF16)
                WF = sb(st, "WF", [128, 8, 8], BF16)
                HT = Rot([sb(st, "hT%d" % i, [128, 8, 512], BF16) for i in range(2)])
                QTA = sb(st, "QTA", [128, NQB * 128], BF16)
                QTB = sb(st, "QTB", [128, NQB * 128], BF16)
                KT = sb(st, "KT", [128, NKB * 128], BF16)
                V = sb(st, "V", [128, NKB, 2, 66], BF16)
                KBC = sb(st, "KBC", [128, 16, 128], BF16)
                K32 = sb(st, "K32", [128, 16, 128], F32)
                V32 = sb(st, "V32", [128, 16, 128], F32)
                SQ = Rot([sb(st, "sq%d" % i, [128, 256], F32) for i in range(2)])
                KST = Rot([sb(st, "kst%d" % i, [128, 128], F32) for i in range(3)])
                VST = Rot([sb(st, "vst%d" % i, [128, 128], F32) for i in range(3)])
                QB = Rot([sb(st, "qb%d" % i, [128, 128], BF16) for i in range(3)])
                KB = Rot([sb(st, "kb%d" % i, [128, 128], BF16) for i in range(3)])
                EB = Rot([sb(st, "eb%d" % i, [128, 512], F32) for i in range(3)])
                SPB = Rot([sb(st, "spb%d" % i, [128, 512], F32) for i in range(3)])
                CBF = Rot([sb(st, "cbf%d" % i, [128, 512], F32) for i in range(2)])
                AB = Rot([sb(st, "ab%d" % i, [128, 512], BF16) for i in range(3)])
                ATB = Rot([sb(st, "atb%d" % i, [128, 512], BF16) for i in range(3)])
                OST = Rot([sb(st, "ost%d" % i, [128, 128], F32) for i in range(2)])
                LF = sb(st, "LF", [128, NKB, 8], F32)
                WC = sb(st, "WC", [128, NKB, 8], F32)
                TB = sb(st, "TB", [128, NKB, 8], F32)
                CS = sb(st, "CS", [128, NKB, 8], F32)
                FM = sb(st, "FM", [128, NKB, 8], F32)
                R1 = sb(st, "R1", [128, NKB, 8], F32)
                FS = sb(st, "FS", [128, NKB, 3, 8], BF16)
                FKT = sb(st, "FKT", [128, NKB * 128], BF16)
                FT1 = sb(st, "FT1", [128, NQB * 8], F32)
                FT2 = sb(st, "FT2", [128, NQB * 8], F32)

                S.op("dve", lambda e: e.memset(V[:, :, :, 64:65], 1.0), [], [V])
                S.op("dve", lambda e: e.memset(V[:, :, :, 65:66], 0.0), [], [V])
                S.op("dve", lambda e: e.memset(LF[:, :, :], 0.0), [], [LF])
                S.op("pool", lambda e: e.memset(QTA[:, :], 0.0), [], [QTA])
                S.op("pool", lambda e: e.memset(QTB[:, :], 0.0), [], [QTB])
                S.op("pool", lambda e: e.memset(FKT[:, :], 0.0), [], [FKT])

                for p in range(8):
                    fox = p < 4
                    pp = p % 4
                    base = 0 if fox else 1544
                    for j in range(3):
                        c0 = base + j * 512 + pp * 128
                        S.dma("sp", WP[:, :, j * 128:(j + 1) * 128],
                              WB["w_in"][l].rearrange("(c p) n -> p c n", p=128)[:, :, c0:c0 + 128], WBk["w_in"][l], [WP])
                    if p == 0:
                        S.dma("sp", WF[:, :, :], WB["w_in"][l].rearrange("(c p) n -> p c n", p=128)[:, :, 1536:1544],
                              WBk["w_in"][l], [WF])
                    ck, cv = (cfk, cfv) if fox else (csk, csv)
                    S.dma("sp", K32[:, :, :], ck[l].rearrange("(b p) n -> p b n", p=128)[:, :, pp * 128:(pp + 1) * 128], [], [K32])
                    copy("pool", KBC[:, :, :], K32[:, :, :], [K32], [KBC])
                    S.dma("sp", V32[:, :, :], cv[l].rearrange("(b p) n -> p b n", p=128)[:, :, pp * 128:(pp + 1) * 128], [], [V32])
                    copy("pool", V[:, 32:48, :, 0:64], V32[:, :, :].rearrange("p b (h d) -> p b h d", h=2), [V32], [V])
                    if p == 0:
                        S.dma("sp", LF[:, 32:48, :], clf[l].rearrange("(b p) h -> p b h", p=128), [], [LF])
                    S.ck("pj_a")
                    FL = PO.bufs[1]
                    ok_out, ov_out = (p_fk, p_fv) if fox else (p_sk, p_sv)
                    sk_out, sv_out = (s_fk, s_fv) if fox else (s_sk, s_sv)
                    for ti in range(9):
                        W_ = 512 if ti < 8 else 128
                        c0 = ti * 512
                        hT = HT.next()
                        S.dma("sp", hT[:, :, 0:W_], HTd[:, :, c0:c0 + W_], HTk[ti * 4:ti * 4 + W_ // 128], [hT])
                        bq = PT.next()
                        bk = PT.next()
                        for tb in range(W_ // 128):
                            blk = ti * 4 + tb
                            kblk = blk if blk < 32 else 48
                            bank = PP.next()
                            for c in range(8):
                                S.op("pe", lambda e, c=c: e.matmul(bank[:, 0:384], hT[:, c, tb * 128:(tb + 1) * 128], WP[:, c, :],
                                                                   start=(c == 0), stop=(c == 7)), [hT, WP], [bank], signal=(c == 7))
                            if p == 0 and "f" not in os.environ.get("KSKIP", ""):
                                for c in range(8):
                                    S.op("pe", lambda e, c=c: e.matmul(FL[:, blk * 8:(blk + 1) * 8], hT[:, c, tb * 128:(tb + 1) * 128], WF[:, c, :],
                                                                       start=(c == 0), stop=(c == 7)), [hT, WF], [FL], signal=(c == 7))
                            S.ck("pj_b")
                            S.ck("pj_b%d" % blk)
                            kst = KST.next()
                            vst = VST.next()
                            qb = QB.next()
                            kb = KB.next()
                            if fox:
                                sq = SQ.next()
                                S.op("act", lambda e: e.activation(out=sq[:, :], in_=bank[:, 0:256], func=AF.Square), [bank], [sq])
                                ss = SMALL.next()
                                S.op("dve", lambda e: e.tensor_reduce(out=ss[:, 0:4], in_=sq[:, :].rearrange("p (g d) -> p g d", g=4),
                                                                      axis=AX.X, op=ALU.add), [sq], [ss])
                                rs = rstd_from_ss(ss, 0, 4, 64.0)
                                for hh in range(2):
                                    S.op("dve", lambda e, hh=hh: e.scalar_tensor_tensor(
                                        out=qb[:, hh * 64:(hh + 1) * 64], in0=bank[:, hh * 64:(hh + 1) * 64], scalar=rs[:, hh:hh + 1],
                                        in1=GB[:, 3072:3136], op0=ALU.mult, op1=ALU.mult), [bank, rs, GB], [qb])
                                    S.op("dve", lambda e, hh=hh: e.scalar_tensor_tensor(
                                        out=kst[:, hh * 64:(hh + 1) * 64], in0=bank[:, 128 + hh * 64:128 + (hh + 1) * 64], scalar=rs[:, 2 + hh:3 + hh],
                                        in1=GB[:, 3136:3200], op0=ALU.mult, op1=ALU.mult), [bank, rs, GB], [kst])
                                copy("pool", kb[:, :], kst[:, :], [kst], [kb])
                            else:
                                S.op("act", lambda e: e.activation(out=qb[:, :], in_=bank[:, 0:128], func=AF.Copy, scale=0.125), [bank], [qb])
                                S.op("act", lambda e: e.activation(out=kst[:, :], in_=bank[:, 128:256], func=AF.Copy), [bank], [kst])
                                S.op("dve", lambda e: e.tensor_copy(out=kb[:, :], in_=bank[:, 128:256]), [bank], [kb])
                            S.op("act", lambda e: e.activation(out=vst[:, :], in_=bank[:, 256:384], func=AF.Copy), [bank], [vst])
                            S.op("dve", lambda e: e.tensor_copy(out=V[:, kblk, :, 0:64], in_=bank[:, 256:384].rearrange("p (h d) -> p h d", h=2)),
                                 [bank], [V])
                            S.ck("pj_c")
                            S.ck("pj_c%d" % blk)
                            if blk < 32:
                                S.dma("sp", ok_out[l, blk * 128:(blk + 1) * 128, pp * 128:(pp + 1) * 128], kst[:, :], [kst], [])
                                S.dma("sp", ov_out[l, blk * 128:(blk + 1) * 128, pp * 128:(pp + 1) * 128], vst[:, :], [vst], [])
                            else:
                                S.dma("sp", sk_out[l, :, pp * 128:(pp + 1) * 128], kst[0:16, :], [kst], [])
                                S.dma("sp", sv_out[l, :, pp * 128:(pp + 1) * 128], vst[0:16, :], [vst], [])
                            S.ck("pj_g%d" % blk)
                            S.op("pe", lambda e: e.transpose(bq[:, tb * 128:(tb + 1) * 128], qb[:, :], IDB()), [qb, CB], [bq])
                            S.op("pe", lambda e: e.transpose(bk[:, tb * 128:(tb + 1) * 128], kb[:, :], IDB()), [kb, CB], [bk])
                            S.ck("pj_h%d" % blk)
                        S.ck("pj_d")
                        S.ck("pj_d%d" % ti)
                        kc0 = c0 if ti < 8 else 48 * 128
                        copy("act", QTA[0:64, c0:c0 + W_], bq[0:64, 0:W_], [bq], [QTA])
                        copy("act", QTB[64:128, c0:c0 + W_], bq[64:128, 0:W_], [bq], [QTB])
                        copy("dve", KT[:, kc0:kc0 + W_], bk[:, 0:W_], [bk], [KT])
                        S.ck("pj_f%d" % ti)
                    S.ck("pj_e")
                    for g in range(4):
                        bk = PT.next()
                        for i in range(4):
                            S.op("pe", lambda e, i=i: e.transpose(bk[:, i * 128:(i + 1) * 128], KBC[:, g * 4 + i, :], IDB()), [KBC, CB], [bk], signal=(i == 3))
                        copy(evac_eng(), KT[:, (32 + g * 4) * 128:(36 + g * 4) * 128], bk[:, 0:512], [bk], [KT])

                    S.ck("proj%d_%d" % (l, p))
                    if p == 0:
                        NB8 = NQB * 8
                        S.op("dve", lambda e: e.tensor_tensor(out=FT1[:, :].rearrange("p (b h) -> p b h", h=8),
                                                              in0=FL[:, 0:NB8].rearrange("p (b h) -> p b h", h=8),
                                                              in1=GB[:, 3456:3464].unsqueeze(1).broadcast_to([128, NQB, 8]), op=ALU.add),
                             [FL, GB], [FT1])
                        S.op("act", lambda e: e.activation(out=FT2[:, :], in_=FT1[:, :], func=AF.Exp, scale=-1.0), [FT1], [FT2])
                        S.op("act", lambda e: e.activation(out=FT1[:, :], in_=FT2[:, :], func=AF.Ln, bias=ONEC[:, 0:1]), [FT2, ONEC], [FT1])
                        S.op("dve", lambda e: e.tensor_scalar(out=LF[:, 0:32, :], in0=FT1[:, 0:256].rearrange("p (b h) -> p b h", h=8),
                                                              scalar1=-1.0, scalar2=None, op0=ALU.mult), [FT1], [LF])
                        S.op("dve", lambda e: e.tensor_scalar(out=LF[:, 48, :], in0=FT1[:, 256:264], scalar1=-1.0, scalar2=None, op0=ALU.mult),
                             [FT1], [LF])
                        for q4 in range(4):
                            S.dma("sp", p_lf[l].rearrange("(b p) h -> p b h", p=128)[:, q4 * 8:(q4 + 1) * 8, :], LF[:, q4 * 8:(q4 + 1) * 8, :], [LF], [])
                        S.dma("sp", s_lf[l], LF[0:16, 48, :], [LF], [])
                        bank = PP.next()
                        S.op("pe", lambda e: e.matmul(bank[:, 0:NKB * 8], TRIU(), LF[:, :, :].rearrange("p b h -> p (b h)"), start=True, stop=True),
                             [CF, LF], [bank])
                        S.op("dve", lambda e: e.tensor_copy(out=WC[:, :, :].rearrange("p b h -> p (b h)"), in_=bank[:, 0:NKB * 8]), [bank], [WC])
                        bank2 = PP.next()
                        S.op("pe", lambda e: e.matmul(bank2[:, 0:NKB * 8], SEL127(), WC[:, :, :].rearrange("p b h -> p (b h)"), start=True, stop=True),
                             [CF, WC], [bank2])
                        S.op("act", lambda e: e.activation(out=TB[:, :, :].rearrange("p b h -> p (b h)"), in_=bank2[:, 0:NKB * 8], func=AF.Copy),
                             [bank2], [TB])
                        for (a, b_) in ((0, 32), (32, 49)):
                            for h in range(8):
                                S.op("dve", lambda e, h=h: e.tensor_tensor_scan(out=CS[:, a:b_, h], data0=ONESF[:, 0:b_ - a], data1=TB[:, a:b_, h],
                                                                                initial=0.0, op0=ALU.mult, op1=ALU.add), [TB, ONESF], [CS])
                        S.op("dve", lambda e: e.tensor_tensor(out=FM[:, :, :], in0=WC[:, :, :], in1=CS[:, :, :], op=ALU.add), [WC, CS], [FM])
                        S.op("dve", lambda e: e.tensor_tensor(out=FM[:, :, :], in0=FM[:, :, :], in1=TB[:, :, :], op=ALU.subtract), [FM, TB], [FM])
                        S.op("dve", lambda e: e.tensor_scalar(out=FS[:, :, 0, :], in0=FM[:, :, :], scalar1=-1.0, scalar2=None, op0=ALU.mult), [FM], [FS])
                        S.op("dve", lambda e: e.scalar_tensor_tensor(out=R1[:, :, :], in0=FM[:, :, :], scalar=-1.0, in1=FS[:, :, 0, :],
                                                                     op0=ALU.mult, op1=ALU.subtract), [FM, FS], [R1])
                        S.op("dve", lambda e: e.tensor_copy(out=FS[:, :, 1, :], in_=R1[:, :, :]), [R1], [FS])
                        S.op("dve", lambda e: e.tensor_tensor(out=R1[:, :, :], in0=R1[:, :, :], in1=FS[:, :, 1, :], op=ALU.subtract), [R1, FS], [R1])
                        S.op("dve", lambda e: e.tensor_copy(out=FS[:, :, 2, :], in_=R1[:, :, :]), [R1], [FS])
                        for g in range(7):
                            n_ = min(8, NKB - g * 8)
                            bk = PT.next()
                            for i in range(n_):
                                S.op("pe", lambda e, i=i: e.transpose(bk[0:24, i * 128:(i + 1) * 128],
                                                                      FS[:, g * 8 + i, :, :].rearrange("p s h -> p (s h)"), IDB()),
                                     [FS, CB], [bk], signal=(i == n_ - 1))
                            copy(evac_eng(), FKT[0:24, g * 1024:g * 1024 + n_ * 128], bk[0:24, 0:n_ * 128], [bk], [FKT])

                    S.ck("f%d_%d" % (l, p))
                    chunks = []
                    for qblk in range(NQB):
                        kbs = list(range(0, qblk + 1)) if qblk < 32 else list(range(32, 49))
                        groups = [kbs[i:i + 4] for i in range(0, len(kbs), 4)]
                        for e_ in range(2):
                            for gi in range(len(groups) - 1, -1, -1):
                                chunks.append(dict(q=qblk, e=e_, kbs=groups[gi], diag=(gi == len(groups) - 1),
                                                   first=(gi == len(groups) - 1), last=(gi == 0), own=kbs[-1]))
                    state = {}

                    def stage_z(ch):
                        z = ZB.next()
                        ch["z"] = z
                        P0 = 64 * ch["e"]
                        w = 128 * len(ch["kbs"])
                        ch["w"] = w
                        q0 = ch["q"] * 128
                        k0 = ch["kbs"][0] * 128
                        more = fox or ch["diag"]
                        QTe = QTA if ch["e"] == 0 else QTB
                        S.op("pe", lambda e: e.matmul(z[:, 0:w], QTe[:, q0:q0 + 128], KT[:, k0:k0 + w], start=True, stop=not more),
                             [QTe, KT], [z], signal=not more)
                        if fox:
                            h = 2 * pp + ch["e"]
                            S.op("pe", lambda e: e.matmul(z[:, 0:w], CB[:, 640 + h * 128:640 + (h + 1) * 128], FKT[:, k0:k0 + w], start=False, stop=not ch["diag"]),
                                 [CB, FKT], [z], signal=not ch["diag"])
                        if ch["diag"]:
                            S.op("pe", lambda e: e.matmul(z[:, w - 128:w], IDB(), NEGI() if fox else NEGS(), start=False, stop=True), [CB], [z])

                    def stage_e1(ch):
                        if fox:
                            return
                        z = ch["z"]
                        w = ch["w"]
                        eb = EB.next()
                        ch["eb"] = eb
                        S.op("act", lambda e: e.activation(out=eb[:, 0:w], in_=z[:, 0:w], func=AF.Exp), [z], [eb])

                    def stage_e1b(ch):
                        if fox:
                            return
                        w = ch["w"]
                        eb = ch["eb"]
                        sp = SPB.next()
                        ch["sp"] = sp
                        S.op("act", lambda e: e.activation(out=sp[:, 0:w], in_=eb[:, 0:w], func=AF.Ln, bias=ONEC[:, 0:1]), [eb, ONEC], [sp])

                    def stage_e2a(ch):
                        if fox:
                            return
                        z = ch["z"]
                        w = ch["w"]
                        eb = ch["eb"]
                        sp = ch["sp"]
                        c = CBF.next()
                        if ch["first"]:
                            S.op("dve", lambda e: e.tensor_tensor_scan(out=c[:, 0:w][:, ::-1], data0=ONESF[:, 0:w], data1=sp[:, 0:w][:, ::-1],
                                                                       initial=0.0, op0=ALU.mult, op1=ALU.add), [sp, ONESF], [c])
                        else:
                            pc = state["prevc"]
                            S.op("dve", lambda e: e.tensor_tensor_scan(out=c[:, 0:w][:, ::-1], data0=ONESF[:, 0:w], data1=sp[:, 0:w][:, ::-1],
                                                                       initial=pc[:, 0:1], op0=ALU.mult, op1=ALU.add), [sp, ONESF, pc], [c])
                        state["prevc"] = c
                        S.op("dve", lambda e: e.tensor_tensor(out=eb[:, 0:w], in0=z[:, 0:w], in1=c[:, 0:w], op=ALU.subtract), [z, c], [eb])

                    def stage_e2b(ch):
                        z = ch["z"]
                        w = ch["w"]
                        a = AB.next()
                        ch["a"] = a
                        if fox:
                            h = 2 * pp + ch["e"]
                            S.op("act", lambda e: e.activation(out=a[:, 0:w], in_=z[:, 0:w], func=AF.Exp, bias=FM[:, ch["own"], h:h + 1]),
                                 [z, FM], [a])
                        else:
                            eb = ch["eb"]
                            S.op("act", lambda e: e.activation(out=a[:, 0:w], in_=eb[:, 0:w], func=AF.Exp), [eb], [a])

                    def stage_pv(ch):
                        a = ch["a"]
                        w = ch["w"]
                        nb = len(ch["kbs"])
                        bt = PT.next()
                        btf = bt[:, :].bitcast(F32)
                        for i in range(nb):
                            S.op("pe", lambda e, i=i: e.matmul(btf[:, i * 128:(i + 1) * 128], a[:, i * 128:(i + 1) * 128], IDB(), start=True, stop=True),
                                 [a, CB], [bt], signal=(i == nb - 1))
                        ch["bt"] = bt

                    def stage_ev(ch):
                        bt = ch["bt"]
                        w = ch["w"]
                        at = ATB.next()
                        ch["at"] = at
                        copy("dve" if fox else evac_eng(), at[:, 0:w], bt[:, :].bitcast(F32)[:, 0:w], [bt], [at])

                    def stage_pvm(ch):
                        at = ch["at"]
                        w = ch["w"]
                        nb = len(ch["kbs"])
                        if ch["first"]:
                            state["o"] = PO.next()
                        o = state["o"]
                        for i in range(nb):
                            kb_ = ch["kbs"][i]
                            lastmm = ch["last"] and i == nb - 1
                            S.op("pe", lambda e, i=i, kb_=kb_: e.matmul(o[:, 0:66], at[:, i * 128:(i + 1) * 128], V[:, kb_, ch["e"], :],
                                                                        start=(ch["first"] and i == 0), stop=lastmm),
                                 [at, V], [o], signal=(i == nb - 1))
                        if ch["last"]:
                            if ch["e"] == 0:
                                state["ost"] = OST.next()
                            ost = state["ost"]
                            e_ = ch["e"]
                            if fox:
                                rc = SMALL.next()
                                S.op("dve", lambda e: e.reciprocal(out=rc[:, 0:1], in_=o[:, 64:65]), [o], [rc])
                                S.op("dve", lambda e: e.tensor_scalar(out=ost[:, e_ * 64:(e_ + 1) * 64], in0=o[:, 0:64], scalar1=rc[:, 0:1], scalar2=None,
                                                                      op0=ALU.mult), [o, rc], [ost])
                            else:
                                copy("act", ost[:, e_ * 64:(e_ + 1) * 64], o[:, 0:64], [o], [ost])
                            if e_ == 1:
                                col0 = (0 if fox else 512) + pp * 128
                                qb_ = ch["q"]
                                S.dma("sp", OT[qb_ * 128:(qb_ + 1) * 128, col0:col0 + 128], ost[:, :], [ost], [OTk[qb_]])

                    n = len(chunks)
                    stage_z(chunks[0])
                    stage_z(chunks[1])
                    stage_e1(chunks[0])
                    stage_e1b(chunks[0])
                    for i in range(n + 3):
                        if 2 <= i <= n + 1:
                            stage_pv(chunks[i - 2])
                        if i + 2 < n:
                            stage_z(chunks[i + 2])
                        if i + 1 < n:
                            stage_e1(chunks[i + 1])
                        if 2 <= i <= n + 1:
                            stage_ev(chunks[i - 2])
                        if i + 1 < n:
                            stage_e1b(chunks[i + 1])
                        if i < n:
                            stage_e2a(chunks[i])
                        if 1 <= i <= n:
                            stage_e2b(chunks[i - 1])
                        if 3 <= i <= n + 2:
                            stage_pvm(chunks[i - 3])
                    S.ck("att%d_%d" % (l, p))
                S.barrier()

            with ExitStack() as st:
                XTL = sb(st, "XTL", [128, 8, 512], F32)
                ACTA = sb(st, "ACTA", [128, 8, 512], BF16)
                ACTB = sb(st, "ACTB", [128, 8, 512], BF16)
                QM = sb(st, "QM", [128, 8, 512], F32)
                SQD = sb(st, "SQD", [128, 8, 512], BF16)
                RS = Rot([sb(st, "rs%d" % i, [128, 512], F32) for i in range(2)])
                PTB = sb(st, "PTB", [128, 2, 512], BF16)
                HID = sb(st, "HID", [128, 16, 512], BF16)
                RL = Rot([sb(st, "rl%d" % i, [128, 512], F32) for i in range(2)])
                WBLK = Rot([sb(st, "wblk%d" % i, [128, 8, 512], BF16) for i in range(4)])
                MKT = [sb(st, "MKT%d" % i, [128, 8, 256], BF16) for i in range(2)]
                MV = [sb(st, "MV%d" % i, [128, 2, 1024], BF16) for i in range(2)]
                MST = Rot([sb(st, "mst%d" % i, [128, 1024], F32) for i in range(2)])
                MB = Rot([sb(st, "mb%d" % i, [128, 1024], BF16) for i in range(2)])
                MT = sb(st, "MT", [128, 8, 256], BF16)

                def load_wblk(name, k0, n0):
                    wb = WBLK.next()
                    S.dma("sp", wb[:, :, :], WB[name][l].rearrange("(c p) n -> p c n", p=128)[:, k0:k0 + 8, n0:n0 + 512], WBk[name][l], [wb])
                    return wb

                for mb_ in range(2):
                    mt = XTOK.next()
                    S.dma("sp", mt[:, :], mem[mb_ * 128:(mb_ + 1) * 128, :], [], [mt])
                    ss = SMALL.next()
                    S.op("act", lambda e: e.activation(out=JUNK[:, :], in_=mt[:, :], func=AF.Square, accum_out=ss[:, 0:1]), [mt], [JUNK, ss])
                    rs = rstd_from_ss(ss, 0, 1, 1024.0)
                    hb = HB.next()
                    S.op("dve", lambda e: e.scalar_tensor_tensor(out=hb[:, :], in0=mt[:, :], scalar=rs[:, 0:1], in1=GB[:, 1024:2048],
                                                                 op0=ALU.mult, op1=ALU.mult), [mt, rs, GB], [hb])
                    bank = PT.next()
                    for c in range(8):
                        S.op("pe", lambda e, c=c: e.transpose(bank[:, c * 128:(c + 1) * 128], hb[:, c * 128:(c + 1) * 128], IDB()),
                             [hb, CB], [bank], signal=(c == 7))
                    copy(evac_eng(), MT[:, :, mb_ * 128:(mb_ + 1) * 128], bank[:, 0:1024].rearrange("p (c t) -> p c t", c=8), [bank], [MT])
                for which in ("w_mk", "w_mv"):
                    for mb_ in range(2):
                        stg = MST.next()
                        for ng in range(2):
                            wb = load_wblk(which, 0, ng * 512)
                            bank = P4.next()
                            for c in range(8):
                                S.op("pe", lambda e: e.matmul(bank[:, :], MT[:, c, mb_ * 128:(mb_ + 1) * 128], wb[:, c, :], start=(c == 0), stop=(c == 7)),
                                     [MT, wb], [bank], signal=(c == 7))
                            if which == "w_mk":
                                ss = SMALL.next()
                                for hh in range(2):
                                    S.op("act", lambda e: e.activation(out=JUNK[:, 0:256], in_=bank[:, hh * 256:(hh + 1) * 256], func=AF.Square,
                                                                       accum_out=ss[:, hh:hh + 1]), [bank], [JUNK, ss])
                                rs = rstd_from_ss(ss, 0, 2, 256.0)
                                for hh in range(2):
                                    S.op("dve", lambda e: e.scalar_tensor_tensor(
                                        out=stg[:, ng * 512 + hh * 256:ng * 512 + (hh + 1) * 256], in0=bank[:, hh * 256:(hh + 1) * 256],
                                        scalar=rs[:, hh:hh + 1], in1=GB[:, 3200:3456], op0=ALU.mult, op1=ALU.mult), [bank, rs, GB], [stg])
                            else:
                                copy("act", stg[:, ng * 512:(ng + 1) * 512], bank[:, :], [bank], [stg])
                        dst = p_mk if which == "w_mk" else p_mv
                        S.dma("sp", dst[l, mb_ * 128:(mb_ + 1) * 128, :], stg[:, :], [stg], [])
                        if which == "w_mk":
                            mbb = MB.next()
                            copy("pool", mbb[:, :], stg[:, :], [stg], [mbb])
                            bank2 = PT.next()
                            for c in range(8):
                                S.op("pe", lambda e: e.transpose(bank2[:, c * 128:(c + 1) * 128], mbb[:, c * 128:(c + 1) * 128], IDB()),
                                     [mbb, CB], [bank2], signal=(c == 7))
                            copy(evac_eng(), MKT[0][:, :, mb_ * 128:(mb_ + 1) * 128], bank2[:, 0:1024].rearrange("p (c t) -> p c t", c=8),
                                 [bank2], [MKT[0]])
                        else:
                            copy("pool", MV[0][:, mb_, :], stg[:, :], [stg], [MV[0]])
                for mb_ in range(2):
                    mbb = MB.next()
                    m32 = XTOK.next()
                    S.dma("sp", m32[:, :], cmk[l, mb_ * 128:(mb_ + 1) * 128, :], [], [m32])
                    copy("pool", mbb[:, :], m32[:, :], [m32], [mbb])
                    bank2 = PT.next()
                    for c in range(8):
                        S.op("pe", lambda e, c=c: e.transpose(bank2[:, c * 128:(c + 1) * 128], mbb[:, c * 128:(c + 1) * 128], IDB()),
                             [mbb, CB], [bank2], signal=(c == 7))
                    copy(evac_eng(), MKT[1][:, :, mb_ * 128:(mb_ + 1) * 128], bank2[:, 0:1024].rearrange("p (c t) -> p c t", c=8), [bank2], [MKT[1]])
                    v32 = XTOK.next()
                    S.dma("sp", v32[:, :], cmv[l, mb_ * 128:(mb_ + 1) * 128, :], [], [v32])
                    copy("pool", MV[1][:, mb_, :], v32[:, :], [v32], [MV[1]])

                S.ck("memkv%d" % l)

                def fm_rmsnorm(W_, gcol0, out_buf):
                    for c in range(8):
                        S.op("act", lambda e, c=c: e.activation(out=SQD[:, c, 0:W_], in_=XTL[:, c, 0:W_], func=AF.Square), [XTL], [SQD])
                    bank = PP.next()
                    for c in range(8):
                        S.op("pe", lambda e, c=c: e.matmul(bank[:, 0:W_], ONESB[:, :], SQD[:, c, 0:W_], start=(c == 0), stop=(c == 7)),
                             [ONESB, SQD], [bank], signal=(c == 7))
                    t1 = RS.next()
                    S.op("act", lambda e: e.activation(out=t1[:, 0:W_], in_=bank[:, 0:W_], func=AF.Ln, scale=1.0 / 1024.0, bias=EPSC[:, 0:1]),
                         [bank, EPSC], [t1])
                    r = RS.next()
                    S.op("act", lambda e: e.activation(out=r[:, 0:W_], in_=t1[:, 0:W_], func=AF.Exp, scale=-0.5), [t1], [r])
                    for c in range(8):
                        S.op("dve", lambda e, c=c: e.scalar_tensor_tensor(out=out_buf[:, c, 0:W_], in0=XTL[:, c, 0:W_], scalar=GC[:, gcol0 + c:gcol0 + c + 1],
                                                                          in1=r[:, 0:W_], op0=ALU.mult, op1=ALU.mult), [XTL, GC, r], [out_buf])

                def dense(name, in_buf, kc0, nk8, n0, nm, W_, out_fn, wk0=0):
                    for mg in range(0, nm, 4):
                        if nk8 == 1:
                            wb = load_wblk(name, wk0, n0 + mg * 128)
                            for m in range(4):
                                bank = P6.next()
                                for c in range(8):
                                    S.op("pe", lambda e: e.matmul(bank[:, 0:W_], wb[:, c, m * 128:(m + 1) * 128], in_buf[:, kc0 + c, 0:W_],
                                                                  start=(c == 0), stop=(c == 7)), [wb, in_buf], [bank], signal=(c == 7))
                                out_fn(mg + m, bank)
                            continue
                        banks = [P6.next() for _ in range(4)]
                        for kg in range(nk8):
                            wb = load_wblk(name, wk0 + kg * 8, n0 + mg * 128)
                            for m in range(4):
                                for c in range(8):
                                    first = (kg == 0 and c == 0)
                                    lastk = (kg == nk8 - 1 and c == 7)
                                    S.op("pe", lambda e: e.matmul(
                                        banks[m][:, 0:W_], wb[:, c, m * 128:(m + 1) * 128], in_buf[:, kc0 + kg * 8 + c, 0:W_], start=first, stop=lastk),
                                         [wb, in_buf], [banks[m]], signal=(c == 7))
                        for m in range(4):
                            out_fn(mg + m, banks[m])

                def add_to_x(W_):
                    def f(m, bank):
                        S.op("dve", lambda e: e.tensor_tensor(out=XTL[:, m, 0:W_], in0=bank[:, 0:W_], in1=XTL[:, m, 0:W_], op=ALU.add), [bank, XTL], [XTL])
                    return f

                for ti in range(9):
                    W_ = 512 if ti < 8 else 128
                    c0 = ti * 512
                    si = 0 if ti < 8 else 1
                    nblk = W_ // 128
                    for tb in range(nblk):
                        blk = ti * 4 + tb
                        ot = XTOK.next()
                        S.dma("sp", ot[:, :], OT[blk * 128:(blk + 1) * 128, :], [OTk[blk]], [ot])
                        ss = SMALL.next()
                        for hf in range(2):
                            S.op("act", lambda e, hf=hf: e.activation(out=JUNK[:, 0:512], in_=ot[:, hf * 512:(hf + 1) * 512], func=AF.Square,
                                                                       accum_out=ss[:, hf:hf + 1]), [ot], [JUNK, ss])
                        rs = rstd_from_ss(ss, 0, 2, 512.0)
                        hb = HB.next()
                        for hf in range(2):
                            S.op("dve", lambda e, hf=hf: e.scalar_tensor_tensor(out=hb[:, hf * 512:(hf + 1) * 512], in0=ot[:, hf * 512:(hf + 1) * 512],
                                                                                scalar=rs[:, hf:hf + 1], in1=GB[:, 2048 + hf * 512:2048 + (hf + 1) * 512],
                                                                                op0=ALU.mult, op1=ALU.mult), [ot, rs, GB], [hb])
                        bank = PT.next()
                        for c in range(8):
                            S.op("pe", lambda e, c=c: e.transpose(bank[:, c * 128:(c + 1) * 128], hb[:, c * 128:(c + 1) * 128], IDB()),
                                 [hb, CB], [bank], signal=(c == 7))
                        copy(evac_eng(), ACTA[:, :, tb * 128:(tb + 1) * 128], bank[:, 0:1024].rearrange("p (c t) -> p c t", c=8), [bank], [ACTA])
                    S.ck("c1_%d_%d" % (l, ti))
                    S.dma("sp", XTL[:, :, 0:W_], XT[:, :, c0:c0 + W_], XTk[ti * 4:ti * 4 + nblk], [XTL])
                    dense("w_out", ACTA, 0, 1, 0, 8, W_, add_to_x(W_))
                    S.ck("wout_%d_%d" % (l, ti))
                    fm_rmsnorm(W_, 8, ACTA)

                    def q_out(m, bank):
                        S.op("act", lambda e: e.activation(out=QM[:, m, 0:W_], in_=bank[:, 0:W_], func=AF.Copy), [bank], [QM])
                        S.op("act", lambda e: e.activation(out=SQD[:, m, 0:W_], in_=bank[:, 0:W_], func=AF.Square), [bank], [SQD])
                    dense("w_mq", ACTA, 0, 1, 0, 8, W_, q_out)
                    for hh in range(4):
                        bank = PP.next()
                        for c in range(2):
                            S.op("pe", lambda e, c=c: e.matmul(bank[:, 0:W_], ONESB[:, :], SQD[:, 2 * hh + c, 0:W_], start=(c == 0), stop=(c == 1)),
                                 [ONESB, SQD], [bank], signal=(c == 1))
                        t1 = RS.next()
                        S.op("act", lambda e: e.activation(out=t1[:, 0:W_], in_=bank[:, 0:W_], func=AF.Ln, scale=1.0 / 256.0, bias=EPSC[:, 0:1]),
                             [bank, EPSC], [t1])
                        r = RS.next()
                        S.op("act", lambda e: e.activation(out=r[:, 0:W_], in_=t1[:, 0:W_], func=AF.Exp, scale=-0.5), [t1], [r])
                        for c in range(2):
                            S.op("dve", lambda e, c=c: e.scalar_tensor_tensor(out=ACTB[:, 2 * hh + c, 0:W_], in0=QM[:, 2 * hh + c, 0:W_],
                                                                              scalar=GC[:, 24 + c:25 + c], in1=r[:, 0:W_], op0=ALU.mult, op1=ALU.mult),
                                 [QM, GC, r], [ACTB])
                    for hh in range(4):
                        for mc in range(2):
                            bank = P4.next()
                            for c in range(2):
                                S.op("pe", lambda e, c=c: e.matmul(bank[:, 0:W_], MKT[si][:, 2 * hh + c, mc * 128:(mc + 1) * 128], ACTB[:, 2 * hh + c, 0:W_],
                                                                   start=(c == 0), stop=(c == 1)), [MKT[si], ACTB], [bank], signal=(c == 1))
                            S.op("act", lambda e: e.activation(out=PTB[:, mc, 0:W_], in_=bank[:, 0:W_], func=AF.Exp), [bank], [PTB])
                        bank = PP.next()
                        for mc in range(2):
                            S.op("pe", lambda e, mc=mc: e.matmul(bank[:, 0:W_], ONESB[:, :], PTB[:, mc, 0:W_], start=(mc == 0), stop=(mc == 1)),
                                 [ONESB, PTB], [bank], signal=(mc == 1))
                        t1 = RS.next()
                        S.op("act", lambda e: e.activation(out=t1[:, 0:W_], in_=bank[:, 0:W_], func=AF.Ln), [bank], [t1])
                        rd = RS.next()
                        S.op("act", lambda e: e.activation(out=rd[:, 0:W_], in_=t1[:, 0:W_], func=AF.Exp, scale=-1.0), [t1], [rd])
                        for dc in range(2):
                            bank = P4.next()
                            for mc in range(2):
                                S.op("pe", lambda e, mc=mc: e.matmul(bank[:, 0:W_], MV[si][:, mc, (2 * hh + dc) * 128:(2 * hh + dc + 1) * 128], PTB[:, mc, 0:W_],
                                                                     start=(mc == 0), stop=(mc == 1)), [MV[si], PTB], [bank], signal=(mc == 1))
                            S.op("dve", lambda e: e.tensor_tensor(out=ACTA[:, 2 * hh + dc, 0:W_], in0=bank[:, 0:W_], in1=rd[:, 0:W_], op=ALU.mult),
                                 [bank, rd], [ACTA])
                    dense("w_mo", ACTA, 0, 1, 0, 8, W_, add_to_x(W_))
                    S.ck("cross_%d_%d" % (l, ti))
                    fm_rmsnorm(W_, 16, ACTB)
                    for half in range(2):
                        def h_out(m, bank):
                            rl = RL.next()
                            S.op("act", lambda e: e.activation(out=rl[:, 0:W_], in_=bank[:, 0:W_], func=AF.Relu), [bank], [rl])
                            S.op("pool", lambda e: e.tensor_tensor(out=HID[:, m, 0:W_], in0=rl[:, 0:W_], in1=rl[:, 0:W_], op=ALU.mult), [rl], [HID])
                        dense("w_ff1", ACTB, 0, 1, half * 2048, 16, W_, h_out)
                        dense("w_ff2", HID, 0, 2, 0, 8, W_, add_to_x(W_), wk0=half * 16)
                    S.ck("ffn_%d_%d" % (l, ti))
                    if l == 0:
                        S.dma("sp", XT[:, :, c0:c0 + W_], XTL[:, :, 0:W_], [XTL], XTk[ti * 4:ti * 4 + nblk])
                        fm_rmsnorm(W_, 0, ACTA)
                        S.dma("sp", HTd[:, :, c0:c0 + W_], ACTA[:, :, 0:W_], [ACTA], HTk[ti * 4:ti * 4 + nblk])
                    else:
                        for tb in range(nblk):
                            blk = ti * 4 + tb
                            yt = XTOK.next()
                            for half in range(2):
                                bank = P4.next()
                                for c in range(4):
                                    cc = half * 4 + c
                                    S.op("pe", lambda e, c=c, cc=cc: e.transpose(bank[:, c * 128:(c + 1) * 128], XTL[:, cc, tb * 128:(tb + 1) * 128], IDF()),
                                         [XTL, CF], [bank], signal=(c == 3))
                                copy(evac_eng(), yt[:, half * 512:(half + 1) * 512], bank[:, 0:512], [bank], [yt])
                            if blk < 32:
                                S.dma("sp", yp[blk * 128:(blk + 1) * 128, :], yt[:, :], [yt], [])
                            else:
                                S.dma("sp", ys[:, :], yt[0:16, :], [yt], [])
                    S.ck("tile_%d_%d" % (l, ti))
                S.barrier()

        S.dead = False
        S.barrier(["sp"])
        print("instructions emitted:", S.ninst, "sp dmas:", sum(v for k, v in S.val.items() if k.startswith("D:sp")) // 16,
              {k: v for k, v in S.val.items() if k.startswith("E:")})
    return nc


_NC_CACHE = {}


def _consts():
    c = np.zeros((128, NCST), np.float32)
    idx = np.arange(128)
    c[:, 0:128] = np.eye(128, dtype=np.float32)
    c[:, 128:256] = (idx[:, None] <= idx[None, :]).astype(np.float32)
    c[127, 256:384] = 1.0
    c[:, 384:512] = np.where(idx[None, :] > idx[:, None], NEGM, 0.0)
    c[:, 512:640] = np.where(idx[None, :] >= idx[:, None], NEGM, 0.0)
    for r in range(24):
        h = r % 8
        c[r, 640 + h * 128:640 + (h + 1) * 128] = 1.0
    return c


def kernel(x_prompt, x_sample, mem_prompt, cache_fox_k, cache_fox_v, cache_fox_logf, cache_sb_k, cache_sb_v,
           cache_mem_k, cache_mem_v, g_mix, w_in, b_forget, g_fox_q, g_fox_k, g_out_fox, g_out_sb, w_out,
           g_cross, g_mem, w_mq, w_mk, w_mv, g_mq, g_mk, w_mo, g_ffn, w_ff1, w_ff2):
    f = lambda a: np.ascontiguousarray(np.asarray(a, dtype=np.float32))
    if "nc" not in _NC_CACHE:
        _NC_CACHE["nc"] = build()
    nc = _NC_CACHE["nc"]
    gbp = np.zeros((2, 128, NGB), np.float32)
    gcp = np.zeros((2, 128, NGC), np.float32)
    for l in range(2):
        row = np.concatenate([f(g_mix)[l], f(g_mem)[l], f(g_out_fox)[l], f(g_out_sb)[l], f(g_fox_q)[l], f(g_fox_k)[l],
                              f(g_mk)[l], f(b_forget)[l]])
        gbp[l] = np.broadcast_to(row[None, :], (128, NGB))
        gcp[l, :, 0:8] = f(g_mix)[min(l + 1, 1)].reshape(8, 128).T
        gcp[l, :, 8:16] = f(g_cross)[l].reshape(8, 128).T
        gcp[l, :, 16:24] = f(g_ffn)[l].reshape(8, 128).T
        gcp[l, :, 24:26] = f(g_mq)[l].reshape(2, 128).T
    cst = _consts()
    shared = {"w_in": f(w_in), "w_out": f(w_out), "w_mq": f(w_mq), "w_mk": f(w_mk), "w_mv": f(w_mv), "w_mo": f(w_mo),
              "w_ff1": f(w_ff1), "w_ff2": f(w_ff2), "gb": gbp, "gc": gcp, "cst": cst}
    in_maps = []
    for b in range(8):
        xs_pad = np.zeros((TS, D), np.float32)
        xs_pad[0:16] = f(x_sample)[b]
        m = dict(shared)
        m.update({
            "xp": f(x_prompt)[b], "xs": xs_pad, "mem": f(mem_prompt)[b],
            "cfk": f(cache_fox_k)[:, b].reshape(2, 2048, 512), "cfv": f(cache_fox_v)[:, b].reshape(2, 2048, 512),
            "clf": f(cache_fox_logf)[:, b], "csk": f(cache_sb_k)[:, b].reshape(2, 2048, 512),
            "csv": f(cache_sb_v)[:, b].reshape(2, 2048, 512),
            "cmk": f(cache_mem_k)[:, b].reshape(2, 256, 1024), "cmv": f(cache_mem_v)[:, b].reshape(2, 256, 1024),
        })
        m = {k: np.ascontiguousarray(v) for k, v in m.items()}
        in_maps.append(m)
    res = run_bass_kernel_spmd(nc, in_maps, core_ids=list(range(8)))
    R = res.results

    def g(name):
        return np.stack([np.asarray(R[b][name]) for b in range(8)], axis=0)

    def kv(name, t):
        return np.ascontiguousarray(np.transpose(g(name), (1, 0, 2, 3)).reshape(2, 8, t, 8, 64))

    y_p = g("yp")
    y_s = g("ys")
    outs = (y_p, y_s,
            kv("p_fk", T), kv("p_fv", T), np.ascontiguousarray(np.transpose(g("p_lf"), (1, 0, 2, 3))),
            kv("p_sk", T), kv("p_sv", T),
            np.ascontiguousarray(np.transpose(g("p_mk"), (1, 0, 2, 3)).reshape(2, 8, 256, 4, 256)),
            np.ascontiguousarray(np.transpose(g("p_mv"), (1, 0, 2, 3)).reshape(2, 8, 256, 4, 256)),
            kv("s_fk", 16), kv("s_fv", 16), np.ascontiguousarray(np.transpose(g("s_lf"), (1, 0, 2, 3))),
            kv("s_sk", 16), kv("s_sv", 16))
    return tuple(np.asarray(o, dtype=np.float32) for o in outs)
```
